# Optimizing a Trainium2 kernel written in Bass

```python
import math
import jax, jax.numpy as jnp
from jax import lax
import numpy as np

D_MODEL = 1024
BATCH = 8
SEQ = 2048
DEPTH = 4

GRID_W = 64
CTX_LEN = 256
N_EVEN = (DEPTH + 1) // 2
N_ODD = DEPTH // 2
MIX_WIDTH = D_MODEL
FFN_HIDDEN = ((8 * D_MODEL + 3 * 256 - 1) // (3 * 256)) * 256
S5_WIDTH = D_MODEL // 2
S5_GROUP = 16
S5_GROUPS = S5_WIDTH // S5_GROUP
S5_STATE = 64
HGRN_HEADS = 4
HGRN_HEAD_DIM = D_MODEL // 8
HGRN_WIDTH = HGRN_HEADS * HGRN_HEAD_DIM
HGRN_CHUNK = 64
MAX_EXP_ARG = 60.0
DIFF_HEADS = 4
DIFF_HEAD_DIM = D_MODEL // 16
DIFF_WIDTH = DIFF_HEADS * 2 * DIFF_HEAD_DIM
MLA_HEADS = 4
MLA_NOPE_DIM = D_MODEL // 8
MLA_ROPE_DIM = D_MODEL // 16
MLA_V_DIM = D_MODEL // 8
MLA_Q_RANK = 3 * D_MODEL // 8
MLA_KV_RANK = D_MODEL // 4
MLA_WIDTH = MLA_HEADS * MLA_V_DIM
ROPE_DIM = D_MODEL // 16
ROPE_BASE = 10000.0
Q_BLOCK = 128
EPS = 1e-6
AB_SPLITS = (S5_WIDTH, S5_WIDTH + HGRN_WIDTH, S5_WIDTH + 2 * HGRN_WIDTH, S5_WIDTH + 3 * HGRN_WIDTH, S5_WIDTH + 4 * HGRN_WIDTH)
AB_IN = S5_WIDTH + 5 * HGRN_WIDTH
CD_SPLITS = (DIFF_WIDTH, 2 * DIFF_WIDTH, 3 * DIFF_WIDTH, 3 * DIFF_WIDTH + MLA_Q_RANK, 3 * DIFF_WIDTH + MLA_Q_RANK + MLA_KV_RANK)
CD_IN = 3 * DIFF_WIDTH + MLA_Q_RANK + MLA_KV_RANK + MLA_ROPE_DIM

kernel_name = "hybrid_s5_hgrn2_diffattn_mla_dit"


def _rmsnorm(x, g):
    xf = x.astype(jnp.float32)
    y = xf * lax.rsqrt(jnp.mean(xf * xf, axis=-1, keepdims=True) + EPS)
    return (y * g.astype(jnp.float32)).astype(x.dtype)


def _modulate(x, shift, scale):
    return x * (1 + scale) + shift


def _swiglu(a, w_in, w_out):
    gate, up = jnp.split(a @ w_in, 2, axis=-1)
    return (jax.nn.silu(gate) * up) @ w_out


def _rope_tables(rows, cols):
    n_freq = ROPE_DIM // 4
    inv = jnp.power(ROPE_BASE, -jnp.arange(n_freq, dtype=jnp.float32) / n_freq)
    ang_r = rows.astype(jnp.float32)[:, None] * inv
    ang_c = cols.astype(jnp.float32)[:, None] * inv
    ang = jnp.concatenate([ang_r, ang_r, ang_c, ang_c], axis=-1)
    return jnp.cos(ang), jnp.sin(ang)


def _apply_rope(x, cos, sin):
    xr = x.reshape(x.shape[:-1] + (2, 2, ROPE_DIM // 4))
    rot = jnp.concatenate([-xr[..., 1:2, :], xr[..., 0:1, :]], axis=-2).reshape(x.shape)
    out = x.astype(jnp.float32) * cos[None, :, None, :] + rot.astype(jnp.float32) * sin[None, :, None, :]
    return out.astype(x.dtype)


def _sweep_query_blocks(fn, *qs):
    b, l = qs[0].shape[:2]
    nb = l // Q_BLOCK
    blocks = tuple(jnp.moveaxis(q.reshape((b, nb, Q_BLOCK) + q.shape[2:]), 1, 0) for q in qs)
    out = lax.map(lambda blk: fn(*blk), blocks)
    return jnp.moveaxis(out, 0, 1).reshape((b, l) + out.shape[3:])


def _softmax_attention(q, k, v, scale):
    def blk(bq):
        s = jnp.einsum('bqhd,bkhd->bhqk', bq, k).astype(jnp.float32) * scale
        p = jax.nn.softmax(s, axis=-1)
        return jnp.einsum('bhqk,bkhe->bqhe', p.astype(v.dtype), v)
    return _sweep_query_blocks(blk, q)


def _diff_attention(q1, q2, k1, k2, v, lam):
    scale = DIFF_HEAD_DIM ** -0.5
    def blk(b1, b2):
        s1 = jnp.einsum('bqhd,bkhd->bhqk', b1, k1).astype(jnp.float32) * scale
        s2 = jnp.einsum('bqhd,bkhd->bhqk', b2, k2).astype(jnp.float32) * scale
        p = jax.nn.softmax(s1, axis=-1) - lam * jax.nn.softmax(s2, axis=-1)
        return jnp.einsum('bhqk,bkhe->bqhe', p.astype(v.dtype), v)
    return _sweep_query_blocks(blk, q1, q2)


def _s5_discretize(lam_re, lam_im, log_step, b_re, b_im):
    lr = jnp.minimum(lam_re.astype(jnp.float32), -1e-4)
    li = lam_im.astype(jnp.float32)
    step = jnp.exp(log_step.astype(jnp.float32))[:, None]
    mag = jnp.exp(lr * step)
    a_r = mag * jnp.cos(li * step)
    a_i = mag * jnp.sin(li * step)
    den = lr * lr + li * li
    coef_r = ((a_r - 1) * lr + a_i * li) / den
    coef_i = (a_i * lr - (a_r - 1) * li) / den
    br = b_re.astype(jnp.float32)
    bi = b_im.astype(jnp.float32)
    bb_r = coef_r[..., None] * br - coef_i[..., None] * bi
    bb_i = coef_r[..., None] * bi + coef_i[..., None] * br
    return a_r, a_i, bb_r, bb_i


def _complex_affine_combine(e1, e2):
    a1r, a1i, b1r, b1i = e1
    a2r, a2i, b2r, b2i = e2
    return (a2r * a1r - a2i * a1i, a2r * a1i + a2i * a1r,
            a2r * b1r - a2i * b1i + b2r, a2r * b1i + a2i * b1r + b2i)


def _s5_scan(u, a_r, a_i, bb_r, bb_i, h0_r, h0_i):
    bu_r = jnp.einsum('blgc,gpc->blgp', u, bb_r)
    bu_i = jnp.einsum('blgc,gpc->blgp', u, bb_i)
    bu_r = bu_r.at[:, 0].add(a_r * h0_r - a_i * h0_i)
    bu_i = bu_i.at[:, 0].add(a_r * h0_i + a_i * h0_r)
    l = u.shape[1]
    ar = jnp.broadcast_to(a_r, (1, l) + a_r.shape)
    ai = jnp.broadcast_to(a_i, (1, l) + a_i.shape)
    _, _, h_r, h_i = lax.associative_scan(_complex_affine_combine, (ar, ai, bu_r, bu_i), axis=1)
    return h_r, h_i


def _s5_readout(h_r, h_i, c_re, c_im):
    return jnp.einsum('blgp,gcp->blgc', h_r, c_re) - jnp.einsum('blgp,gcp->blgc', h_i, c_im)


def _s5_mixer(ux, uc, lam_re, lam_im, log_step, b_re, b_im, c_re, c_im, d_skip, glu_w, glu_b, need_ctx):
    def groups(u):
        return u.astype(jnp.float32).reshape(u.shape[:2] + (S5_GROUPS, S5_GROUP))
    gx, gc = groups(ux), groups(uc)
    d_g = d_skip.astype(jnp.float32).reshape(S5_GROUPS, S5_GROUP)
    zeros = jnp.zeros((ux.shape[0], S5_GROUPS, S5_STATE), jnp.float32)
    y_x = gx * d_g
    y_c = gc * d_g
    for d in range(2):
        a_r, a_i, bb_r, bb_i = _s5_discretize(lam_re[d], lam_im[d], log_step[d], b_re[d], b_im[d])
        cr, ci = c_re[d].astype(jnp.float32), c_im[d].astype(jnp.float32)
        flip = d == 1
        def orient(t):
            return jnp.flip(t, axis=1) if flip else t
        hc_r, hc_i = _s5_scan(orient(gc), a_r, a_i, bb_r, bb_i, zeros, zeros)
        hx_r, hx_i = _s5_scan(orient(gx), a_r, a_i, bb_r, bb_i, hc_r[:, -1], hc_i[:, -1])
        y_x = y_x + orient(_s5_readout(hx_r, hx_i, cr, ci))
        if need_ctx:
            y_c = y_c + orient(_s5_readout(hc_r, hc_i, cr, ci))
    def glu(y):
        z = jax.nn.gelu(y.reshape(y.shape[:2] + (S5_WIDTH,)))
        return z * jax.nn.sigmoid(z @ glu_w.astype(jnp.float32) + glu_b.astype(jnp.float32))
    out_x = glu(y_x).astype(ux.dtype)
    out_c = glu(y_c).astype(uc.dtype) if need_ctx else None
    return out_x, out_c


def _gla_chunk_scan(q, k, log_f, v, s0):
    b, l, h, _ = q.shape
    dv = v.shape[-1]
    n = l // HGRN_CHUNK
    def chunks(t):
        return t.reshape(b, n, HGRN_CHUNK, h, t.shape[-1]).transpose(1, 0, 3, 2, 4)
    seen = jnp.tril(jnp.ones((HGRN_CHUNK, HGRN_CHUNK), dtype=bool))[None, None, :, :, None]
    def step(state, inp):
        qc, kc, gc, vc = inp
        cum = jnp.cumsum(gc, axis=2)
        inter = jnp.einsum('bhtk,bhkv->bhtv', qc * jnp.exp(cum), state)
        rel = cum[:, :, :, None, :] - cum[:, :, None, :, :]
        decay = jnp.where(seen, jnp.exp(jnp.where(seen, rel, 0.0)), 0.0)
        scores = jnp.einsum('bhtk,bhsk,bhtsk->bhts', qc, kc, decay)
        intra = jnp.einsum('bhts,bhsv->bhtv', scores, vc)
        last = cum[:, :, -1:, :]
        new_state = (jnp.exp(last[:, :, 0, :])[..., None] * state
                     + jnp.einsum('bhsk,bhsv->bhkv', kc * jnp.exp(last - cum), vc))
        return new_state, inter + intra
    s_fin, o = lax.scan(step, s0, (chunks(q), chunks(k), chunks(log_f), chunks(v)))
    o = o.transpose(1, 0, 3, 2, 4).reshape(b, l, h, dv)
    return o, s_fin


def _hgrn2_mixer(qx, qc, ffx, ffc, fbx, fbc, ix, ic, gx, gc, lb, out_norm, need_ctx):
    def heads(t):
        return t.astype(jnp.float32).reshape(t.shape[:2] + (HGRN_HEADS, HGRN_HEAD_DIM))
    lb_h = lb.reshape(HGRN_HEADS, HGRN_HEAD_DIM)
    def gates(fl):
        fl = heads(fl)
        k = (1 - lb_h) * jax.nn.sigmoid(-fl)
        log_f = jax.nn.log_sigmoid(fl) + jnp.log1p(lb_h * jnp.exp(jnp.minimum(-fl, MAX_EXP_ARG)))
        return k, log_f
    q_x, q_c = jax.nn.silu(heads(qx)), jax.nn.silu(heads(qc))
    i_x, i_c = heads(ix), heads(ic)
    s0 = jnp.zeros((qx.shape[0], HGRN_HEADS, HGRN_HEAD_DIM, HGRN_HEAD_DIM), jnp.float32)
    o_x_parts, o_c_parts = [], []
    for d, (f_x, f_c) in enumerate(((ffx, ffc), (fbx, fbc))):
        flip = d == 1
        def orient(t):
            return jnp.flip(t, axis=1) if flip else t
        k_x, lf_x = gates(f_x)
        k_c, lf_c = gates(f_c)
        o_c, s_ctx = _gla_chunk_scan(orient(q_c), orient(k_c), orient(lf_c), orient(i_c), s0)
        o_x, _ = _gla_chunk_scan(orient(q_x), orient(k_x), orient(lf_x), orient(i_x), s_ctx)
        o_x_parts.append(orient(o_x))
        o_c_parts.append(orient(o_c))
    def readout(o, g):
        y = _rmsnorm(o, out_norm) * jax.nn.silu(heads(g))
        return y.reshape(o.shape[:2] + (HGRN_WIDTH,)).astype(g.dtype)
    out_x = readout(o_x_parts[0] + o_x_parts[1], gx)
    out_c = readout(o_c_parts[0] + o_c_parts[1], gc) if need_ctx else None
    return out_x, out_c


def _even_mixer(ax, ac, w_in, lam_re, lam_im, log_step, b_re, b_im, c_re, c_im, d_skip, glu_w, glu_b,
                lb, out_norm, need_ctx):
    ux, qx, ffx, fbx, ix, gx = jnp.split(ax @ w_in, AB_SPLITS, axis=-1)
    uc, qc, ffc, fbc, ic, gc = jnp.split(ac @ w_in, AB_SPLITS, axis=-1)
    s5x, s5c = _s5_mixer(ux, uc, lam_re, lam_im, log_step, b_re, b_im, c_re, c_im, d_skip, glu_w, glu_b, need_ctx)
    hx, hc = _hgrn2_mixer(qx, qc, ffx, ffc, fbx, fbc, ix, ic, gx, gc, lb, out_norm, need_ctx)
    out_x = jnp.concatenate([s5x, hx], axis=-1)
    out_c = jnp.concatenate([s5c, hc], axis=-1) if need_ctx else None
    return out_x, out_c


def _odd_mixer(ax, ac, w_in, lam_vec, lam_init, qk_norm, subln, q_a_norm, kv_a_norm, w_uq, w_ukv,
               nope_norm, rope_norm, cos, sin, need_ctx):
    dqx, dkx, dvx, cqx, ckvx, krx = jnp.split(ax @ w_in, CD_SPLITS, axis=-1)
    dqc, dkc, dvc, cqc, ckvc, krc = jnp.split(ac @ w_in, CD_SPLITS, axis=-1)
    lv = lam_vec.astype(jnp.float32)
    lam = jnp.exp(jnp.sum(lv[0] * lv[1])) - jnp.exp(jnp.sum(lv[2] * lv[3])) + lam_init

    def diff_q(p, rope):
        q = _rmsnorm(p.reshape(p.shape[:2] + (DIFF_HEADS, 2, DIFF_HEAD_DIM)), qk_norm[0])
        q1, q2 = q[..., 0, :], q[..., 1, :]
        if rope:
            q1, q2 = _apply_rope(q1, cos, sin), _apply_rope(q2, cos, sin)
        return q1, q2

    def diff_kv(pk, pv, rope):
        k = _rmsnorm(pk.reshape(pk.shape[:2] + (DIFF_HEADS, 2, DIFF_HEAD_DIM)), qk_norm[1])
        k1, k2 = k[..., 0, :], k[..., 1, :]
        if rope:
            k1, k2 = _apply_rope(k1, cos, sin), _apply_rope(k2, cos, sin)
        return k1, k2, pv.reshape(pv.shape[:2] + (DIFF_HEADS, 2 * DIFF_HEAD_DIM))

    def mla_q(cq, rope):
        q = (_rmsnorm(cq, q_a_norm) @ w_uq).reshape(cq.shape[:2] + (MLA_HEADS, MLA_NOPE_DIM + MLA_ROPE_DIM))
        q_nope = _rmsnorm(q[..., :MLA_NOPE_DIM], nope_norm[0])
        q_rope = _rmsnorm(q[..., MLA_NOPE_DIM:], rope_norm[0])
        if rope:
            q_rope = _apply_rope(q_rope, cos, sin)
        return jnp.concatenate([q_nope, q_rope], axis=-1)

    def mla_kv(ckv, kr, rope):
        shp = ckv.shape[:2]
        kv = (_rmsnorm(ckv, kv_a_norm) @ w_ukv).reshape(shp + (MLA_HEADS, MLA_NOPE_DIM + MLA_V_DIM))
        k_nope = _rmsnorm(kv[..., :MLA_NOPE_DIM], nope_norm[1])
        k_rope = _rmsnorm(kr[:, :, None, :], rope_norm[1])
        if rope:
            k_rope = _apply_rope(k_rope, cos, sin)
        k = jnp.concatenate([k_nope, jnp.broadcast_to(k_rope, shp + (MLA_HEADS, MLA_ROPE_DIM))], axis=-1)
        return k, kv[..., MLA_NOPE_DIM:]

    def cat(a, b):
        return jnp.concatenate([a, b], axis=1)

    def diff_out(o):
        return (_rmsnorm(o, subln) * (1.0 - lam_init)).reshape(o.shape[:2] + (DIFF_WIDTH,))

    mla_scale = (MLA_NOPE_DIM + MLA_ROPE_DIM) ** -0.5
    k1c, k2c, vc = diff_kv(dkc, dvc, False)
    k1x, k2x, vx = diff_kv(dkx, dvx, True)
    mkc, mvc = mla_kv(ckvc, krc, False)
    mkx, mvx = mla_kv(ckvx, krx, True)
    q1x, q2x = diff_q(dqx, True)
    c_out_x = diff_out(_diff_attention(q1x, q2x, cat(k1c, k1x), cat(k2c, k2x), cat(vc, vx), lam))
    d_out_x = _softmax_attention(mla_q(cqx, True), cat(mkc, mkx), cat(mvc, mvx), mla_scale)
    out_x = jnp.concatenate([c_out_x, d_out_x.reshape(cqx.shape[:2] + (MLA_WIDTH,))], axis=-1)
    out_c = None
    if need_ctx:
        q1c, q2c = diff_q(dqc, False)
        c_out_c = diff_out(_diff_attention(q1c, q2c, k1c, k2c, vc, lam))
        d_out_c = _softmax_attention(mla_q(cqc, False), mkc, mvc, mla_scale)
        out_c = jnp.concatenate([c_out_c, d_out_c.reshape(cqc.shape[:2] + (MLA_WIDTH,))], axis=-1)
    return out_x, out_c


def setup_inputs(seed: int = 0) -> dict:
    key = jax.random.key(seed)
    ks = iter(jax.random.split(key, 34))
    f32 = jnp.float32

    def nrm(shape, std):
        return std * jax.random.normal(next(ks), shape, f32)

    def gain(shape):
        return 1.0 + nrm(shape, 0.1)

    inputs = {
        "x": nrm((BATCH, SEQ, D_MODEL), 1.0),
        "c": nrm((BATCH, D_MODEL), 1.0),
        "ctx": nrm((BATCH, CTX_LEN, D_MODEL), 1.0),
        "c_ctx": nrm((D_MODEL,), 1.0),
        "ada_w": nrm((DEPTH, D_MODEL, 6 * D_MODEL), 0.5 * D_MODEL ** -0.5),
        "ada_b": nrm((DEPTH, 6 * D_MODEL), 0.02),
        "norm_mix": gain((DEPTH, D_MODEL)),
        "norm_ffn": gain((DEPTH, D_MODEL)),
        "w_out": nrm((DEPTH, MIX_WIDTH, D_MODEL), MIX_WIDTH ** -0.5),
        "ffn_w_in": nrm((DEPTH, D_MODEL, 2 * FFN_HIDDEN), D_MODEL ** -0.5),
        "ffn_w_out": nrm((DEPTH, FFN_HIDDEN, D_MODEL), FFN_HIDDEN ** -0.5),
        "ab_w_in": nrm((N_EVEN, D_MODEL, AB_IN), D_MODEL ** -0.5),
        "s5_lambda_re": -0.5 + nrm((N_EVEN, 2, S5_GROUPS, S5_STATE), 0.01),
        "s5_lambda_im": jnp.pi * jnp.arange(S5_STATE, dtype=f32) + nrm((N_EVEN, 2, S5_GROUPS, S5_STATE), 0.01),
        "s5_log_step": jax.random.uniform(next(ks), (N_EVEN, 2, S5_GROUPS), f32, math.log(1e-3), math.log(1e-1)),
        "s5_b_re": nrm((N_EVEN, 2, S5_GROUPS, S5_STATE, S5_GROUP), (2 * S5_GROUP) ** -0.5),
        "s5_b_im": nrm((N_EVEN, 2, S5_GROUPS, S5_STATE, S5_GROUP), (2 * S5_GROUP) ** -0.5),
        "s5_c_re": nrm((N_EVEN, 2, S5_GROUPS, S5_GROUP, S5_STATE), (2 * S5_STATE) ** -0.5),
        "s5_c_im": nrm((N_EVEN, 2, S5_GROUPS, S5_GROUP, S5_STATE), (2 * S5_STATE) ** -0.5),
        "s5_d": nrm((N_EVEN, S5_WIDTH), 1.0),
        "s5_glu_w": nrm((N_EVEN, S5_WIDTH, S5_WIDTH), S5_WIDTH ** -0.5),
        "s5_glu_b": nrm((N_EVEN, S5_WIDTH), 0.02),
        "hgrn_lb_logits": nrm((N_EVEN, HGRN_WIDTH), 0.5),
        "hgrn_out_norm": gain((N_EVEN, HGRN_HEAD_DIM)),
        "cd_w_in": nrm((N_ODD, D_MODEL, CD_IN), D_MODEL ** -0.5),
        "diff_lambda": nrm((N_ODD, 4, DIFF_HEAD_DIM), 0.1),
        "diff_qk_norm": gain((N_ODD, 2, DIFF_HEAD_DIM)),
        "diff_subln": gain((N_ODD, 2 * DIFF_HEAD_DIM)),
        "mla_q_a_norm": gain((N_ODD, MLA_Q_RANK)),
        "mla_kv_a_norm": gain((N_ODD, MLA_KV_RANK)),
        "mla_w_uq": nrm((N_ODD, MLA_Q_RANK, MLA_HEADS * (MLA_NOPE_DIM + MLA_ROPE_DIM)), MLA_Q_RANK ** -0.5),
        "mla_w_ukv": nrm((N_ODD, MLA_KV_RANK, MLA_HEADS * (MLA_NOPE_DIM + MLA_V_DIM)), MLA_KV_RANK ** -0.5),
        "mla_nope_norm": gain((N_ODD, 2, MLA_NOPE_DIM)),
        "mla_rope_norm": gain((N_ODD, 2, MLA_ROPE_DIM)),
    }
    return inputs


def reference(x, c, ctx, c_ctx, ada_w, ada_b, norm_mix, norm_ffn, w_out, ffn_w_in, ffn_w_out, ab_w_in,
              s5_lambda_re, s5_lambda_im, s5_log_step, s5_b_re, s5_b_im, s5_c_re, s5_c_im, s5_d, s5_glu_w,
              s5_glu_b, hgrn_lb_logits, hgrn_out_norm, cd_w_in, diff_lambda, diff_qk_norm, diff_subln,
              mla_q_a_norm, mla_kv_a_norm, mla_w_uq, mla_w_ukv, mla_nope_norm, mla_rope_norm):
    n_tok = x.shape[1]
    ROWS = n_tok // GRID_W
    rows = jnp.repeat(jnp.arange(ROWS, dtype=jnp.int32), GRID_W)
    cols = jnp.tile(jnp.arange(GRID_W, dtype=jnp.int32), ROWS)
    cos, sin = _rope_tables(rows, cols)

    lb_p = jax.nn.softmax(hgrn_lb_logits.astype(jnp.float32), axis=0)
    lower_bounds = jnp.cumsum(lb_p, axis=0) - lb_p[0:1]

    sc = jax.nn.silu(c)
    scc = jax.nn.silu(c_ctx)
    h, hc = x, ctx
    for l in range(DEPTH):
        need_ctx = l < DEPTH - 1
        mod_x = jnp.split((sc @ ada_w[l] + ada_b[l])[:, None, :], 6, axis=-1)
        mod_c = jnp.split((scc @ ada_w[l] + ada_b[l])[None, None, :], 6, axis=-1)
        ax = _modulate(_rmsnorm(h, norm_mix[l]), mod_x[0], mod_x[1])
        ac = _modulate(_rmsnorm(hc, norm_mix[l]), mod_c[0], mod_c[1])
        if l % 2 == 0:
            e = l // 2
            mx, mc = _even_mixer(ax, ac, ab_w_in[e], s5_lambda_re[e], s5_lambda_im[e], s5_log_step[e],
                                 s5_b_re[e], s5_b_im[e], s5_c_re[e], s5_c_im[e], s5_d[e], s5_glu_w[e],
                                 s5_glu_b[e], lower_bounds[e], hgrn_out_norm[e], need_ctx)
        else:
            o = l // 2
            lam_init = 0.8 - 0.6 * math.exp(-0.3 * l)
            mx, mc = _odd_mixer(ax, ac, cd_w_in[o], diff_lambda[o], lam_init, diff_qk_norm[o], diff_subln[o],
                                mla_q_a_norm[o], mla_kv_a_norm[o], mla_w_uq[o], mla_w_ukv[o],
                                mla_nope_norm[o], mla_rope_norm[o], cos, sin, need_ctx)
        h = h + mod_x[2] * (mx @ w_out[l])
        h = h + mod_x[5] * _swiglu(_modulate(_rmsnorm(h, norm_ffn[l]), mod_x[3], mod_x[4]), ffn_w_in[l], ffn_w_out[l])
        if need_ctx:
            hc = hc + mod_c[2] * (mc @ w_out[l])
            hc = hc + mod_c[5] * _swiglu(_modulate(_rmsnorm(hc, norm_ffn[l]), mod_c[3], mod_c[4]), ffn_w_in[l], ffn_w_out[l])
    return h
```

```python
from contextlib import ExitStack
import math
import numpy as np
import concourse.bass as bass
import concourse.mybir as mybir
from concourse.bass_utils import run_bass_kernel_spmd

F32 = mybir.dt.float32
BF16 = mybir.dt.bfloat16
I32 = mybir.dt.int32
ALU = mybir.AluOpType
AF = mybir.ActivationFunctionType

D = 1024
DEPTH = 4
NT = 2304
CTX = 256
SEQ = 2048
FFH = 2816
EPS = 1e-6
TB = [(0, 256, 1), (256, 512, 0), (768, 512, 0), (1280, 512, 0), (1792, 512, 0)]
TWO_PI = 2.0 * math.pi

ENGS = ("pe", "dve", "act", "pool", "sp")
NDSEM = 12
FENCE_DMA = True
FENCE_ON = True
PREFETCH_ADA = True
SKIP = None
LIM = {}


class Op:
    __slots__ = ("eng", "fn", "deps", "sig", "cnt", "dma", "dsem", "dval", "idx")


class KB:
    def __init__(self, nc):
        self.nc = nc
        self.ops = []
        self.last_w = {}
        self.readers = {}
        self.known = {e: {} for e in ENGS}
        self.dma_cnt = {e: 0 for e in ENGS}
        self.dma_last = {}
        self.dma_n = {}
        self.pending = {e: [] for e in ENGS}
        self.last_op = {}

    def fence(self):
        toks = list(self.last_op.values()) + (list(self.dma_last.values()) if FENCE_DMA else [])
        for e in ENGS:
            self.pending[e] = list(toks)

    def _need(self, eng, tok, deps):
        if tok is None:
            return
        src = self.ops[tok]
        if src.dma:
            key = ("d", src.eng, src.dsem)
        else:
            key = src.eng
            if src.eng == "pe" and eng == "pe":
                return
        if self.known[eng].get(key, -1) >= tok:
            return
        self.known[eng][key] = tok
        deps.append(tok)

    def op(self, eng, fn, R=(), W=(), dma=False):
        o = Op()
        o.eng, o.fn, o.dma, o.sig, o.cnt = eng, fn, dma, False, 0
        o.idx = len(self.ops)
        deps = []
        if self.pending[eng]:
            for t in self.pending[eng]:
                self._need(eng, t, deps)
            self.pending[eng] = []
        for r in R:
            self._need(eng, self.last_w.get(r), deps)
            if isinstance(r, str) and r.startswith("ps"):
                for k, t in self.readers.get(r, {}).items():
                    if k != eng:
                        self._need(eng, t, deps)
        for w in W:
            self._need(eng, self.last_w.get(w), deps)
            for t in self.readers.get(w, {}).values():
                self._need(eng, t, deps)
        if dma:
            k = self.dma_cnt[eng] % NDSEM
            self.dma_cnt[eng] += 1
            o.dsem = k
            prev = self.dma_last.get((eng, k))
            if prev is not None:
                self._need(eng, prev, deps)
            self.dma_last[(eng, k)] = o.idx
            self.dma_n[(eng, k)] = self.dma_n.get((eng, k), 0) + 1
            o.dval = 16 * self.dma_n[(eng, k)]
        o.deps = deps
        self.ops.append(o)
        if not dma:
            self.last_op[eng] = o.idx
        for r in R:
            self.readers.setdefault(r, {})[eng if not dma else ("d", o.idx)] = o.idx
        for w in W:
            self.last_w[w] = o.idx
            self.readers[w] = {}
        return o.idx

    def emit(self, final_wait_ops=()):
        nc = self.nc
        for o in self.ops:
            for d in o.deps:
                self.ops[d].sig = True
        cnt = {e: 0 for e in ENGS}
        for o in self.ops:
            if o.dma:
                continue
            if o.sig:
                cnt[o.eng] += 1
            o.cnt = cnt[o.eng]
        engobj = {"pe": nc.tensor, "dve": nc.vector, "act": nc.scalar, "pool": nc.gpsimd, "sp": nc.sync}
        with ExitStack() as es:
            sem = {e: es.enter_context(nc.semaphore("s_" + e)) for e in ENGS}
            dsem = {}
            for e in ("sp", "act", "pool"):
                for k in range(NDSEM):
                    dsem[(e, k)] = es.enter_context(nc.semaphore(f"d_{e}{k}"))
            per = {e: [] for e in ENGS}
            for o in self.ops:
                per[o.eng].append(o)
            fin = [self.ops[i] for i in final_wait_ops]

            def run(e):
                eo = engobj[e]
                for o in per[e]:
                    for d in o.deps:
                        s = self.ops[d]
                        if s.dma:
                            eo.wait_ge(dsem[(s.eng, s.dsem)], s.dval)
                        else:
                            eo.wait_ge(sem[s.eng], s.cnt)
                    ins = o.fn(eo)
                    if o.dma:
                        ins.then_inc(dsem[(o.eng, o.dsem)], 16)
                    elif o.sig:
                        ins.then_inc(sem[o.eng], 1)
                if e == "sp":
                    for s in fin:
                        eo.wait_ge(dsem[(s.eng, s.dsem)], s.dval)

            with nc.Block() as block:
                @block.tensor
                def _(t):
                    run("pe")

                @block.vector
                def _(v):
                    run("dve")

                @block.scalar
                def _(s):
                    run("act")

                @block.gpsimd
                def _(g):
                    run("pool")

                @block.sync
                def _(s):
                    run("sp")


def rev(ap2d, n):
    a = [list(x) for x in ap2d.ap]
    assert len(a) == 2 and a[1][1] == n
    return bass.AP(ap2d.tensor, ap2d.offset + a[1][0] * (n - 1), [a[0], [-a[1][0], n]])


CF_ID, CF_J, CF_MF, CF_MB, CF_MLO, CF_MHI, CF_SGN, CF_BD, CF_N = (0, 128, 256, 320, 384, 385, 386, 387, 515)
C2_IOTA, C2_CMASK, C2_CMASKB, C2_N = 0, 512, 1024, 1536


def host_consts():
    cf = np.zeros((128, CF_N), np.float32)
    cf[:, CF_ID:CF_ID + 128] = np.eye(128, dtype=np.float32)
    J = np.zeros((128, 128), np.float32)
    for p in range(64):
        J[p, p + 64] = 1.0
        J[p + 64, p] = 1.0
    cf[:, CF_J:CF_J + 128] = J
    c2 = np.zeros((128, C2_N), np.float32)
    c2[:, C2_IOTA:C2_IOTA + 512] = np.arange(512, dtype=np.float32)[None, :]
    cm = np.ones(512, np.float32)
    cm[::32] = 0.0
    c2[:, C2_CMASK:C2_CMASK + 512] = cm[None, :]
    cmb = np.ones(512, np.float32)
    cmb[31::32] = 0.0
    c2[:, C2_CMASKB:C2_CMASKB + 512] = cmb[None, :]
    s = np.arange(64)
    cf[:64, CF_MF:CF_MF + 64] = (s[:, None] <= s[None, :]).astype(np.float32)
    cf[:64, CF_MB:CF_MB + 64] = (s[:, None] >= s[None, :]).astype(np.float32)
    cf[:64, CF_MLO] = 1.0
    cf[64:, CF_MHI] = 1.0
    cf[:64, CF_SGN] = 1.0
    cf[64:, CF_SGN] = -1.0
    bd = np.zeros((128, 128), np.float32)
    bd[:64, :64] = 1.0
    bd[64:, 64:] = 1.0
    cf[:, CF_BD:CF_BD + 128] = bd
    return cf, c2


ROPE_DIM = 64


def host_rope():
    n_freq = ROPE_DIM // 4
    inv = np.power(np.float32(10000.0), -np.arange(n_freq, dtype=np.float32) / np.float32(n_freq)).astype(np.float32)
    t = np.arange(SEQ)
    rows = (t // 64).astype(np.float32)
    cols = (t % 64).astype(np.float32)
    ang_r = rows[:, None] * inv[None, :]
    ang_c = cols[:, None] * inv[None, :]
    ang = np.concatenate([ang_r, ang_r, ang_c, ang_c], axis=-1).astype(np.float32)
    cos = np.cos(ang).astype(np.float32).T
    sin = np.sin(ang).astype(np.float32).T
    sgn = np.ones((64, 1), np.float32)
    sgn[0:16] = -1.0
    sgn[32:48] = -1.0
    sins = sin * sgn
    cos2 = np.concatenate([cos, cos], 0)
    sin2 = np.concatenate([sins, sins], 0)
    perm = np.zeros((128, 128), np.float32)
    for base in (0, 64):
        for m in range(64):
            seg = m // 32
            r = m % 32
            k = seg * 32 + (r + 16) % 32
            perm[base + k, base + m] = 1.0
    return np.ascontiguousarray(cos2), np.ascontiguousarray(sin2), perm


PV_ADAB, PV_NM, PV_NF, PV_N = 0, 48, 56, 64
PE_SD, PE_GB, PE_LBL, PE_ON, PE_LRE, PE_LIM, PE_LST, PE_CR, PE_CI, PE_N = 0, 4, 8, 16, 17, 81, 145, 209, 1233, 2257
PO_LAM, PO_QG, PO_KG, PO_SUB, PO_QA, PO_KVA, PO_NQ, PO_NK, PO_RQ, PO_RK, PO_N = 0, 4, 5, 6, 7, 10, 12, 13, 14, 15, 16


def colmaj(v, nch):
    return np.ascontiguousarray(np.asarray(v, np.float32).reshape(nch, 128).T)


def pack_inputs(inp):
    f = lambda a: np.asarray(a, np.float32)
    pv = np.zeros((DEPTH, 128, PV_N), np.float32)
    for l in range(DEPTH):
        pv[l, :, PV_ADAB:PV_ADAB + 48] = colmaj(f(inp["ada_b"])[l], 48)
        pv[l, :, PV_NM:PV_NM + 8] = colmaj(f(inp["norm_mix"])[l], 8)
        pv[l, :, PV_NF:PV_NF + 8] = colmaj(f(inp["norm_ffn"])[l], 8)
    ne = 2
    pe = np.zeros((ne, 128, PE_N), np.float32)
    bt = np.zeros((ne, 2, 32, 16, 128), np.float32)
    dup = lambda a: np.concatenate([a, a], 0)
    for e in range(ne):
        pe[e, :, PE_SD:PE_SD + 4] = colmaj(f(inp["s5_d"])[e], 4)
        pe[e, :, PE_GB:PE_GB + 4] = colmaj(f(inp["s5_glu_b"])[e], 4)
        for e2 in range(ne):
            pe[e, :, PE_LBL + 4 * e2:PE_LBL + 4 * e2 + 4] = colmaj(f(inp["hgrn_lb_logits"])[e2], 4)
        pe[e, :, PE_ON] = f(inp["hgrn_out_norm"])[e]
        for d in range(2):
            pe[e, :, PE_LRE + 32 * d:PE_LRE + 32 * d + 32] = dup(f(inp["s5_lambda_re"])[e, d].T)
            pe[e, :, PE_LIM + 32 * d:PE_LIM + 32 * d + 32] = dup(f(inp["s5_lambda_im"])[e, d].T)
            pe[e, :, PE_LST + 32 * d:PE_LST + 32 * d + 32] = f(inp["s5_log_step"])[e, d][None, :]
            cr = f(inp["s5_c_re"])[e, d]
            ci = f(inp["s5_c_im"])[e, d]
            crt = dup(cr.transpose(2, 0, 1).reshape(64, 32 * 16))
            cit = dup(ci.transpose(2, 0, 1).reshape(64, 32 * 16))
            pe[e, :, PE_CR + 512 * d:PE_CR + 512 * d + 512] = crt
            pe[e, :, PE_CI + 512 * d:PE_CI + 512 * d + 512] = cit
            br = f(inp["s5_b_re"])[e, d]
            bi = f(inp["s5_b_im"])[e, d]
            bt[e, d, :, :, 0:64] = br.transpose(0, 2, 1)
            bt[e, d, :, :, 64:128] = bi.transpose(0, 2, 1)
    no = 2
    po = np.zeros((no, 128, PO_N), np.float32)
    for o in range(no):
        po[o, :64, PO_LAM:PO_LAM + 4] = f(inp["diff_lambda"])[o].T
        po[o, :, PO_QG] = np.tile(f(inp["diff_qk_norm"])[o, 0], 2)
        po[o, :, PO_KG] = np.tile(f(inp["diff_qk_norm"])[o, 1], 2)
        po[o, :, PO_SUB] = f(inp["diff_subln"])[o]
        po[o, :, PO_QA:PO_QA + 3] = colmaj(f(inp["mla_q_a_norm"])[o], 3)
        po[o, :, PO_KVA:PO_KVA + 2] = colmaj(f(inp["mla_kv_a_norm"])[o], 2)
        po[o, :, PO_NQ] = f(inp["mla_nope_norm"])[o, 0]
        po[o, :, PO_NK] = f(inp["mla_nope_norm"])[o, 1]
        po[o, :64, PO_RQ] = f(inp["mla_rope_norm"])[o, 0]
        po[o, :64, PO_RK] = f(inp["mla_rope_norm"])[o, 1]
    return pv, pe, bt, po


class Builder:
    def __init__(self, layers, dbg=None):
        self.layers = layers
        self.dbg = dbg
        nc = bass.Bass("TRN2", target_bir_lowering=False)
        self.nc = nc
        self.kb = KB(nc)
        dt = lambda name, shape, kind="ExternalInput", dtype=F32: nc.dram_tensor(name, list(shape), dtype, kind=kind).ap()
        self.d_h0 = dt("h0", [D, NT])
        self.d_sc = dt("sc", [128, 8, 2])
        self.d_cf = dt("cf", [128, CF_N])
        self.d_c2 = dt("c2", [128, C2_N])
        self.d_cos = dt("ropecos", [128, SEQ])
        self.d_sin = dt("ropesin", [128, SEQ])
        self.d_perm = dt("ropeperm", [128, 128])
        self.d_pv = dt("pv", [DEPTH, 128, PV_N])
        self.d_pe = dt("pe", [2, 128, PE_N])
        self.d_bt = dt("bt", [2, 2, 32, 16, 128])
        self.d_po = dt("po", [2, 128, PO_N])
        self.d_ada_w = dt("ada_w", [DEPTH, D, 6 * D])
        self.d_w_out = dt("w_out", [DEPTH, D, D])
        self.d_ffn_w_in = dt("ffn_w_in", [DEPTH, D, 2 * FFH])
        self.d_ffn_w_out = dt("ffn_w_out", [DEPTH, FFH, D])
        self.d_ab_w_in = dt("ab_w_in", [2, D, 3072])
        self.d_glu_w = dt("s5_glu_w", [2, 512, 512])
        self.d_cd_w_in = dt("cd_w_in", [2, D, 2240])
        self.d_w_uq = dt("mla_w_uq", [2, 384, 768])
        self.d_w_ukv = dt("mla_w_ukv", [2, 256, 1024])
        self.d_out = dt("out", [D, SEQ], kind="ExternalOutput")
        self.final = []
        self.skip_ctx = False
        self.need_fence = False
        self.wb_i = 0
        self.ps_i = 0
        self.ps_avail = list(range(8))
        self.uid = 0

    def scope(self):
        b = self

        class _Scope(ExitStack):
            def __exit__(self, *a):
                r = ExitStack.__exit__(self, *a)
                b.need_fence = True
                return r

            def close(self):
                ExitStack.close(self)
                b.need_fence = True
        return _Scope()

    def sb(self, es, name, shape, dtype=F32):
        if self.need_fence and FENCE_ON:
            self.kb.fence()
            self.need_fence = False
        return es.enter_context(self.nc.sbuf_tensor("sb_" + name, list(shape), dtype))

    def mm(self, out, lhsT, rhs, start, stop, R, W):
        self.kb.op("pe", lambda e: e.matmul(out, lhsT=lhsT, rhs=rhs, start=start, stop=stop), R, W)

    def tr(self, out, in_, ident, R, W):
        self.kb.op("pe", lambda e: e.transpose(out, in_, ident), R, W)

    def act(self, out, in_, func, R, W, bias=None, scale=1.0):
        if bias is None:
            self.kb.op("act", lambda e: e.activation(out=out, in_=in_, func=func, scale=scale), R, W)
        else:
            self.kb.op("act", lambda e: e.activation(out=out, in_=in_, func=func, bias=bias, scale=scale), R, W)

    def tt(self, out, in0, in1, op, R, W, eng="dve"):
        self.kb.op(eng, lambda e: e.tensor_tensor(out=out, in0=in0, in1=in1, op=op), R, W)

    def ts(self, out, in0, s1, s2, op0, op1, R, W, eng="dve"):
        if s2 is None:
            self.kb.op(eng, lambda e: e.tensor_scalar(out=out, in0=in0, scalar1=s1, scalar2=None, op0=op0), R, W)
        else:
            self.kb.op(eng, lambda e: e.tensor_scalar(out=out, in0=in0, scalar1=s1, scalar2=s2, op0=op0, op1=op1), R, W)

    def stt(self, out, in0, scalar, in1, op0, op1, R, W, eng="dve"):
        self.kb.op(eng, lambda e: e.scalar_tensor_tensor(out=out, in0=in0, scalar=scalar, in1=in1, op0=op0, op1=op1), R, W)

    def cp(self, out, in_, R, W, eng="dve"):
        if eng == "act":
            self.kb.op("act", lambda e: e.copy(out=out, in_=in_), R, W)
        else:
            self.kb.op(eng, lambda e: e.tensor_copy(out=out, in_=in_), R, W)

    def memset(self, ap, val, W, eng="dve"):
        self.kb.op(eng, lambda e: e.memset(ap, val), (), W)

    def dma(self, out, in_, R, W, q="sp"):
        return self.kb.op(q, lambda e: e.dma_start(out=out, in_=in_), R, W, dma=True)

    def dump(self, name, ap, shape, dtype, R):
        if not self.dbg:
            return
        t = self.nc.dram_tensor("dbg_" + name, list(shape), dtype, kind="ExternalOutput").ap()
        self.final.append(self.dma(t, ap, R, (), q="sp"))

    def scan(self, out, d0, d1, init, R, W):
        self.kb.op("dve", lambda e: e.tensor_tensor_scan(out=out, data0=d0, data1=d1, initial=init, op0=ALU.mult, op1=ALU.add), R, W)

    def ps(self):
        k = self.ps_avail[self.ps_i % len(self.ps_avail)]
        self.ps_i += 1
        return self.psum[k], f"ps{k}"

    def wload(self, src, kc, n):
        i = self.wb_i % len(self.wb)
        self.wb_i += 1
        t = self.wb[i]
        key = f"wb{i}"
        assert kc * n <= 4096
        v = bass.AP(t.tensor, t.offset, [list(t.ap[0]), [n, kc], [1, n]])
        self.dma(v, src.rearrange("(kc p) n -> p kc n", p=128), (), [key], q="pool")
        return v, key

    def build(self):
        nc = self.nc
        with ExitStack() as es:
            self.H = self.sb(es, "H", [128, 8, NT])
            self.A = self.sb(es, "A", [128, 8, NT], BF16)
            self.cf = self.sb(es, "cf", [128, CF_N])
            self.onesb = self.sb(es, "onesb", [128, 128], BF16)
            self.s2 = self.sb(es, "s2", [128, 8, 2], BF16)
            self.s2f = self.sb(es, "s2f", [128, 8, 2])
            self.mod2 = [self.sb(es, f"mod{i}", [128, 48, 2]) for i in range(2)]
            self.gs2 = [self.sb(es, f"gs{i}", [128, 2, 8, 2]) for i in range(2)]
            pvt1 = self.sb(es, "pvt", [128, PV_N])
            self.pvt2 = [pvt1, pvt1]
            self.epsT = self.sb(es, "epsT", [128, 1])
            self.oneT = self.sb(es, "oneT", [128, 1])
            self.wbt = [self.sb(es, f"wb{i}", [128, 4096], BF16) for i in range(3)]
            self.wb = [t[:] for t in self.wbt]
            self.psum = [es.enter_context(nc.psum_tensor(f"ps{i}", [128, 512], F32)) for i in range(8)]
            self.ident = self.cf[:, CF_ID:CF_ID + 128]

            self.dma(self.cf[:], self.d_cf, (), ["cf"])
            self.dma(self.s2f[:], self.d_sc, (), ["s2f"])
            for c in range(8):
                self.dma(self.H[:, c, :], self.d_h0[c * 128:(c + 1) * 128, :], (), [f"H{c}.{b}" for b in range(5)], q="sp")
            self.memset(self.onesb[:], 1.0, ["onesb"])
            self.memset(self.epsT[:], EPS, ["epsT"])
            self.memset(self.oneT[:], 1.0, ["oneT"])
            self.act(self.s2[:], self.s2f[:], AF.Silu, ["s2f"], ["s2"])

            for l in self.layers:
                self.layer(l)

            for c in range(8):
                self.final.append(self.dma(self.d_out[c * 128:(c + 1) * 128, :], self.H[:, c, CTX:NT],
                                           [f"H{c}.{b}" for b in range(5)], (), q="sp"))
            self.kb.emit(final_wait_ops=self.final)
        return nc

    def ada_begin(self, l, bank):
        self.dma(self.pvt2[l % 2][:], self.d_pv[l], (), ["pvt"])
        self.ada_ps = (self.psum[bank], f"ps{bank}")

    def ada_piece(self, l, jg, loader):
        pst, pk = self.ada_ps
        w, wk = loader(self.d_ada_w[l][:, jg * 512:(jg + 1) * 512])
        for jj in range(4):
            j = jg * 4 + jj
            for kc in range(8):
                self.mm(pst[:, 2 * j:2 * j + 2], w[:, kc, jj * 128:(jj + 1) * 128], self.s2[:, kc, :], kc == 0, kc == 7,
                        [wk, "s2"], [pk])

    def ada_end(self, l):
        pst, pk = self.ada_ps
        p = l % 2
        mod, gs, pvt = self.mod2[p], self.gs2[p], self.pvt2[p]
        pv3 = pst[:, 0:96].rearrange("p (j v) -> p j v", v=2)
        self.tt(mod[:], pv3, pvt[:, PV_ADAB:PV_ADAB + 48].unsqueeze(2).to_broadcast([128, 48, 2]), ALU.add,
                [pk, "pvt"], [f"mod{p}"])
        for w_, (nofs, sofs) in enumerate(((PV_NM, 8), (PV_NF, 32))):
            self.ts(gs[:, w_, :, :], mod[:, sofs:sofs + 8, :], 1.0, None, ALU.add, None, [f"mod{p}"], [f"gs{p}"])
            self.tt(gs[:, w_, :, :], gs[:, w_, :, :],
                    pvt[:, nofs:nofs + 8].unsqueeze(2).to_broadcast([128, 8, 2]), ALU.mult, [f"gs{p}", "pvt"], [f"gs{p}"])

    def ada(self, l):
        self.ada_begin(l, 7)
        for jg in range(12):
            self.ada_piece(l, jg, lambda src: self.wload(src, 8, 512))
        self.ada_end(l)

    def norm_mod(self, w_, shift_ofs, es, skip0=False):
        sq = self.sb(es, f"nsq{self.uid}", [128, 2, 512], BF16)
        rstd = self.sb(es, f"nrs{self.uid}", [128, 512])
        tmp = self.sb(es, f"ntmp{self.uid}", [128, 2, 512])
        self.uid += 1
        for b, (t0, n, v) in enumerate(TB):
            if skip0 and b == 0:
                continue
            pst, pk = self.ps()
            for c in range(8):
                self.act(sq[:, c % 2, :n], self.H[:, c, t0:t0 + n], AF.Square, [f"H{c}.{b}"], [f"nsq{c % 2}"])
                self.mm(pst[:, :n], self.onesb[:], sq[:, c % 2, :n], c == 0, c == 7, [f"nsq{c % 2}", "onesb"], [pk])
            self.act(rstd[:, :n], pst[:, :n], AF.Ln, [pk, "epsT"], ["nrstd"], bias=self.epsT[:], scale=1.0 / D)
            self.act(rstd[:, :n], rstd[:, :n], AF.Exp, ["nrstd"], ["nrstd"], scale=-0.5)
            for c in range(8):
                self.tt(tmp[:, c % 2, :n], self.H[:, c, t0:t0 + n], rstd[:, :n], ALU.mult, [f"H{c}.{b}", "nrstd"], [f"ntmp{c % 2}"])
                self.act(self.A[:, c, t0:t0 + n], tmp[:, c % 2, :n], AF.Identity, [f"ntmp{c % 2}", self.kgs, self.kmod], [f"A{c}.{b}"],
                         bias=self.mod[:, shift_ofs + c, v:v + 1], scale=self.gs[:, w_, c, v:v + 1])

    def out_proj(self, l, kcs, src, keyf=None):
        nk = len(kcs)
        ws = []
        for og in range(2):
            ws.append(self.wload(self.d_w_out[l][kcs[0] * 128:(kcs[-1] + 1) * 128, og * 512:(og + 1) * 512], nk, 512))
        for o in range(8):
            w, wk = ws[o // 4]
            for b, (t0, n, v) in enumerate(TB):
                if self.skip_ctx and b == 0:
                    continue
                pst, pk = self.ps()
                for i, kc in enumerate(kcs):
                    self.mm(pst[:, :n], w[:, i, (o % 4) * 128:(o % 4 + 1) * 128], src(kc)[:, t0:t0 + n], i == 0, i == nk - 1,
                            [wk, keyf(kc, b) if keyf else f"M{kc}.{b}"], [pk])
                self.stt(self.H[:, o, t0:t0 + n], pst[:, :n], self.mod[:, 16 + o, v:v + 1], self.H[:, o, t0:t0 + n],
                         ALU.mult, ALU.add, [pk, self.kmod, f"H{o}.{b}"], [f"H{o}.{b}"])

    def ffn(self, l, es, nxt=None):
        hact = self.sb(es, f"hact{l}", [128, 4, NT], BF16)
        sg = self.sb(es, f"fsg{l}", [128, 2, 512])
        groups = [(g * 4, 4) for g in range(5)] + [(20, 2)]
        if nxt is not None:
            adaw = [self.sb(es, f"adaw{l}_{i}", [128, 4096], BF16) for i in range(2)]
            acnt = [0]

            def aload(src):
                i = acnt[0] % 2
                acnt[0] += 1
                t = adaw[i][:]
                v_ = bass.AP(t.tensor, t.offset, [list(t.ap[0]), [512, 8], [1, 512]])
                self.dma(v_, src.rearrange("(kc p) n -> p kc n", p=128), (), [f"adaw{i}"], q="pool")
                return v_, f"adaw{i}"
            self.ada_begin(nxt, 7)
            self.ps_avail = list(range(7))
        for gi, (hc0, ng) in enumerate(groups):
            if nxt is not None:
                for jg in (2 * gi, 2 * gi + 1):
                    self.ada_piece(nxt, jg, aload)
            wg, wgk = self.wload(self.d_ffn_w_in[l][:, hc0 * 128:(hc0 + ng) * 128], 8, ng * 128)
            wu, wuk = self.wload(self.d_ffn_w_in[l][:, FFH + hc0 * 128:FFH + (hc0 + ng) * 128], 8, ng * 128)
            wo, wok = self.wload(self.d_ffn_w_out[l][hc0 * 128:(hc0 + ng) * 128, :], ng, 1024)
            for j in range(ng):
                for b, (t0, n, v) in enumerate(TB):
                    if self.skip_ctx and b == 0:
                        continue
                    pg, pgk = self.ps()
                    pu, puk = self.ps()
                    for kc in range(8):
                        self.mm(pg[:, :n], wg[:, kc, j * 128:(j + 1) * 128], self.A[:, kc, t0:t0 + n], kc == 0, kc == 7,
                                [wgk, f"A{kc}.{b}"], [pgk])
                    for kc in range(8):
                        self.mm(pu[:, :n], wu[:, kc, j * 128:(j + 1) * 128], self.A[:, kc, t0:t0 + n], kc == 0, kc == 7,
                                [wuk, f"A{kc}.{b}"], [puk])
                    s = (j * 5 + b) % 2
                    self.act(sg[:, s, :n], pg[:, :n], AF.Silu, [pgk], [f"fsg{s}"])
                    self.tt(hact[:, j, t0:t0 + n], sg[:, s, :n], pu[:, :n], ALU.mult, [f"fsg{s}", puk], [f"hact{j}.{b}"])
            for o in range(8):
                for b, (t0, n, v) in enumerate(TB):
                    if self.skip_ctx and b == 0:
                        continue
                    pst, pk = self.ps()
                    for j in range(ng):
                        self.mm(pst[:, :n], wo[:, j, o * 128:(o + 1) * 128], hact[:, j, t0:t0 + n], j == 0, j == ng - 1,
                                [wok, f"hact{j}.{b}"], [pk])
                    self.stt(self.H[:, o, t0:t0 + n], pst[:, :n], self.mod[:, 40 + o, v:v + 1], self.H[:, o, t0:t0 + n],
                             ALU.mult, ALU.add, [pk, self.kmod, f"H{o}.{b}"], [f"H{o}.{b}"])
        if nxt is not None:
            self.ada_end(nxt)
            self.ps_avail = list(range(8))

    def layer(self, l):
        self.skip_ctx = (l == DEPTH - 1)
        p = l % 2
        self.mod, self.gs, self.pvt = self.mod2[p], self.gs2[p], self.pvt2[p]
        self.kmod, self.kgs = f"mod{p}", f"gs{p}"
        if l == self.layers[0] or not PREFETCH_ADA:
            self.ada(l)
        with self.scope() as es:
            self.norm_mod(0, 0, es)
        with self.scope() as es:
            if l % 2 == 0:
                self.even_mixer(l // 2, es)
            else:
                self.odd_mixer(l // 2, l, es)
        with self.scope() as es:
            self.norm_mod(1, 24, es, skip0=self.skip_ctx)
        with self.scope() as es:
            nxt = self.layers[self.layers.index(l) + 1] if self.layers.index(l) + 1 < len(self.layers) else None
            self.ffn(l, es, nxt if PREFETCH_ADA else None)

    def frac_sin(self, out, q, add, tf, ti, R, W, tag):
        self.ts(tf, q, float(add), None, ALU.add, None, R, [tag + "f"])
        self.cp(ti, tf, [tag + "f"], [tag + "i"])
        self.tt(tf, tf, ti, ALU.subtract, [tag + "f", tag + "i"], [tag + "f"])
        self.act(out, tf, AF.Sin, [tag + "f"], W, scale=TWO_PI)

    def proj_chunk(self, w, wk, col0, ncols, evac):
        for b, (t0, n, v) in enumerate(TB):
            pst, pk = self.ps()
            for kc in range(8):
                self.mm(pst[:ncols, :n], w[:, kc, col0:col0 + ncols], self.A[:, kc, t0:t0 + n], kc == 0, kc == 7,
                        [wk, f"A{kc}.{b}"], [pk])
            evac(pst, pk, b, t0, n, v)

    def even_mixer(self, e, es):
        l = 2 * e
        prm = self.sb(es, f"eprm{e}", [128, PE_CR])
        self.dma(prm[:], self.d_pe[e][:, 0:PE_CR], (), ["eprm"])
        with self.scope() as es2:
            self.Ma = self.sb(es2, f"Ma{self.uid}", [128, 4, NT], BF16)
            self.uid += 1
            if SKIP == "s5":
                self.memset(self.Ma[:], 0.0, [f"M{c}.{b}" for c in range(4) for b in range(5)], eng="pool")
            else:
                self.s5(e, prm, es2)
            self.dump(f"Ma{e}", self.Ma[:], [128, 4, NT], BF16, [f"M{c}.{b}" for c in range(4) for b in range(5)])
            self.out_proj(l, [0, 1, 2, 3], lambda kc: self.Ma[:, kc, :])
        with self.scope() as es2:
            self.Mb = self.sb(es2, f"Mb{self.uid}", [128, 4, NT], BF16)
            self.uid += 1
            if SKIP == "hgrn":
                self.memset(self.Mb[:], 0.0, [f"M{c}.{b}" for c in range(4, 8) for b in range(5)], eng="pool")
            else:
                self.hgrn(e, prm, es2)
            self.dump(f"Mb{e}", self.Mb[:], [128, 4, NT], BF16, [f"M{c}.{b}" for c in range(4, 8) for b in range(5)])
            self.out_proj(l, [4, 5, 6, 7], lambda kc: self.Mb[:, kc - 4, :])

    def mchunk(self, c):
        return self.Ma[:, c, :] if c < 4 else self.Mb[:, c - 4, :]

    def s5(self, e, prm, es_outer):
        nc = self.nc
        es = es_outer.enter_context(self.scope())
        S = lambda name, shape, dt=F32: self.sb(es, f"s5{name}{e}", shape, dt)
        P64 = {k: S(k, [128, 64]) for k in ("r", "q1", "c256", "s256", "c512", "s512")}
        L1f = S("L1f", [128, 64, 16], BF16)
        L2f = S("L2f", [128, 64, 16], BF16)
        esc = es.enter_context(self.scope())
        for k in ("cr", "ci"):
            P64[k] = self.sb(esc, f"s5{k}{e}", [128, 64])
        esp = es.enter_context(self.scope())
        for k in ("lr", "step", "cs", "sn", "ar", "ai", "den", "t0", "t1", "tf"):
            P64[k] = self.sb(esp, f"s5{k}{e}", [128, 64])
        ti64 = self.sb(esp, f"s5ti64{e}", [128, 64], I32)
        lre, lim, lst = prm[:, PE_LRE:PE_LRE + 64], prm[:, PE_LIM:PE_LIM + 64], prm[:, PE_LST:PE_LST + 64]
        p = lambda k: P64[k][:]
        self.ts(p("lr"), lre, -1e-4, None, ALU.min, None, ["eprm"], ["s5lr"])
        self.act(p("step"), lst, AF.Exp, ["eprm"], ["s5step"])
        self.tt(p("t0"), p("lr"), p("step"), ALU.mult, ["s5lr", "s5step"], ["s5t0"])
        self.act(p("r"), p("t0"), AF.Exp, ["s5t0"], ["s5r"])
        self.tt(p("q1"), lim, p("step"), ALU.mult, ["eprm", "s5step"], ["s5q1"])
        self.ts(p("q1"), p("q1"), 1.0 / TWO_PI, None, ALU.mult, None, ["s5q1"], ["s5q1"])
        self.frac_sin(p("cs"), p("q1"), 0.25, p("tf"), ti64[:], ["s5q1"], ["s5cs"], "s5x")
        self.frac_sin(p("sn"), p("q1"), 0.0, p("tf"), ti64[:], ["s5q1"], ["s5sn"], "s5x")
        for T, ck, sk in ((256.0, "c256", "s256"), (512.0, "c512", "s512")):
            self.ts(p("t1"), p("q1"), T, None, ALU.mult, None, ["s5q1"], ["s5t1"])
            self.frac_sin(p(ck), p("t1"), 0.25, p("tf"), ti64[:], ["s5t1"], ["s5" + ck], "s5x")
            self.frac_sin(p(sk), p("t1"), 0.0, p("tf"), ti64[:], ["s5t1"], ["s5" + sk], "s5x")
            self.ts(p(sk), p(sk), self.cf[:, CF_SGN:CF_SGN + 1], None, ALU.mult, None, ["s5" + sk, "cf"], ["s5" + sk])
        self.tt(p("ar"), p("r"), p("cs"), ALU.mult, ["s5r", "s5cs"], ["s5ar"])
        self.tt(p("ai"), p("r"), p("sn"), ALU.mult, ["s5r", "s5sn"], ["s5ai"])
        self.ts(p("ar"), p("ar"), -1.0, None, ALU.add, None, ["s5ar"], ["s5ar"])
        self.tt(p("den"), p("lr"), p("lr"), ALU.mult, ["s5lr"], ["s5den"])
        self.tt(p("t0"), lim, lim, ALU.mult, ["eprm", "s5r"], ["s5t0"])
        self.tt(p("den"), p("den"), p("t0"), ALU.add, ["s5den", "s5t0"], ["s5den"])
        self.kb.op("dve", lambda e_: e_.reciprocal(out=p("den"), in_=p("den")), ["s5den"], ["s5den"])
        self.tt(p("t0"), p("ar"), p("lr"), ALU.mult, ["s5ar", "s5lr"], ["s5t0"])
        self.tt(p("t1"), p("ai"), lim, ALU.mult, ["s5ai", "eprm"], ["s5t1"])
        self.tt(p("t0"), p("t0"), p("t1"), ALU.add, ["s5t0", "s5t1"], ["s5t0"])
        self.tt(p("cr"), p("t0"), p("den"), ALU.mult, ["s5t0", "s5den"], ["s5cr"])
        self.tt(p("t0"), p("ai"), p("lr"), ALU.mult, ["s5ai", "s5lr", "s5cr"], ["s5t0"])
        self.tt(p("t1"), p("ar"), lim, ALU.mult, ["s5ar", "eprm"], ["s5t1"])
        self.tt(p("t0"), p("t0"), p("t1"), ALU.subtract, ["s5t0", "s5t1"], ["s5t0"])
        self.tt(p("ci"), p("t0"), p("den"), ALU.mult, ["s5t0", "s5den"], ["s5ci"])
        esp.close()
        with self.scope() as es3:
            CR = self.sb(es3, f"s5CR{e}", [128, 64, 16])
            CI = self.sb(es3, f"s5CI{e}", [128, 64, 16])
            Cr_ = self.sb(es3, f"s5Cr_{e}", [128, 64, 16])
            Ci_ = self.sb(es3, f"s5Ci_{e}", [128, 64, 16])
            tq = self.sb(es3, f"s5tq{e}", [128, 64, 16])
            self.dma(CR[:].rearrange("p a b -> p (a b)"), self.d_pe[e][:, PE_CR:PE_CR + 1024], (), ["s5CR"])
            self.dma(CI[:].rearrange("p a b -> p (a b)"), self.d_pe[e][:, PE_CI:PE_CI + 1024], (), ["s5CI"])
            crb = p("cr").unsqueeze(2).to_broadcast([128, 64, 16])
            cib = p("ci").unsqueeze(2).to_broadcast([128, 64, 16])
            self.tt(Cr_[:], CR[:], crb, ALU.mult, ["s5CR", "s5cr"], ["s5Cr_"])
            self.tt(tq[:], CI[:], cib, ALU.mult, ["s5CI", "s5ci"], ["s5tq"])
            self.tt(Cr_[:], Cr_[:], tq[:], ALU.subtract, ["s5Cr_", "s5tq"], ["s5Cr_"])
            self.tt(Ci_[:], CR[:], cib, ALU.mult, ["s5CR", "s5ci"], ["s5Ci_"])
            self.tt(tq[:], CI[:], crb, ALU.mult, ["s5CI", "s5cr", "s5Cr_"], ["s5tq"])
            self.tt(Ci_[:], Ci_[:], tq[:], ALU.add, ["s5Ci_", "s5tq"], ["s5Ci_"])
            mlo, mhi = self.cf[:, CF_MLO:CF_MLO + 1], self.cf[:, CF_MHI:CF_MHI + 1]
            self.ts(tq[:], Ci_[:], mhi, None, ALU.mult, None, ["s5Ci_", "cf", "s5Ci_"], ["s5tq"])
            self.stt(L1f[:], Cr_[:], mlo, tq[:], ALU.mult, ALU.subtract, ["s5Cr_", "s5tq", "cf"], ["s5L1f"])
            self.ts(tq[:], Cr_[:], mhi, -1.0, ALU.mult, ALU.mult, ["s5Cr_", "cf", "s5L1f"], ["s5tq"])
            self.ts(Ci_[:], Ci_[:], mlo, None, ALU.mult, None, ["s5Ci_", "cf"], ["s5Ci_"])
            self.tt(L2f[:], tq[:], Ci_[:], ALU.subtract, ["s5tq", "s5Ci_"], ["s5L2f"])
        esc.close()

        u = S("u", [128, NT], BF16)
        y = S("y", [128, NT])
        BW1 = S("BW1", [128, 8, 128], BF16)
        BW2 = S("BW2", [128, 8, 128], BF16)
        LR1 = S("LR1", [128, 8, 128], BF16)
        LR2 = S("LR2", [128, 8, 128], BF16)
        Ec = [S(f"Ec{i}", [128, 512]) for i in range(2)]
        Es = [S(f"Es{i}", [128, 512]) for i in range(2)]
        iot = S("iot", [128, 512])
        self.dma(iot[:], self.d_c2[:, C2_IOTA:C2_IOTA + 512], (), ["s5iot"])
        tib = S("tib", [128, 512], I32)
        R512 = [S(f"R512{i}", [128, 128]) for i in range(2)]
        R256 = [S(f"R256{i}", [128, 128]) for i in range(2)]
        Ta = [S(f"Ta{i}", [128, 512]) for i in range(2)]
        Tb = [S(f"Tb{i}", [128, 512]) for i in range(2)]
        cg = [S(f"cg{i}", [128, 512], BF16) for i in range(2)]
        sg = [S(f"sg{i}", [128, 512], BF16) for i in range(2)]
        carry = S("carry", [128, 2])
        qtr = S("qtr", [128, 1])
        self.memset(qtr[:], 0.25, ["s5qtr"])
        iota = iot[:]
        Jm = self.cf[:, CF_J:CF_J + 128]

        def diag(t):
            a = t[:]
            return bass.AP(a.tensor, a.offset, [list(a.ap[0]), [144, 8], [1, 16]])

        order = {0: [0, 1, 2, 3, 4], 1: [0, 4, 3, 2, 1]}
        step = 0
        for c4 in range(LIM.get('c4', 4)):
            w, wk = self.wload(self.d_ab_w_in[e][:, c4 * 128:(c4 + 1) * 128], 8, 128)

            def evac_u(pst, pk, b, t0, n, v, c4=c4):
                self.act(u[:, t0:t0 + n], pst[:, :n], AF.Copy, [pk], [f"s5u.{b}"])
                self.act(y[:, t0:t0 + n], pst[:, :n], AF.Copy, [pk, "eprm"], [f"s5y.{b}"],
                         scale=prm[:, PE_SD + c4:PE_SD + c4 + 1])
            self.proj_chunk(w, wk, 0, 128, evac_u)
            for d in range(LIM.get('d', 2)):
                bwk = [f"s5BW1.{j}" for j in range(8)]
                self.memset(BW1[:], 0.0, bwk, eng="pool")
                for j in range(8):
                    self.dma(BW1[16 * j:16 * j + 16, j, :], self.d_bt[e, d, c4 * 8 + j], (), [bwk[j]], q="pool")
                self.cp(BW2[:, :, 0:64], BW1[:, :, 64:128], bwk, ["s5BW2"], eng="pool")
                self.ts(BW2[:, :, 64:128], BW1[:, :, 0:64], -1.0, None, ALU.mult, None, bwk, ["s5BW2"], eng="pool")
                base = (d * 32 + c4 * 8) * 16
                for LR, Lf, key in ((LR1, L1f, "s5L1f"), (LR2, L2f, "s5L2f")):
                    self.memset(LR[:], 0.0, ["s5" + ("LR1" if LR is LR1 else "LR2")], eng="pool")
                    src = Lf[:].rearrange("p a b -> p (a b)")[:, base:base + 128].rearrange("p (j c) -> p j c", c=16)
                    self.cp(diag(LR), src, [key], ["s5" + ("LR1" if LR is LR1 else "LR2")], eng="pool")
                self.ps_avail = [0, 1]
                blocks = order[d][:LIM.get('b', 5)]
                for jp in range(0, LIM.get('j', 8), 2):
                    chains = []
                    for jj in range(2):
                        j = jp + jj
                        col = d * 32 + c4 * 8 + j
                        cs_ = lambda k, col=col: P64[k][:, col:col + 1]
                        for tbl, add, tk in ((Es[jj], 0.0, f"s5Es{jj}"), (Ec[jj], 0.25, f"s5Ec{jj}")):
                            if add == 0.0:
                                self.act(tbl[:], iota, AF.Copy, ["s5iot", "s5q1"], [tk], scale=cs_("q1"))
                            else:
                                self.act(tbl[:], iota, AF.Identity, ["s5iot", "s5q1", "s5qtr"], [tk], scale=cs_("q1"), bias=qtr[:])
                            self.cp(tib[:], tbl[:], [tk], ["s5yi"])
                            self.tt(tbl[:], tbl[:], tib[:], ALU.subtract, [tk, "s5yi"], [tk])
                            self.act(tbl[:], tbl[:], AF.Sin, [tk], [tk], scale=TWO_PI)
                        for Rm, rk, ck, sk in ((R512[jj], f"s5R512{jj}", "c512", "s512"), (R256[jj], f"s5R256{jj}", "c256", "s256")):
                            self.act(Rm[:], self.ident, AF.Copy, ["cf", "s5" + ck], [rk], scale=cs_(ck))
                            self.stt(Rm[:], Jm, cs_(sk), Rm[:], ALU.mult, ALU.add, ["cf", "s5" + sk, rk], [rk])
                        chains.append((jj, j, cs_))

                    def mmM(jj, j, bi):
                        t0, n, v = TB[blocks[bi]]
                        ub = u[:, t0:t0 + n]
                        if d == 1:
                            ub = rev(ub, n)
                        M1, k1, M2, k2 = self.psum[2 + 2 * jj], f"ps{2 + 2 * jj}", self.psum[3 + 2 * jj], f"ps{3 + 2 * jj}"
                        self.mm(M1[:, :n], BW1[:, j, :], ub, True, True, [f"s5BW1.{j}", f"s5u.{blocks[bi]}"], [k1])
                        self.mm(M2[:, :n], BW2[:, j, :], ub, True, True, ["s5BW2", f"s5u.{blocks[bi]}"], [k2])

                    for jj, j, cs_ in chains:
                        mmM(jj, j, 0)
                    pend = None
                    for bi, b in enumerate(blocks):
                        t0, n, v = TB[b]
                        for jj, j, cs_ in chains:
                            M1, k1, M2, k2 = self.psum[2 + 2 * jj], f"ps{2 + 2 * jj}", self.psum[3 + 2 * jj], f"ps{3 + 2 * jj}"
                            self.tt(Ta[jj][:, :n], M1[:, :n], Ec[jj][:, :n], ALU.mult, [k1, f"s5Ec{jj}"], [f"s5Ta{jj}"])
                            self.tt(Tb[jj][:, :n], M2[:, :n], Es[jj][:, :n], ALU.mult, [k2, f"s5Es{jj}"], [f"s5Tb{jj}"])
                            if bi < len(blocks) - 1:
                                mmM(jj, j, bi + 1)
                            self.tt(Ta[jj][:, :n], Ta[jj][:, :n], Tb[jj][:, :n], ALU.add, [f"s5Ta{jj}", f"s5Tb{jj}"], [f"s5Ta{jj}"])
                            init = 0.0 if bi == 0 else carry[:, jj:jj + 1]
                            self.scan(Tb[jj][:, :n], cs_("r").to_broadcast([128, n]), Ta[jj][:, :n], init,
                                      [f"s5Ta{jj}", "s5r", f"s5carry{jj}"], [f"s5Tb{jj}"])
                            self.tt(cg[jj][:, :n], Ec[jj][:, :n], Tb[jj][:, :n], ALU.mult, [f"s5Ec{jj}", f"s5Tb{jj}"], [f"s5cg{jj}"], eng="pool")
                            self.tt(sg[jj][:, :n], Es[jj][:, :n], Tb[jj][:, :n], ALU.mult, [f"s5Es{jj}", f"s5Tb{jj}"], [f"s5sg{jj}"], eng="pool")
                            if bi < len(blocks) - 1:
                                Rm, rk = (R256[jj], f"s5R256{jj}") if n == 256 else (R512[jj], f"s5R512{jj}")
                                pc, pck = self.psum[6 + jj], f"ps{6 + jj}"
                                self.mm(pc[:, 0:1], Rm[:], Tb[jj][:, n - 1:n], True, True, [rk, f"s5Tb{jj}"], [pck])
                                self.cp(carry[:, jj:jj + 1], pc[:, 0:1], [pck], [f"s5carry{jj}"], eng="act")
                        if pend is not None:
                            pend()
                        yb, ybk = self.ps()
                        for jj, j, cs_ in chains:
                            self.mm(yb[:, :n], LR1[:, j, :], cg[jj][:, :n], jj == 0, False, ["s5LR1", f"s5cg{jj}"], [ybk])
                            self.mm(yb[:, :n], LR2[:, j, :], sg[jj][:, :n], False, jj == 1, ["s5LR2", f"s5sg{jj}"], [ybk])

                        def pend(yb=yb, ybk=ybk, t0=t0, n=n, b=b):
                            src = yb[:, :n]
                            if d == 1:
                                src = rev(src, n)
                            self.tt(y[:, t0:t0 + n], y[:, t0:t0 + n], src, ALU.add, [f"s5y.{b}", ybk], [f"s5y.{b}"])
                    pend()
                self.ps_avail = list(range(8))
            for b, (t0, n, v) in enumerate(TB):
                yb_ = y[:, t0:t0 + n]
                self.tt(Ta[0][:, :n], yb_, yb_, ALU.mult, [f"s5y.{b}"], ["s5Ta0"])
                self.ts(Ta[0][:, :n], Ta[0][:, :n], 0.044715, 1.0, ALU.mult, ALU.add, ["s5Ta0"], ["s5Ta0"])
                self.tt(Ta[0][:, :n], Ta[0][:, :n], yb_, ALU.mult, ["s5Ta0", f"s5y.{b}"], ["s5Ta0"])
                self.act(Tb[0][:, :n], Ta[0][:, :n], AF.Sigmoid, ["s5Ta0"], ["s5Tb0"], scale=1.5957691216057308)
                self.tt(self.Ma[:, c4, t0:t0 + n], yb_, Tb[0][:, :n], ALU.mult, [f"s5y.{b}", "s5Tb0"], [f"M{c4}.{b}"])
        self.dump(f"s5u{e}", u[:], [128, NT], BF16, [f"s5u.{b}" for b in range(5)])
        self.dump(f"s5y{e}", y[:], [128, NT], F32, [f"s5y.{b}" for b in range(5)])
        self.dump(f"s5prm{e}", prm[:], [128, PE_CR], F32, ["eprm"])
        self.dump(f"s5L1f{e}", L1f[:], [128, 64, 16], BF16, ["s5L1f"])
        self.dump(f"s5Tb{e}", Tb[0][:], [128, 512], F32, ["s5Tb0"])
        self.dump(f"s5r{e}", P64["r"][:], [128, 64], F32, ["s5r"])
        self.dump(f"s5BW1{e}", BW1[:], [128, 8, 128], BF16, [f"s5BW1.{j}" for j in range(8)])
        es.close()
        wg, wgk = self.wload(self.d_glu_w[e], 4, 512)
        SG = self.sb(es_outer, f"s5SG{e}", [128, 4, 512], BF16)
        for b, (t0, n, v) in enumerate(TB):
            for c in range(4):
                pst, pk = self.ps()
                for kc in range(4):
                    self.mm(pst[:, :n], wg[:, kc, c * 128:(c + 1) * 128], self.Ma[:, kc, t0:t0 + n], kc == 0, kc == 3,
                            [wgk, f"M{kc}.{b}"], [pk])
                self.act(SG[:, c, :n], pst[:, :n], AF.Sigmoid, [pk, "eprm"], [f"s5SG{c}"], bias=prm[:, PE_GB + c:PE_GB + c + 1])
            for c in range(4):
                self.tt(self.Ma[:, c, t0:t0 + n], self.Ma[:, c, t0:t0 + n], SG[:, c, :n], ALU.mult,
                        [f"M{c}.{b}", f"s5SG{c}"], [f"M{c}.{b}"])

    def hgrn(self, e, prm, es):
        S = lambda name, shape, dt=F32: self.sb(es, f"hg{name}{e}", shape, dt)
        E0, E1, Ss, LB, OML, NOML = (S(k, [128, 4]) for k in ("E0", "E1", "Ss", "LB", "OML", "NOML"))
        self.act(E0[:], prm[:, PE_LBL:PE_LBL + 4], AF.Exp, ["eprm"], ["hgE0"])
        self.act(E1[:], prm[:, PE_LBL + 4:PE_LBL + 8], AF.Exp, ["eprm"], ["hgE1"])
        self.tt(Ss[:], E0[:], E1[:], ALU.add, ["hgE0", "hgE1"], ["hgSs"])
        self.kb.op("dve", lambda e_: e_.reciprocal(out=Ss[:], in_=Ss[:]), ["hgSs"], ["hgSs"])
        self.tt(E0[:], E0[:], Ss[:], ALU.mult, ["hgE0", "hgSs"], ["hgE0"])
        self.tt(E1[:], E1[:], Ss[:], ALU.mult, ["hgE1", "hgSs"], ["hgE1"])
        if e == 0:
            self.tt(LB[:], E0[:], E0[:], ALU.subtract, ["hgE0"], ["hgLB"])
        else:
            self.tt(LB[:], E0[:], E1[:], ALU.add, ["hgE0", "hgE1"], ["hgLB"])
            self.tt(LB[:], LB[:], E0[:], ALU.subtract, ["hgLB", "hgE0"], ["hgLB"])
        self.ts(OML[:], LB[:], -1.0, 1.0, ALU.mult, ALU.add, ["hgLB"], ["hgOML"])
        self.ts(NOML[:], LB[:], -1.0, None, ALU.add, None, ["hgLB"], ["hgNOML"])

        qs = S("qs", [128, NT], BF16)
        cmk = S("cmk", [128, 1024], BF16)
        self.dma(cmk[:], self.d_c2[:, C2_CMASK:C2_CMASK + 1024], (), ["hgcmk"], q="pool")
        CH = 32
        vtm = S("vtm", [CH, NT // CH, 128], BF16)
        O = S("O", [128, NT])
        T = {i: S(f"T{i}", [128, 512]) for i in (1, 2, 3, 5)}
        EB = S("EB", [128, 512])
        KK = S("KK", [128, 512])
        KH = T[3]
        QA = S("QA", [128, 512])
        QR = S("QR", [128, 512], BF16)
        KT = S("KT", [128, 512], BF16)
        sqb = QR
        PM = [S(f"PM{i}", [CH, CH], BF16) for i in range(3)]
        KHt = [S(f"KHt{i}", [CH, 128], BF16) for i in range(3)]
        St = [S(f"St{i}", [128, 128]) for i in range(2)]
        cm_f = cmk[:, 0:512]
        cm_b = cmk[:, 512:1024]
        mask = {0: self.cf[0:CH, CF_MF:CF_MF + CH], 1: self.cf[0:CH, CF_MB:CF_MB + CH]}
        order = {0: [0, 1, 2, 3, 4], 1: [0, 4, 3, 2, 1]}
        wcols = [512, 1024, 1536, 2048, 2560]
        for h in range(4):
            wl = lambda wi: self.wload(self.d_ab_w_in[e][:, wcols[wi] + 128 * h:wcols[wi] + 128 * h + 128], 8, 128)

            def evac_q(pst, pk, b, t0, n, v):
                self.act(qs[:, t0:t0 + n], pst[:, :n], AF.Silu, [pk], [f"hgqs.{b}"])
            wq, wqk = wl(0)
            self.proj_chunk(wq, wqk, 0, 128, evac_q)

            def evac_v(pst, pk, b, t0, n, v):
                self.cp(T[1][:, :n], pst[:, :n], [pk], ["hgT1"], eng="act")
                for ci in range(n // CH):
                    pt, ptk = self.ps()
                    self.tr(pt[0:CH, 0:128], T[1][:, ci * CH:ci * CH + CH], self.ident, ["hgT1", "cf"], [ptk])
                    self.cp(vtm[:, t0 // CH + ci, :], pt[0:CH, 0:128], [ptk], [f"hgvtm.{b}"])
            wv, wvk = wl(3)
            self.proj_chunk(wv, wvk, 0, 128, evac_v)

            for d in range(2):
                self.memset(St[0][:], 0.0, ["hgSt0"])
                kst = [0]
                lc = CH - 1 if d == 0 else 0
                wf, wfk = wl(1 + d)
                def part1(b):
                    t0, n, v = TB[b]
                    pf, pfk = self.ps()
                    for kc in range(8):
                        self.mm(pf[:, :n], wf[:, kc, :], self.A[:, kc, t0:t0 + n], kc == 0, kc == 7,
                                [wfk, f"A{kc}.{b}"], [pfk])
                    t = lambda i: T[i][:, :n]
                    self.act(t(1), pf[:, :n], AF.Exp, [pfk], ["hgT1"], scale=-1.0)
                    self.ts(t(1), t(1), 1.1420073898156842e26, None, ALU.min, None, ["hgT1"], ["hgT1"])
                    self.act(t(2), t(1), AF.Ln, ["hgT1", "oneT"], ["hgT2"], bias=self.oneT[:])
                    self.act(t(5), t(2), AF.Exp, ["hgT2"], ["hgT5"], scale=-1.0)
                    self.ts(KK[:, :n], t(5), NOML[:, h:h + 1], OML[:, h:h + 1], ALU.mult, ALU.add, ["hgT5", "hgNOML", "hgOML"], ["hgKK"])

                def part2(b):
                    t0, n, v = TB[b]
                    nch = n // CH
                    t = lambda i: T[i][:, :n]
                    v3 = lambda ap: ap.rearrange("p (c s) -> p c s", s=CH)
                    bc = lambda ap, col: v3(ap)[:, :, col:col + 1].to_broadcast([128, nch, CH])
                    self.act(t(3), t(1), AF.Ln, ["hgT1", "oneT", "hgLB"], ["hgT3"], bias=self.oneT[:], scale=LB[:, h:h + 1])
                    self.tt(t(3), t(3), t(2), ALU.subtract, ["hgT3", "hgT2"], ["hgT3"])
                    if d == 0:
                        self.scan(t(5), cm_f[:, :n], t(3), 0.0, ["hgT3", "hgcmk", "hgKK"], ["hgT5"])
                    else:
                        self.scan(rev(t(5), n), rev(cm_b[:, :n], n), rev(t(3), n), 0.0, ["hgT3", "hgcmk", "hgKK"], ["hgT5"])
                    self.act(EB[:, :n], t(5), AF.Exp, ["hgT5"], ["hgEB"])
                    self.tt(QA[:, :n], qs[:, t0:t0 + n], EB[:, :n], ALU.mult, [f"hgqs.{b}", "hgEB"], ["hgQA"])
                    self.tt(v3(t(2)), v3(t(5)), bc(t(5), CH // 2), ALU.subtract, ["hgT5"], ["hgT2"])
                    self.act(t(1), t(2), AF.Exp, ["hgT2"], ["hgT1"])
                    self.tt(QR[:, :n], qs[:, t0:t0 + n], t(1), ALU.mult, [f"hgqs.{b}", "hgT1"], ["hgQR"])
                    self.act(t(1), t(2), AF.Exp, ["hgT2", "hgQR"], ["hgT1"], scale=-1.0)
                    self.tt(KT[:, :n], KK[:, :n], t(1), ALU.mult, ["hgKK", "hgT1"], ["hgKT"])
                    self.tt(v3(t(2)), bc(t(5), lc), v3(t(5)), ALU.subtract, ["hgT5", "hgKT"], ["hgT2"])
                    self.act(t(2), t(2), AF.Exp, ["hgT2"], ["hgT2"])
                    self.tt(KH[:, :n], KK[:, :n], t(2), ALU.mult, ["hgKK", "hgT2", "hgT3"], ["hgT3"])

                blocks_ = order[d]
                part1(blocks_[0])
                for bi_, b in enumerate(blocks_):
                    t0, n, v = TB[b]
                    nch = n // CH
                    nxt_b = blocks_[bi_ + 1] if bi_ + 1 < len(blocks_) else None
                    part2(b)
                    clist = list(range(nch)) if d == 0 else list(range(nch - 1, -1, -1))
                    st1 = {}

                    def stage1a(ci):
                        c0 = ci * CH
                        gch = t0 // CH + ci
                        par = gch % 3
                        pS, pSk = self.ps()
                        self.mm(pS[0:CH, 0:CH], KT[:, c0:c0 + CH], QR[:, c0:c0 + CH], True, True, ["hgKT", "hgQR"], [pSk])
                        self.tt(PM[par][:], pS[0:CH, 0:CH], mask[d], ALU.mult, [pSk, "cf"], [f"hgPM{par}"])
                        pT, pTk = self.ps()
                        self.tr(pT[0:CH, 0:128], KH[:, c0:c0 + CH], self.ident, ["hgT3", "cf"], [pTk])
                        self.cp(KHt[par][:], pT[0:CH, 0:128], [pTk], [f"hgKHt{par}"], eng="act")

                    def stage1b(ci):
                        gch = t0 // CH + ci
                        par = gch % 3
                        pD, pDk = self.ps()
                        self.mm(pD[:, 0:128], KHt[par][:], vtm[:, gch, :], True, True, [f"hgKHt{par}", f"hgvtm.{b}"], [pDk])
                        st1[ci] = (pD, pDk)

                    def stage2(ci):
                        c0 = ci * CH
                        gch = t0 // CH + ci
                        par = gch % 3
                        pD, pDk = st1.pop(ci)
                        sp, sn = kst[0] % 2, (kst[0] + 1) % 2
                        kst[0] += 1
                        pO, pOk = self.ps()
                        self.mm(pO[:, 0:CH], St[sp][:], QA[:, c0:c0 + CH], True, False, [f"hgSt{sp}", "hgQA"], [pOk])
                        self.mm(pO[:, 0:CH], vtm[:, gch, :], PM[par][:], False, True, [f"hgvtm.{b}", f"hgPM{par}"], [pOk])
                        self.stt(St[sn][:], St[sp][:], EB[:, c0 + lc:c0 + lc + 1], pD[:, 0:128], ALU.mult, ALU.add,
                                 [f"hgSt{sp}", "hgEB", pDk], [f"hgSt{sn}"])
                        if d == 0:
                            self.cp(O[:, t0 + c0:t0 + c0 + CH], pO[:, 0:CH], [pOk], [f"hgO.{b}"], eng="act")
                        else:
                            self.tt(O[:, t0 + c0:t0 + c0 + CH], O[:, t0 + c0:t0 + c0 + CH], pO[:, 0:CH], ALU.add,
                                    [pOk, f"hgO.{b}"], [f"hgO.{b}"])

                    stage1a(clist[0])
                    if len(clist) > 1:
                        stage1a(clist[1])
                    stage1b(clist[0])
                    for i_, ci in enumerate(clist):
                        if i_ + 2 < len(clist):
                            stage1a(clist[i_ + 2])
                        if i_ + 1 < len(clist):
                            stage1b(clist[i_ + 1])
                        stage2(ci)
                        if i_ == 3 and nxt_b is not None:
                            part1(nxt_b)
            if h == 3:
                self.dump(f"hgqs{e}", qs[:], [128, NT], BF16, [f"hgqs.{b}" for b in range(5)])
                self.dump(f"hgO{e}", O[:], [128, NT], F32, [f"hgO.{b}" for b in range(5)])
                self.dump(f"hgvtm{e}", vtm[:], [32, 72, 128], BF16, [f"hgvtm.{b}" for b in range(5)])
                self.dump(f"hgEB{e}", EB[:], [128, 512], F32, ["hgEB"])
                self.dump(f"hgKK{e}", KK[:], [128, 512], F32, ["hgKK"])
                self.dump(f"hgT5{e}", T[5][:], [128, 512], F32, ["hgT5"])
                self.dump(f"hgLB{e}", LB[:], [128, 4], F32, ["hgLB"])
            wgt, wgtk = wl(4)
            for b, (t0, n, v) in enumerate(TB):
                self.act(sqb[:, :n], O[:, t0:t0 + n], AF.Square, [f"hgO.{b}"], ["hgQR"])
                pR, pRk = self.ps()
                self.mm(pR[:, :n], self.onesb[:], sqb[:, :n], True, True, ["onesb", "hgQR"], [pRk])
                self.act(T[1][:, :n], pR[:, :n], AF.Ln, [pRk, "epsT"], ["hgT1"], bias=self.epsT[:], scale=1.0 / 128.0)
                self.act(T[1][:, :n], T[1][:, :n], AF.Exp, ["hgT1"], ["hgT1"], scale=-0.5)
                pg, pgk = self.ps()
                for kc in range(8):
                    self.mm(pg[:, :n], wgt[:, kc, :], self.A[:, kc, t0:t0 + n], kc == 0, kc == 7, [wgtk, f"A{kc}.{b}"], [pgk])
                self.act(T[2][:, :n], pg[:, :n], AF.Silu, [pgk], ["hgT2"])
                self.tt(T[1][:, :n], T[1][:, :n], O[:, t0:t0 + n], ALU.mult, ["hgT1", f"hgO.{b}"], ["hgT1"])
                self.tt(T[1][:, :n], T[1][:, :n], T[2][:, :n], ALU.mult, ["hgT1", "hgT2"], ["hgT1"])
                self.act(self.Mb[:, h, t0:t0 + n], T[1][:, :n], AF.Identity, ["hgT1", "eprm"], [f"M{4 + h}.{b}"],
                         scale=prm[:, PE_ON:PE_ON + 1])

    def norm_rope(self, pq, pqk, rows, gm, gmk, inv_dim, gain, t0, n, is_x, dest, destk, tm, tag="", alt=False):
        sq, rs, qn, qb, t1, t2, cosT, sinT = tm
        ksq, krs = f"nr{tag}_sq", f"nr{tag}_rs"
        if alt:
            assert not is_x
            sq, rs, ksq, krs = qb, t1, f"nr{tag}_qb", f"nr{tag}_t1"
        self.act(sq[:rows, :n], pq, AF.Square, [pqk], [ksq])
        pn, pnk = self.ps()
        self.mm(pn[:rows, :n], gm, sq[:rows, :n], True, True, [gmk, ksq], [pnk])
        self.act(rs[:rows, :n], pn[:rows, :n], AF.Ln, [pnk, "epsT"], [krs], bias=self.epsT[:rows, :], scale=inv_dim)
        self.act(rs[:rows, :n], rs[:rows, :n], AF.Exp, [krs], [krs], scale=-0.5)
        if not is_x:
            self.stt(dest, pq, gain, rs[:rows, :n], ALU.mult, ALU.mult, [pqk, krs, "oprm"], destk)
            return
        self.stt(qn[:rows, :n], pq, gain, rs[:rows, :n], ALU.mult, ALU.mult, [pqk, krs, "oprm"], [f"nr{tag}_qn"])
        self.cp(qb[:rows, :n], qn[:rows, :n], [f"nr{tag}_qn"], [f"nr{tag}_qb"], eng="act")
        pr, prk = self.ps()
        self.mm(pr[:rows, :n], self.permb[:rows, :rows], qb[:rows, :n], True, True, ["permb", f"nr{tag}_qb"], [prk])
        x0 = t0 - CTX
        self.dma(cosT[:rows, :n], self.d_cos[0:rows, x0:x0 + n], (), [f"nr{tag}_cos"])
        self.dma(sinT[:rows, :n], self.d_sin[0:rows, x0:x0 + n], (), [f"nr{tag}_sin"])
        self.tt(t1[:rows, :n], qn[:rows, :n], cosT[:rows, :n], ALU.mult, [f"nr{tag}_qn", f"nr{tag}_cos"], [f"nr{tag}_t1"])
        self.tt(t2[:rows, :n], pr[:rows, :n], sinT[:rows, :n], ALU.mult, [prk, f"nr{tag}_sin"], [f"nr{tag}_t2"])
        self.tt(dest, t1[:rows, :n], t2[:rows, :n], ALU.add, [f"nr{tag}_t1", f"nr{tag}_t2"], destk)

    def odd_mixer(self, o, l, es):
        lam_init = 0.8 - 0.6 * math.exp(-0.3 * l)
        prm = self.sb(es, f"oprm{o}", [128, PO_N])
        self.dma(prm[:], self.d_po[o], (), ["oprm"])
        self.permb = self.sb(es, f"permb{o}", [128, 128], BF16)
        self.dma(self.permb[:], self.d_perm, (), ["permb"], q="pool")
        self.bdb = self.sb(es, f"bdb{o}", [128, 128], BF16)
        self.cp(self.bdb[:], self.cf[:, CF_BD:CF_BD + 128], ["cf"], ["bdb"])
        S0 = lambda name, shape, dt=F32: self.sb(es, f"od{name}{o}", shape, dt)
        lp = S0("lp", [128, 2])
        nlam = S0("nlam", [128, 1])
        subg = S0("subg", [128, 1])
        with self.scope() as esl:
            onesf = self.sb(esl, f"onesf{o}", [128, 128])
            self.memset(onesf[:], 1.0, ["onesf"])
            self.memset(lp[:], 0.0, ["odlp"])
            self.tt(lp[0:64, 0:1], prm[0:64, PO_LAM:PO_LAM + 1], prm[0:64, PO_LAM + 1:PO_LAM + 2], ALU.mult, ["oprm", "odlp"], ["odlp"])
            self.tt(lp[0:64, 1:2], prm[0:64, PO_LAM + 2:PO_LAM + 3], prm[0:64, PO_LAM + 3:PO_LAM + 4], ALU.mult, ["oprm", "odlp"], ["odlp"])
            pl, plk = self.ps()
            self.mm(pl[:, 0:2], onesf[:], lp[:], True, True, ["onesf", "odlp"], [plk])
            self.act(lp[:], pl[:, 0:2], AF.Exp, [plk], ["odlp"])
            self.tt(nlam[:], lp[:, 1:2], lp[:, 0:1], ALU.subtract, ["odlp"], ["odnlam"])
            self.ts(nlam[:], nlam[:], -lam_init, None, ALU.add, None, ["odnlam"], ["odnlam"])
            self.ts(subg[:], prm[:, PO_SUB:PO_SUB + 1], 1.0 - lam_init, None, ALU.mult, None, ["oprm"], ["odsubg"])
        tm = (S0("sq", [128, 512], BF16), S0("rs", [128, 512]), S0("qn", [128, 512]), S0("qb", [128, 512], BF16),
              S0("t1", [128, 512]), S0("t2", [128, 512]), S0("cosT", [128, 512]), S0("sinT", [128, 512]))
        Pt = [S0(f"P{i}", [128, 512], BF16) for i in range(4)]
        orec = S0("orec", [128, 512])
        oacc = S0("oacc", [128, 512])

        def attend(qblk, ktiles, score_fn, v_fn, nacc, scale, finish, zacc=None):
            t0, n, v = TB[qblk]
            accs = [(self.psum[2 * i], f"ps{2 * i}", self.psum[2 * i + 1], f"ps{2 * i + 1}") for i in range(nacc)]
            self.ps_avail = list(range(2 * nacc, 8))
            nk = len(ktiles)
            depth = 1 if nacc == 2 else 3
            sc_ = {}

            def do_scores(ki):
                for i in range(nacc):
                    pS, pSk = self.ps()
                    score_fn(i, pS, pSk, ktiles[ki], t0, n)
                    sc_[(ki, i)] = (pS, pSk)

            for ki in range(min(depth, nk)):
                do_scores(ki)
            for ki, kt in enumerate(ktiles):
                if ki + depth < nk:
                    do_scores(ki + depth)
                for i in range(nacc):
                    pS, pSk = sc_.pop((ki, i))
                    P = Pt[(nacc * ki + i) % 4]
                    Pk = f"odP{(nacc * ki + i) % 4}"
                    self.act(P[:, :n], pS[:, :n], AF.Exp, [pSk], [Pk], scale=scale)
                    O_, Ok, Z_, Zk = accs[i]
                    vl, vk = v_fn(kt)
                    self.mm(O_[:, :n], vl, P[:, :n], ki == 0, ki == nk - 1, [vk, Pk], [Ok])
                    if zacc is None or i != 0:
                        self.mm(Z_[:, :n], self.onesb[:], P[:, :n], ki == 0, ki == nk - 1, ["onesb", Pk], [Zk])
                    else:
                        Zf, _ = zacc
                        if ki == 0:
                            self.cp(Zf[:, :n], P[:, :n], [Pk], ["odZf"])
                        else:
                            self.tt(Zf[:, :n], Zf[:, :n], P[:, :n], ALU.add, ["odZf", Pk], ["odZf"])
            if zacc is not None:
                Zf, ones_f = zacc
                O_, Ok, Z_, Zk = accs[0]
                self.mm(Z_[:, :n], ones_f[:], Zf[:, :n], True, True, ["odonesf", "odZf"], [Zk])
            finish(accs, t0, n, qblk)
            self.ps_avail = list(range(8))

        with self.scope() as es2:
            self.Ma = self.sb(es2, f"Ma{self.uid}", [128, 4, NT], BF16)
            self.uid += 1
            S = lambda name, shape, dt=F32: self.sb(es2, f"df{name}{o}", shape, dt)
            QD = S("QD", [128, NT], BF16)
            KD = S("KD", [128, NT], BF16)
            Vt = S("Vt", [128, 18, 128], BF16)
            sqo = S("sqo", [128, 512], BF16)
            tmB = (S("sqB", [128, 512], BF16), S("rsB", [128, 512]), S("qnB", [128, 512]), S("qbB", [128, 512], BF16),
                   S("t1B", [128, 512]), S("t2B", [128, 512]), S("cosB", [128, 512]), S("sinB", [128, 512]))
            tms = [(tm, ""), (tmB, "B")]
            Zf = S("Zf", [128, 512])
            ones_f = S("onesf2", [128, 128])
            self.memset(ones_f[:], 1.0, ["odonesf"])
            zacc = (Zf, ones_f)
            ncall = [0]
            for h in range(4):
                for which, col0, dst, dk, gcol in ((0, 0, QD, "dfQD", PO_QG), (1, 512, KD, "dfKD", PO_KG)):
                    w, wk = self.wload(self.d_cd_w_in[o][:, col0 + 128 * h:col0 + 128 * h + 128], 8, 128)

                    def evac(pst, pk, b, t0, n, v, dst=dst, dk=dk, gcol=gcol):
                        tmx, tagx = tms[ncall[0] % 2]
                        ncall[0] += 1
                        self.norm_rope(pst[:, :n], pk, 128, self.bdb[:], "bdb", 1.0 / 64.0, prm[:, gcol:gcol + 1], t0, n, v == 0,
                                       dst[:, t0:t0 + n], [f"{dk}.{b}"], tmx, tagx)
                    self.proj_chunk(w, wk, 0, 128, evac)
                wv, wvk = self.wload(self.d_cd_w_in[o][:, 1024 + 128 * h:1024 + 128 * h + 128], 8, 128)
                for tt_ in range(18):
                    b = 0 if tt_ < 2 else 1 + (tt_ - 2) // 4
                    pv_, pvk = self.ps()
                    for kc in range(8):
                        self.mm(pv_[:, 0:128], self.A[:, kc, tt_ * 128:(tt_ + 1) * 128], wv[:, kc, :], kc == 0, kc == 7,
                                [wvk, f"A{kc}.{b}"], [pvk])
                    self.cp(Vt[:, tt_, :], pv_[:, 0:128], [pvk], [f"dfVt.{tt_}"], eng="act")

                def score(i, pS, pSk, kt, t0, n):
                    bq = [b for b, tb in enumerate(TB) if tb[0] == t0][0]
                    bk = 0 if kt < 2 else 1 + (kt - 2) // 4
                    self.mm(pS[:, :n], KD[64 * i:64 * i + 64, kt * 128:(kt + 1) * 128], QD[64 * i:64 * i + 64, t0:t0 + n], True, True,
                            [f"dfKD.{bk}", f"dfQD.{bq}"], [pSk])

                def vfn(kt):
                    return Vt[:, kt, :], f"dfVt.{kt}"

                def finish(accs, t0, n, qblk, h=h):
                    (O1, O1k, Z1, Z1k), (O2, O2k, Z2, Z2k) = accs
                    r2 = tm[4]
                    self.act(orec[:, :n], Z1[:, :n], AF.Ln, [Z1k], ["odorec"])
                    self.act(orec[:, :n], orec[:, :n], AF.Exp, ["odorec"], ["odorec"], scale=-1.0)
                    self.act(r2[:, :n], Z2[:, :n], AF.Ln, [Z2k], ["nr_t1"])
                    self.act(r2[:, :n], r2[:, :n], AF.Exp, ["nr_t1"], ["nr_t1"], scale=-1.0)
                    self.tt(oacc[:, :n], O1[:, :n], orec[:, :n], ALU.mult, [O1k, "odorec"], ["odoacc"])
                    self.tt(orec[:, :n], O2[:, :n], r2[:, :n], ALU.mult, [O2k, "nr_t1", "odoacc"], ["odorec"])
                    self.stt(oacc[:, :n], orec[:, :n], nlam[:, 0:1], oacc[:, :n], ALU.mult, ALU.add, ["odorec", "odnlam", "odoacc"], ["odoacc"])
                    self.act(sqo[:, :n], oacc[:, :n], AF.Square, ["odoacc"], ["dfsqo"])
                    pn, pnk = self.ps()
                    self.mm(pn[:, :n], self.onesb[:], sqo[:, :n], True, True, ["onesb", "dfsqo"], [pnk])
                    self.act(orec[:, :n], pn[:, :n], AF.Ln, [pnk, "epsT"], ["odorec"], bias=self.epsT[:], scale=1.0 / 128.0)
                    self.act(orec[:, :n], orec[:, :n], AF.Exp, ["odorec"], ["odorec"], scale=-0.5)
                    self.stt(self.Ma[:, h, t0:t0 + n], oacc[:, :n], subg[:, 0:1], orec[:, :n], ALU.mult, ALU.mult,
                             ["odoacc", "odsubg", "odorec"], [f"M{h}.{qblk}"])
                if not self.skip_ctx:
                    attend(0, [0, 1], score, vfn, 2, 0.125, finish, zacc)
                for qblk in range(1, 5):
                    attend(qblk, list(range(18)), score, vfn, 2, 0.125, finish, zacc)
            self.dump(f"Mo{o}a", self.Ma[:], [128, 4, NT], BF16, [f"M{c}.{b}" for c in range(4) for b in range(5)])
            self.out_proj(l, [0, 1, 2, 3], lambda kc: self.Ma[:, kc, :])

        with self.scope() as es2:
            S = lambda name, shape, dt=F32: self.sb(es2, f"ml{name}{o}", shape, dt)
            CQn = S("CQn", [128, 3, NT], BF16)
            CKVn = S("CKVn", [128, 2, NT], BF16)
            KR = S("KR", [64, NT], BF16)
            Mh = S("Mh", [128, NT], BF16)
            VMh = S("VMh", [128, 18, 128], BF16)
            QN = S("QN", [128, NT], BF16)
            QR = S("QR", [64, NT], BF16)
            KN = S("KN", [128, NT], BF16)
            rawt = [tm[2], tm[4], tm[5]]
            rawk = ["nr_qn", "nr_t1", "nr_t2"]
            sq3, rs3 = tm[0], tm[1]
            for (col0, nch, dst, dk, gofs) in ((1536, 3, CQn, "mlCQn", PO_QA), (1920, 2, CKVn, "mlCKVn", PO_KVA)):
                w, wk = self.wload(self.d_cd_w_in[o][:, col0:col0 + nch * 128], 8, nch * 128)
                for b, (t0, n, v) in enumerate(TB):
                    pn, pnk = self.ps()
                    for c in range(nch):
                        pst, pk = self.ps()
                        for kc in range(8):
                            self.mm(pst[:, :n], w[:, kc, c * 128:(c + 1) * 128], self.A[:, kc, t0:t0 + n], kc == 0, kc == 7,
                                    [wk, f"A{kc}.{b}"], [pk])
                        self.cp(rawt[c][:, :n], pst[:, :n], [pk], [rawk[c]], eng="act")
                        self.act(sq3[:, :n], pst[:, :n], AF.Square, [pk], ["nr_sq"])
                        self.mm(pn[:, :n], self.onesb[:], sq3[:, :n], c == 0, c == nch - 1, ["onesb", "nr_sq"], [pnk])
                    self.act(rs3[:, :n], pn[:, :n], AF.Ln, [pnk, "epsT"], ["nr_rs"], bias=self.epsT[:], scale=1.0 / (nch * 128.0))
                    self.act(rs3[:, :n], rs3[:, :n], AF.Exp, ["nr_rs"], ["nr_rs"], scale=-0.5)
                    for c in range(nch):
                        self.stt(dst[:, c, t0:t0 + n], rawt[c][:, :n], prm[:, gofs + c:gofs + c + 1], rs3[:, :n], ALU.mult, ALU.mult,
                                 [rawk[c], "oprm", "nr_rs"], [f"{dk}{c}.{b}"])
            w, wk = self.wload(self.d_cd_w_in[o][:, 2176:2240], 8, 64)

            def evac_kr(pst, pk, b, t0, n, v):
                self.norm_rope(pst[0:64, :n], pk, 64, self.onesb[0:64, 0:64], "onesb", 1.0 / 64.0, prm[0:64, PO_RK:PO_RK + 1], t0, n,
                               v == 0, KR[:, t0:t0 + n], [f"mlKR.{b}"], tm)
            self.proj_chunk(w, wk, 0, 64, evac_kr)
            mscale = 192.0 ** -0.5
            for h in range(4):
                wuq, wuqk = self.wload(self.d_w_uq[o], 3, 768)
                wukv, wukvk = self.wload(self.d_w_ukv[o], 2, 1024)
                for b, (t0, n, v) in enumerate(TB):
                    pq, pqk = self.ps()
                    for kc in range(3):
                        self.mm(pq[:, :n], wuq[:, kc, h * 192:h * 192 + 128], CQn[:, kc, t0:t0 + n], kc == 0, kc == 2,
                                [wuqk, f"mlCQn{kc}.{b}"], [pqk])
                    self.norm_rope(pq[:, :n], pqk, 128, self.onesb[:], "onesb", 1.0 / 128.0, prm[:, PO_NQ:PO_NQ + 1], t0, n, False,
                                   QN[:, t0:t0 + n], [f"mlQN.{b}"], tm)
                    pk_, pkk_ = self.ps()
                    for kc in range(2):
                        self.mm(pk_[:, :n], wukv[:, kc, h * 256:h * 256 + 128], CKVn[:, kc, t0:t0 + n], kc == 0, kc == 1,
                                [wukvk, f"mlCKVn{kc}.{b}"], [pkk_])
                    self.norm_rope(pk_[:, :n], pkk_, 128, self.onesb[:], "onesb", 1.0 / 128.0, prm[:, PO_NK:PO_NK + 1], t0, n, False,
                                   KN[:, t0:t0 + n], [f"mlKN.{b}"], tm, "", True)
                    pr_, prk_ = self.ps()
                    for kc in range(3):
                        self.mm(pr_[0:64, :n], wuq[:, kc, h * 192 + 128:h * 192 + 192], CQn[:, kc, t0:t0 + n], kc == 0, kc == 2,
                                [wuqk, f"mlCQn{kc}.{b}"], [prk_])
                    self.norm_rope(pr_[0:64, :n], prk_, 64, self.onesb[0:64, 0:64], "onesb", 1.0 / 64.0, prm[0:64, PO_RQ:PO_RQ + 1],
                                   t0, n, v == 0, QR[:, t0:t0 + n], [f"mlQR.{b}"], tm)
                for tt_ in range(18):
                    b = 0 if tt_ < 2 else 1 + (tt_ - 2) // 4
                    pv_, pvk = self.ps()
                    for kc in range(2):
                        self.mm(pv_[:, 0:128], CKVn[:, kc, tt_ * 128:(tt_ + 1) * 128], wukv[:, kc, h * 256 + 128:h * 256 + 256],
                                kc == 0, kc == 1, [wukvk, f"mlCKVn{kc}.{b}"], [pvk])
                    self.cp(VMh[:, tt_, :], pv_[:, 0:128], [pvk], [f"mlVM.{tt_}"], eng="act")

                def score(i, pS, pSk, kt, t0, n):
                    bq = [b for b, tb in enumerate(TB) if tb[0] == t0][0]
                    bk = 0 if kt < 2 else 1 + (kt - 2) // 4
                    self.mm(pS[:, :n], KN[:, kt * 128:(kt + 1) * 128], QN[:, t0:t0 + n], True, False, [f"mlKN.{bk}", f"mlQN.{bq}"], [pSk])
                    self.mm(pS[:, :n], KR[:, kt * 128:(kt + 1) * 128], QR[:, t0:t0 + n], False, True, [f"mlKR.{bk}", f"mlQR.{bq}"], [pSk])

                def vfn(kt):
                    return VMh[:, kt, :], f"mlVM.{kt}"

                def finish(accs, t0, n, qblk, h=h):
                    ((O1, O1k, Z1, Z1k),) = accs
                    self.act(orec[:, :n], Z1[:, :n], AF.Ln, [Z1k], ["odorec"])
                    self.act(orec[:, :n], orec[:, :n], AF.Exp, ["odorec"], ["odorec"], scale=-1.0)
                    self.tt(Mh[:, t0:t0 + n], O1[:, :n], orec[:, :n], ALU.mult, [O1k, "odorec"], [f"mlMh.{qblk}"])
                if not self.skip_ctx:
                    attend(0, [0, 1], score, vfn, 1, mscale, finish)
                for qblk in range(1, 5):
                    attend(qblk, list(range(18)), score, vfn, 1, mscale, finish)
                self.dump(f"Mo{o}b{h}", Mh[:], [128, NT], BF16, [f"mlMh.{b}" for b in range(5)])
                self.out_proj(l, [4 + h], lambda kc: Mh[:, :], keyf=lambda kc, b: f"mlMh.{b}")


def make_in_maps(inp, batches):
    cf, c2 = host_consts()
    cos2, sin2, perm = host_rope()
    pv, pe, bt, po = pack_inputs(inp)
    f = lambda a: np.ascontiguousarray(np.asarray(a, np.float32))
    shared = {
        "cf": cf, "c2": c2, "ropecos": cos2, "ropesin": sin2, "ropeperm": perm, "pv": pv, "pe": pe, "bt": bt, "po": po,
        "ada_w": f(inp["ada_w"]), "w_out": f(inp["w_out"]), "ffn_w_in": f(inp["ffn_w_in"]), "ffn_w_out": f(inp["ffn_w_out"]),
        "ab_w_in": f(inp["ab_w_in"]), "s5_glu_w": f(inp["s5_glu_w"]), "cd_w_in": f(inp["cd_w_in"]),
        "mla_w_uq": f(inp["mla_w_uq"]), "mla_w_ukv": f(inp["mla_w_ukv"]),
    }
    maps = []
    cc = colmaj(f(inp["c_ctx"]), 8)
    for b in batches:
        m = dict(shared)
        m["h0"] = np.ascontiguousarray(np.concatenate([f(inp["ctx"])[b].T, f(inp["x"])[b].T], axis=1))
        m["sc"] = np.ascontiguousarray(np.stack([colmaj(f(inp["c"])[b], 8), cc], axis=-1))
        maps.append(m)
    return maps


def kernel(**inputs):
    nb = 8
    b = Builder(list(range(DEPTH)))
    nc = b.build()
    maps = make_in_maps(inputs, list(range(nb)))
    res = run_bass_kernel_spmd(nc, maps, core_ids=list(range(nb)))
    out = np.stack([np.asarray(res.results[i]["out"], np.float32).T for i in range(nb)], axis=0)
    return np.ascontiguousarray(out)
```

```python
from contextlib import ExitStack
import math
import numpy as np
import concourse.bass as bass
import concourse.mybir as mybir
from concourse.bass_utils import run_bass_kernel_spmd

F32 = mybir.dt.float32
BF16 = mybir.dt.bfloat16
I32 = mybir.dt.int32
ALU = mybir.AluOpType
AF = mybir.ActivationFunctionType

D = 1024
DEPTH = 4
NT = 2304
CTX = 256
SEQ = 2048
FFH = 2816
EPS = 1e-6
TB = [(0, 256, 1), (256, 512, 0), (768, 512, 0), (1280, 512, 0), (1792, 512, 0)]
TWO_PI = 2.0 * math.pi

ENGS = ("pe", "dve", "act", "pool", "sp")
NDSEM = 12
FENCE_DMA = True
FENCE_ON = True
PREFETCH_ADA = True
SKIP = None
LIM = {}


class Op:
    __slots__ = ("eng", "fn", "deps", "sig", "cnt", "dma", "dsem", "dval", "idx")


class KB:
    def __init__(self, nc):
        self.nc = nc
        self.ops = []
        self.last_w = {}
        self.readers = {}
        self.known = {e: {} for e in ENGS}
        self.dma_cnt = {e: 0 for e in ENGS}
        self.dma_last = {}
        self.dma_n = {}
        self.pending = {e: [] for e in ENGS}
        self.last_op = {}

    def fence(self):
        toks = list(self.last_op.values()) + (list(self.dma_last.values()) if FENCE_DMA else [])
        for e in ENGS:
            self.pending[e] = list(toks)

    def _need(self, eng, tok, deps):
        if tok is None:
            return
        src = self.ops[tok]
        if src.dma:
            key = ("d", src.eng, src.dsem)
        else:
            key = src.eng
            if src.eng == "pe" and eng == "pe":
                return
        if self.known[eng].get(key, -1) >= tok:
            return
        self.known[eng][key] = tok
        deps.append(tok)

    def op(self, eng, fn, R=(), W=(), dma=False):
        o = Op()
        o.eng, o.fn, o.dma, o.sig, o.cnt = eng, fn, dma, False, 0
        o.idx = len(self.ops)
        deps = []
        if self.pending[eng]:
            for t in self.pending[eng]:
                self._need(eng, t, deps)
            self.pending[eng] = []
        for r in R:
            self._need(eng, self.last_w.get(r), deps)
            if isinstance(r, str) and r.startswith("ps"):
                for k, t in self.readers.get(r, {}).items():
                    if k != eng:
                        self._need(eng, t, deps)
        for w in W:
            self._need(eng, self.last_w.get(w), deps)
            for t in self.readers.get(w, {}).values():
                self._need(eng, t, deps)
        if dma:
            k = self.dma_cnt[eng] % NDSEM
            self.dma_cnt[eng] += 1
            o.dsem = k
            prev = self.dma_last.get((eng, k))
            if prev is not None:
                self._need(eng, prev, deps)
            self.dma_last[(eng, k)] = o.idx
            self.dma_n[(eng, k)] = self.dma_n.get((eng, k), 0) + 1
            o.dval = 16 * self.dma_n[(eng, k)]
        o.deps = deps
        self.ops.append(o)
        if not dma:
            self.last_op[eng] = o.idx
        for r in R:
            self.readers.setdefault(r, {})[eng if not dma else ("d", o.idx)] = o.idx
        for w in W:
            self.last_w[w] = o.idx
            self.readers[w] = {}
        return o.idx

    def emit(self, final_wait_ops=()):
        nc = self.nc
        for o in self.ops:
            for d in o.deps:
                self.ops[d].sig = True
        cnt = {e: 0 for e in ENGS}
        for o in self.ops:
            if o.dma:
                continue
            if o.sig:
                cnt[o.eng] += 1
            o.cnt = cnt[o.eng]
        engobj = {"pe": nc.tensor, "dve": nc.vector, "act": nc.scalar, "pool": nc.gpsimd, "sp": nc.sync}
        with ExitStack() as es:
            sem = {e: es.enter_context(nc.semaphore("s_" + e)) for e in ENGS}
            dsem = {}
            for e in ("sp", "act", "pool"):
                for k in range(NDSEM):
                    dsem[(e, k)] = es.enter_context(nc.semaphore(f"d_{e}{k}"))
            per = {e: [] for e in ENGS}
            for o in self.ops:
                per[o.eng].append(o)
            fin = [self.ops[i] for i in final_wait_ops]

            def run(e):
                eo = engobj[e]
                for o in per[e]:
                    for d in o.deps:
                        s = self.ops[d]
                        if s.dma:
                            eo.wait_ge(dsem[(s.eng, s.dsem)], s.dval)
                        else:
                            eo.wait_ge(sem[s.eng], s.cnt)
                    ins = o.fn(eo)
                    if o.dma:
                        ins.then_inc(dsem[(o.eng, o.dsem)], 16)
                    elif o.sig:
                        ins.then_inc(sem[o.eng], 1)
                if e == "sp":
                    for s in fin:
                        eo.wait_ge(dsem[(s.eng, s.dsem)], s.dval)

            with nc.Block() as block:
                @block.tensor
                def _(t):
                    run("pe")

                @block.vector
                def _(v):
                    run("dve")

                @block.scalar
                def _(s):
                    run("act")

                @block.gpsimd
                def _(g):
                    run("pool")

                @block.sync
                def _(s):
                    run("sp")


def rev(ap2d, n):
    a = [list(x) for x in ap2d.ap]
    assert len(a) == 2 and a[1][1] == n
    return bass.AP(ap2d.tensor, ap2d.offset + a[1][0] * (n - 1), [a[0], [-a[1][0], n]])


CF_ID, CF_J, CF_MF, CF_MB, CF_MLO, CF_MHI, CF_SGN, CF_BD, CF_N = (0, 128, 256, 320, 384, 385, 386, 387, 515)
C2_IOTA, C2_CMASK, C2_CMASKB, C2_N = 0, 512, 1024, 1536


def host_consts():
    cf = np.zeros((128, CF_N), np.float32)
    cf[:, CF_ID:CF_ID + 128] = np.eye(128, dtype=np.float32)
    J = np.zeros((128, 128), np.float32)
    for p in range(64):
        J[p, p + 64] = 1.0
        J[p + 64, p] = 1.0
    cf[:, CF_J:CF_J + 128] = J
    c2 = np.zeros((128, C2_N), np.float32)
    c2[:, C2_IOTA:C2_IOTA + 512] = np.arange(512, dtype=np.float32)[None, :]
    cm = np.ones(512, np.float32)
    cm[::32] = 0.0
    c2[:, C2_CMASK:C2_CMASK + 512] = cm[None, :]
    cmb = np.ones(512, np.float32)
    cmb[31::32] = 0.0
    c2[:, C2_CMASKB:C2_CMASKB + 512] = cmb[None, :]
    s = np.arange(64)
    cf[:64, CF_MF:CF_MF + 64] = (s[:, None] <= s[None, :]).astype(np.float32)
    cf[:64, CF_MB:CF_MB + 64] = (s[:, None] >= s[None, :]).astype(np.float32)
    cf[:64, CF_MLO] = 1.0
    cf[64:, CF_MHI] = 1.0
    cf[:64, CF_SGN] = 1.0
    cf[64:, CF_SGN] = -1.0
    bd = np.zeros((128, 128), np.float32)
    bd[:64, :64] = 1.0
    bd[64:, 64:] = 1.0
    cf[:, CF_BD:CF_BD + 128] = bd
    return cf, c2


ROPE_DIM = 64


def host_rope():
    n_freq = ROPE_DIM // 4
    inv = np.power(np.float32(10000.0), -np.arange(n_freq, dtype=np.float32) / np.float32(n_freq)).astype(np.float32)
    t = np.arange(SEQ)
    rows = (t // 64).astype(np.float32)
    cols = (t % 64).astype(np.float32)
    ang_r = rows[:, None] * inv[None, :]
    ang_c = cols[:, None] * inv[None, :]
    ang = np.concatenate([ang_r, ang_r, ang_c, ang_c], axis=-1).astype(np.float32)
    cos = np.cos(ang).astype(np.float32).T
    sin = np.sin(ang).astype(np.float32).T
    sgn = np.ones((64, 1), np.float32)
    sgn[0:16] = -1.0
    sgn[32:48] = -1.0
    sins = sin * sgn
    cos2 = np.concatenate([cos, cos], 0)
    sin2 = np.concatenate([sins, sins], 0)
    perm = np.zeros((128, 128), np.float32)
    for base in (0, 64):
        for m in range(64):
            seg = m // 32
            r = m % 32
            k = seg * 32 + (r + 16) % 32
            perm[base + k, base + m] = 1.0
    return np.ascontiguousarray(cos2), np.ascontiguousarray(sin2), perm


PV_ADAB, PV_NM, PV_NF, PV_N = 0, 48, 56, 64
PE_SD, PE_GB, PE_LBL, PE_ON, PE_LRE, PE_LIM, PE_LST, PE_CR, PE_CI, PE_N = 0, 4, 8, 16, 17, 81, 145, 209, 1233, 2257
PO_LAM, PO_QG, PO_KG, PO_SUB, PO_QA, PO_KVA, PO_NQ, PO_NK, PO_RQ, PO_RK, PO_N = 0, 4, 5, 6, 7, 10, 12, 13, 14, 15, 16


def colmaj(v, nch):
    return np.ascontiguousarray(np.asarray(v, np.float32).reshape(nch, 128).T)


def pack_inputs(inp):
    f = lambda a: np.asarray(a, np.float32)
    pv = np.zeros((DEPTH, 128, PV_N), np.float32)
    for l in range(DEPTH):
        pv[l, :, PV_ADAB:PV_ADAB + 48] = colmaj(f(inp["ada_b"])[l], 48)
        pv[l, :, PV_NM:PV_NM + 8] = colmaj(f(inp["norm_mix"])[l], 8)
        pv[l, :, PV_NF:PV_NF + 8] = colmaj(f(inp["norm_ffn"])[l], 8)
    ne = 2
    pe = np.zeros((ne, 128, PE_N), np.float32)
    bt = np.zeros((ne, 2, 32, 16, 128), np.float32)
    dup = lambda a: np.concatenate([a, a], 0)
    for e in range(ne):
        pe[e, :, PE_SD:PE_SD + 4] = colmaj(f(inp["s5_d"])[e], 4)
        pe[e, :, PE_GB:PE_GB + 4] = colmaj(f(inp["s5_glu_b"])[e], 4)
        for e2 in range(ne):
            pe[e, :, PE_LBL + 4 * e2:PE_LBL + 4 * e2 + 4] = colmaj(f(inp["hgrn_lb_logits"])[e2], 4)
        pe[e, :, PE_ON] = f(inp["hgrn_out_norm"])[e]
        for d in range(2):
            pe[e, :, PE_LRE + 32 * d:PE_LRE + 32 * d + 32] = dup(f(inp["s5_lambda_re"])[e, d].T)
            pe[e, :, PE_LIM + 32 * d:PE_LIM + 32 * d + 32] = dup(f(inp["s5_lambda_im"])[e, d].T)
            pe[e, :, PE_LST + 32 * d:PE_LST + 32 * d + 32] = f(inp["s5_log_step"])[e, d][None, :]
            cr = f(inp["s5_c_re"])[e, d]
            ci = f(inp["s5_c_im"])[e, d]
            crt = dup(cr.transpose(2, 0, 1).reshape(64, 32 * 16))
            cit = dup(ci.transpose(2, 0, 1).reshape(64, 32 * 16))
            pe[e, :, PE_CR + 512 * d:PE_CR + 512 * d + 512] = crt
            pe[e, :, PE_CI + 512 * d:PE_CI + 512 * d + 512] = cit
            br = f(inp["s5_b_re"])[e, d]
            bi = f(inp["s5_b_im"])[e, d]
            bt[e, d, :, :, 0:64] = br.transpose(0, 2, 1)
            bt[e, d, :, :, 64:128] = bi.transpose(0, 2, 1)
    no = 2
    po = np.zeros((no, 128, PO_N), np.float32)
    for o in range(no):
        po[o, :64, PO_LAM:PO_LAM + 4] = f(inp["diff_lambda"])[o].T
        po[o, :, PO_QG] = np.tile(f(inp["diff_qk_norm"])[o, 0], 2)
        po[o, :, PO_KG] = np.tile(f(inp["diff_qk_norm"])[o, 1], 2)
        po[o, :, PO_SUB] = f(inp["diff_subln"])[o]
        po[o, :, PO_QA:PO_QA + 3] = colmaj(f(inp["mla_q_a_norm"])[o], 3)
        po[o, :, PO_KVA:PO_KVA + 2] = colmaj(f(inp["mla_kv_a_norm"])[o], 2)
        po[o, :, PO_NQ] = f(inp["mla_nope_norm"])[o, 0]
        po[o, :, PO_NK] = f(inp["mla_nope_norm"])[o, 1]
        po[o, :64, PO_RQ] = f(inp["mla_rope_norm"])[o, 0]
        po[o, :64, PO_RK] = f(inp["mla_rope_norm"])[o, 1]
    return pv, pe, bt, po


class Builder:
    def __init__(self, layers, dbg=None):
        self.layers = layers
        self.dbg = dbg
        nc = bass.Bass("TRN2", target_bir_lowering=False)
        self.nc = nc
        self.kb = KB(nc)
        dt = lambda name, shape, kind="ExternalInput", dtype=F32: nc.dram_tensor(name, list(shape), dtype, kind=kind).ap()
        self.d_h0 = dt("h0", [D, NT])
        self.d_sc = dt("sc", [128, 8, 2])
        self.d_cf = dt("cf", [128, CF_N])
        self.d_c2 = dt("c2", [128, C2_N])
        self.d_cos = dt("ropecos", [128, SEQ])
        self.d_sin = dt("ropesin", [128, SEQ])
        self.d_perm = dt("ropeperm", [128, 128])
        self.d_pv = dt("pv", [DEPTH, 128, PV_N])
        self.d_pe = dt("pe", [2, 128, PE_N])
        self.d_bt = dt("bt", [2, 2, 32, 16, 128])
        self.d_po = dt("po", [2, 128, PO_N])
        self.d_ada_w = dt("ada_w", [DEPTH, D, 6 * D])
        self.d_w_out = dt("w_out", [DEPTH, D, D])
        self.d_ffn_w_in = dt("ffn_w_in", [DEPTH, D, 2 * FFH])
        self.d_ffn_w_out = dt("ffn_w_out", [DEPTH, FFH, D])
        self.d_ab_w_in = dt("ab_w_in", [2, D, 3072])
        self.d_glu_w = dt("s5_glu_w", [2, 512, 512])
        self.d_cd_w_in = dt("cd_w_in", [2, D, 2240])
        self.d_w_uq = dt("mla_w_uq", [2, 384, 768])
        self.d_w_ukv = dt("mla_w_ukv", [2, 256, 1024])
        self.d_out = dt("out", [D, SEQ], kind="ExternalOutput")
        self.final = []
        self.skip_ctx = False
        self.need_fence = False
        self.wb_i = 0
        self.ps_i = 0
        self.ps_avail = list(range(8))
        self.uid = 0

    def scope(self):
        b = self

        class _Scope(ExitStack):
            def __exit__(self, *a):
                r = ExitStack.__exit__(self, *a)
                b.need_fence = True
                return r

            def close(self):
                ExitStack.close(self)
                b.need_fence = True
        return _Scope()

    def sb(self, es, name, shape, dtype=F32):
        if self.need_fence and FENCE_ON:
            self.kb.fence()
            self.need_fence = False
        return es.enter_context(self.nc.sbuf_tensor("sb_" + name, list(shape), dtype))

    def mm(self, out, lhsT, rhs, start, stop, R, W):
        self.kb.op("pe", lambda e: e.matmul(out, lhsT=lhsT, rhs=rhs, start=start, stop=stop), R, W)

    def tr(self, out, in_, ident, R, W):
        self.kb.op("pe", lambda e: e.transpose(out, in_, ident), R, W)

    def act(self, out, in_, func, R, W, bias=None, scale=1.0):
        if bias is None:
            self.kb.op("act", lambda e: e.activation(out=out, in_=in_, func=func, scale=scale), R, W)
        else:
            self.kb.op("act", lambda e: e.activation(out=out, in_=in_, func=func, bias=bias, scale=scale), R, W)

    def tt(self, out, in0, in1, op, R, W, eng="dve"):
        self.kb.op(eng, lambda e: e.tensor_tensor(out=out, in0=in0, in1=in1, op=op), R, W)

    def ts(self, out, in0, s1, s2, op0, op1, R, W, eng="dve"):
        if s2 is None:
            self.kb.op(eng, lambda e: e.tensor_scalar(out=out, in0=in0, scalar1=s1, scalar2=None, op0=op0), R, W)
        else:
            self.kb.op(eng, lambda e: e.tensor_scalar(out=out, in0=in0, scalar1=s1, scalar2=s2, op0=op0, op1=op1), R, W)

    def stt(self, out, in0, scalar, in1, op0, op1, R, W, eng="dve"):
        self.kb.op(eng, lambda e: e.scalar_tensor_tensor(out=out, in0=in0, scalar=scalar, in1=in1, op0=op0, op1=op1), R, W)

    def cp(self, out, in_, R, W, eng="dve"):
        if eng == "act":
            self.kb.op("act", lambda e: e.copy(out=out, in_=in_), R, W)
        else:
            self.kb.op(eng, lambda e: e.tensor_copy(out=out, in_=in_), R, W)

    def memset(self, ap, val, W, eng="dve"):
        self.kb.op(eng, lambda e: e.memset(ap, val), (), W)

    def dma(self, out, in_, R, W, q="sp"):
        return self.kb.op(q, lambda e: e.dma_start(out=out, in_=in_), R, W, dma=True)

    def dump(self, name, ap, shape, dtype, R):
        if not self.dbg:
            return
        t = self.nc.dram_tensor("dbg_" + name, list(shape), dtype, kind="ExternalOutput").ap()
        self.final.append(self.dma(t, ap, R, (), q="sp"))

    def scan(self, out, d0, d1, init, R, W):
        self.kb.op("dve", lambda e: e.tensor_tensor_scan(out=out, data0=d0, data1=d1, initial=init, op0=ALU.mult, op1=ALU.add), R, W)

    def ps(self):
        k = self.ps_avail[self.ps_i % len(self.ps_avail)]
        self.ps_i += 1
        return self.psum[k], f"ps{k}"

    def wload(self, src, kc, n):
        i = self.wb_i % len(self.wb)
        self.wb_i += 1
        t = self.wb[i]
        key = f"wb{i}"
        assert kc * n <= 4096
        v = bass.AP(t.tensor, t.offset, [list(t.ap[0]), [n, kc], [1, n]])
        self.dma(v, src.rearrange("(kc p) n -> p kc n", p=128), (), [key], q="pool")
        return v, key

    def build(self):
        nc = self.nc
        with ExitStack() as es:
            self.H = self.sb(es, "H", [128, 8, NT])
            self.A = self.sb(es, "A", [128, 8, NT], BF16)
            self.cf = self.sb(es, "cf", [128, CF_N])
            self.onesb = self.sb(es, "onesb", [128, 128], BF16)
            self.s2 = self.sb(es, "s2", [128, 8, 2], BF16)
            self.s2f = self.sb(es, "s2f", [128, 8, 2])
            self.mod2 = [self.sb(es, f"mod{i}", [128, 48, 2]) for i in range(2)]
            self.gs2 = [self.sb(es, f"gs{i}", [128, 2, 8, 2]) for i in range(2)]
            pvt1 = self.sb(es, "pvt", [128, PV_N])
            self.pvt2 = [pvt1, pvt1]
            self.epsT = self.sb(es, "epsT", [128, 1])
            self.oneT = self.sb(es, "oneT", [128, 1])
            self.wbt = [self.sb(es, f"wb{i}", [128, 4096], BF16) for i in range(3)]
            self.wb = [t[:] for t in self.wbt]
            self.psum = [es.enter_context(nc.psum_tensor(f"ps{i}", [128, 512], F32)) for i in range(8)]
            self.ident = self.cf[:, CF_ID:CF_ID + 128]

            self.dma(self.cf[:], self.d_cf, (), ["cf"])
            self.dma(self.s2f[:], self.d_sc, (), ["s2f"])
            for c in range(8):
                self.dma(self.H[:, c, :], self.d_h0[c * 128:(c + 1) * 128, :], (), [f"H{c}.{b}" for b in range(5)], q="sp")
            self.memset(self.onesb[:], 1.0, ["onesb"])
            self.memset(self.epsT[:], EPS, ["epsT"])
            self.memset(self.oneT[:], 1.0, ["oneT"])
            self.act(self.s2[:], self.s2f[:], AF.Silu, ["s2f"], ["s2"])

            for l in self.layers:
                self.layer(l)

            for c in range(8):
                self.final.append(self.dma(self.d_out[c * 128:(c + 1) * 128, :], self.H[:, c, CTX:NT],
                                           [f"H{c}.{b}" for b in range(5)], (), q="sp"))
            self.kb.emit(final_wait_ops=self.final)
        return nc

    def ada_begin(self, l, bank):
        self.dma(self.pvt2[l % 2][:], self.d_pv[l], (), ["pvt"])
        self.ada_ps = (self.psum[bank], f"ps{bank}")

    def ada_piece(self, l, jg, loader):
        pst, pk = self.ada_ps
        w, wk = loader(self.d_ada_w[l][:, jg * 512:(jg + 1) * 512])
        for jj in range(4):
            j = jg * 4 + jj
            for kc in range(8):
                self.mm(pst[:, 2 * j:2 * j + 2], w[:, kc, jj * 128:(jj + 1) * 128], self.s2[:, kc, :], kc == 0, kc == 7,
                        [wk, "s2"], [pk])

    def ada_end(self, l):
        pst, pk = self.ada_ps
        p = l % 2
        mod, gs, pvt = self.mod2[p], self.gs2[p], self.pvt2[p]
        pv3 = pst[:, 0:96].rearrange("p (j v) -> p j v", v=2)
        self.tt(mod[:], pv3, pvt[:, PV_ADAB:PV_ADAB + 48].unsqueeze(2).to_broadcast([128, 48, 2]), ALU.add,
                [pk, "pvt"], [f"mod{p}"])
        for w_, (nofs, sofs) in enumerate(((PV_NM, 8), (PV_NF, 32))):
            self.ts(gs[:, w_, :, :], mod[:, sofs:sofs + 8, :], 1.0, None, ALU.add, None, [f"mod{p}"], [f"gs{p}"])
            self.tt(gs[:, w_, :, :], gs[:, w_, :, :],
                    pvt[:, nofs:nofs + 8].unsqueeze(2).to_broadcast([128, 8, 2]), ALU.mult, [f"gs{p}", "pvt"], [f"gs{p}"])

    def ada(self, l):
        self.ada_begin(l, 7)
        for jg in range(12):
            self.ada_piece(l, jg, lambda src: self.wload(src, 8, 512))
        self.ada_end(l)

    def norm_mod(self, w_, shift_ofs, es, skip0=False):
        sq = self.sb(es, f"nsq{self.uid}", [128, 2, 512], BF16)
        rstd = self.sb(es, f"nrs{self.uid}", [128, 512])
        tmp = self.sb(es, f"ntmp{self.uid}", [128, 2, 512])
        self.uid += 1
        for b, (t0, n, v) in enumerate(TB):
            if skip0 and b == 0:
                continue
            pst, pk = self.ps()
            for c in range(8):
                self.act(sq[:, c % 2, :n], self.H[:, c, t0:t0 + n], AF.Square, [f"H{c}.{b}"], [f"nsq{c % 2}"])
                self.mm(pst[:, :n], self.onesb[:], sq[:, c % 2, :n], c == 0, c == 7, [f"nsq{c % 2}", "onesb"], [pk])
            self.act(rstd[:, :n], pst[:, :n], AF.Ln, [pk, "epsT"], ["nrstd"], bias=self.epsT[:], scale=1.0 / D)
            self.act(rstd[:, :n], rstd[:, :n], AF.Exp, ["nrstd"], ["nrstd"], scale=-0.5)
            for c in range(8):
                self.tt(tmp[:, c % 2, :n], self.H[:, c, t0:t0 + n], rstd[:, :n], ALU.mult, [f"H{c}.{b}", "nrstd"], [f"ntmp{c % 2}"])
                self.act(self.A[:, c, t0:t0 + n], tmp[:, c % 2, :n], AF.Identity, [f"ntmp{c % 2}", self.kgs, self.kmod], [f"A{c}.{b}"],
                         bias=self.mod[:, shift_ofs + c, v:v + 1], scale=self.gs[:, w_, c, v:v + 1])

    def out_proj(self, l, kcs, src, keyf=None):
        nk = len(kcs)
        ws = []
        for og in range(2):
            ws.append(self.wload(self.d_w_out[l][kcs[0] * 128:(kcs[-1] + 1) * 128, og * 512:(og + 1) * 512], nk, 512))
        for o in range(8):
            w, wk = ws[o // 4]
            for b, (t0, n, v) in enumerate(TB):
                if self.skip_ctx and b == 0:
                    continue
                pst, pk = self.ps()
                for i, kc in enumerate(kcs):
                    self.mm(pst[:, :n], w[:, i, (o % 4) * 128:(o % 4 + 1) * 128], src(kc)[:, t0:t0 + n], i == 0, i == nk - 1,
                            [wk, keyf(kc, b) if keyf else f"M{kc}.{b}"], [pk])
                self.stt(self.H[:, o, t0:t0 + n], pst[:, :n], self.mod[:, 16 + o, v:v + 1], self.H[:, o, t0:t0 + n],
                         ALU.mult, ALU.add, [pk, self.kmod, f"H{o}.{b}"], [f"H{o}.{b}"])

    def ffn(self, l, es, nxt=None):
        hact = self.sb(es, f"hact{l}", [128, 4, NT], BF16)
        sg = self.sb(es, f"fsg{l}", [128, 2, 512])
        groups = [(g * 4, 4) for g in range(5)] + [(20, 2)]
        if nxt is not None:
            adaw = [self.sb(es, f"adaw{l}_{i}", [128, 4096], BF16) for i in range(2)]
            acnt = [0]

            def aload(src):
                i = acnt[0] % 2
                acnt[0] += 1
                t = adaw[i][:]
                v_ = bass.AP(t.tensor, t.offset, [list(t.ap[0]), [512, 8], [1, 512]])
                self.dma(v_, src.rearrange("(kc p) n -> p kc n", p=128), (), [f"adaw{i}"], q="pool")
                return v_, f"adaw{i}"
            self.ada_begin(nxt, 7)
            self.ps_avail = list(range(7))
        for gi, (hc0, ng) in enumerate(groups):
            if nxt is not None:
                for jg in (2 * gi, 2 * gi + 1):
                    self.ada_piece(nxt, jg, aload)
            wg, wgk = self.wload(self.d_ffn_w_in[l][:, hc0 * 128:(hc0 + ng) * 128], 8, ng * 128)
            wu, wuk = self.wload(self.d_ffn_w_in[l][:, FFH + hc0 * 128:FFH + (hc0 + ng) * 128], 8, ng * 128)
            wo, wok = self.wload(self.d_ffn_w_out[l][hc0 * 128:(hc0 + ng) * 128, :], ng, 1024)
            for j in range(ng):
                for b, (t0, n, v) in enumerate(TB):
                    if self.skip_ctx and b == 0:
                        continue
                    pg, pgk = self.ps()
                    pu, puk = self.ps()
                    for kc in range(8):
                        self.mm(pg[:, :n], wg[:, kc, j * 128:(j + 1) * 128], self.A[:, kc, t0:t0 + n], kc == 0, kc == 7,
                                [wgk, f"A{kc}.{b}"], [pgk])
                    for kc in range(8):
                        self.mm(pu[:, :n], wu[:, kc, j * 128:(j + 1) * 128], self.A[:, kc, t0:t0 + n], kc == 0, kc == 7,
                                [wuk, f"A{kc}.{b}"], [puk])
                    s = (j * 5 + b) % 2
                    self.act(sg[:, s, :n], pg[:, :n], AF.Silu, [pgk], [f"fsg{s}"])
                    self.tt(hact[:, j, t0:t0 + n], sg[:, s, :n], pu[:, :n], ALU.mult, [f"fsg{s}", puk], [f"hact{j}.{b}"])
            for o in range(8):
                for b, (t0, n, v) in enumerate(TB):
                    if self.skip_ctx and b == 0:
                        continue
                    pst, pk = self.ps()
                    for j in range(ng):
                        self.mm(pst[:, :n], wo[:, j, o * 128:(o + 1) * 128], hact[:, j, t0:t0 + n], j == 0, j == ng - 1,
                                [wok, f"hact{j}.{b}"], [pk])
                    self.stt(self.H[:, o, t0:t0 + n], pst[:, :n], self.mod[:, 40 + o, v:v + 1], self.H[:, o, t0:t0 + n],
                             ALU.mult, ALU.add, [pk, self.kmod, f"H{o}.{b}"], [f"H{o}.{b}"])
        if nxt is not None:
            self.ada_end(nxt)
            self.ps_avail = list(range(8))

    def layer(self, l):
        self.skip_ctx = (l == DEPTH - 1)
        p = l % 2
        self.mod, self.gs, self.pvt = self.mod2[p], self.gs2[p], self.pvt2[p]
        self.kmod, self.kgs = f"mod{p}", f"gs{p}"
        if l == self.layers[0] or not PREFETCH_ADA:
            self.ada(l)
        with self.scope() as es:
            self.norm_mod(0, 0, es)
        with self.scope() as es:
            if l % 2 == 0:
                self.even_mixer(l // 2, es)
            else:
                self.odd_mixer(l // 2, l, es)
        with self.scope() as es:
            self.norm_mod(1, 24, es, skip0=self.skip_ctx)
        with self.scope() as es:
            nxt = self.layers[self.layers.index(l) + 1] if self.layers.index(l) + 1 < len(self.layers) else None
            self.ffn(l, es, nxt if PREFETCH_ADA else None)

    def frac_sin(self, out, q, add, tf, ti, R, W, tag):
        self.ts(tf, q, float(add), None, ALU.add, None, R, [tag + "f"])
        self.cp(ti, tf, [tag + "f"], [tag + "i"])
        self.tt(tf, tf, ti, ALU.subtract, [tag + "f", tag + "i"], [tag + "f"])
        self.act(out, tf, AF.Sin, [tag + "f"], W, scale=TWO_PI)

    def proj_chunk(self, w, wk, col0, ncols, evac):
        for b, (t0, n, v) in enumerate(TB):
            pst, pk = self.ps()
            for kc in range(8):
                self.mm(pst[:ncols, :n], w[:, kc, col0:col0 + ncols], self.A[:, kc, t0:t0 + n], kc == 0, kc == 7,
                        [wk, f"A{kc}.{b}"], [pk])
            evac(pst, pk, b, t0, n, v)

    def even_mixer(self, e, es):
        l = 2 * e
        prm = self.sb(es, f"eprm{e}", [128, PE_CR])
        self.dma(prm[:], self.d_pe[e][:, 0:PE_CR], (), ["eprm"])
        with self.scope() as es2:
            self.Ma = self.sb(es2, f"Ma{self.uid}", [128, 4, NT], BF16)
            self.uid += 1
            if SKIP == "s5":
                self.memset(self.Ma[:], 0.0, [f"M{c}.{b}" for c in range(4) for b in range(5)], eng="pool")
            else:
                self.s5(e, prm, es2)
            self.dump(f"Ma{e}", self.Ma[:], [128, 4, NT], BF16, [f"M{c}.{b}" for c in range(4) for b in range(5)])
            self.out_proj(l, [0, 1, 2, 3], lambda kc: self.Ma[:, kc, :])
        with self.scope() as es2:
            self.Mb = self.sb(es2, f"Mb{self.uid}", [128, 4, NT], BF16)
            self.uid += 1
            if SKIP == "hgrn":
                self.memset(self.Mb[:], 0.0, [f"M{c}.{b}" for c in range(4, 8) for b in range(5)], eng="pool")
            else:
                self.hgrn(e, prm, es2)
            self.dump(f"Mb{e}", self.Mb[:], [128, 4, NT], BF16, [f"M{c}.{b}" for c in range(4, 8) for b in range(5)])
            self.out_proj(l, [4, 5, 6, 7], lambda kc: self.Mb[:, kc - 4, :])

    def mchunk(self, c):
        return self.Ma[:, c, :] if c < 4 else self.Mb[:, c - 4, :]

    def s5(self, e, prm, es_outer):
        nc = self.nc
        es = es_outer.enter_context(self.scope())
        S = lambda name, shape, dt=F32: self.sb(es, f"s5{name}{e}", shape, dt)
        P64 = {k: S(k, [128, 64]) for k in ("r", "q1", "c256", "s256", "c512", "s512")}
        L1f = S("L1f", [128, 64, 16], BF16)
        L2f = S("L2f", [128, 64, 16], BF16)
        esc = es.enter_context(self.scope())
        for k in ("cr", "ci"):
            P64[k] = self.sb(esc, f"s5{k}{e}", [128, 64])
        esp = es.enter_context(self.scope())
        for k in ("lr", "step", "cs", "sn", "ar", "ai", "den", "t0", "t1", "tf"):
            P64[k] = self.sb(esp, f"s5{k}{e}", [128, 64])
        ti64 = self.sb(esp, f"s5ti64{e}", [128, 64], I32)
        lre, lim, lst = prm[:, PE_LRE:PE_LRE + 64], prm[:, PE_LIM:PE_LIM + 64], prm[:, PE_LST:PE_LST + 64]
        p = lambda k: P64[k][:]
        self.ts(p("lr"), lre, -1e-4, None, ALU.min, None, ["eprm"], ["s5lr"])
        self.act(p("step"), lst, AF.Exp, ["eprm"], ["s5step"])
        self.tt(p("t0"), p("lr"), p("step"), ALU.mult, ["s5lr", "s5step"], ["s5t0"])
        self.act(p("r"), p("t0"), AF.Exp, ["s5t0"], ["s5r"])
        self.tt(p("q1"), lim, p("step"), ALU.mult, ["eprm", "s5step"], ["s5q1"])
        self.ts(p("q1"), p("q1"), 1.0 / TWO_PI, None, ALU.mult, None, ["s5q1"], ["s5q1"])
        self.frac_sin(p("cs"), p("q1"), 0.25, p("tf"), ti64[:], ["s5q1"], ["s5cs"], "s5x")
        self.frac_sin(p("sn"), p("q1"), 0.0, p("tf"), ti64[:], ["s5q1"], ["s5sn"], "s5x")
        for T, ck, sk in ((256.0, "c256", "s256"), (512.0, "c512", "s512")):
            self.ts(p("t1"), p("q1"), T, None, ALU.mult, None, ["s5q1"], ["s5t1"])
            self.frac_sin(p(ck), p("t1"), 0.25, p("tf"), ti64[:], ["s5t1"], ["s5" + ck], "s5x")
            self.frac_sin(p(sk), p("t1"), 0.0, p("tf"), ti64[:], ["s5t1"], ["s5" + sk], "s5x")
            self.ts(p(sk), p(sk), self.cf[:, CF_SGN:CF_SGN + 1], None, ALU.mult, None, ["s5" + sk, "cf"], ["s5" + sk])
        self.tt(p("ar"), p("r"), p("cs"), ALU.mult, ["s5r", "s5cs"], ["s5ar"])
        self.tt(p("ai"), p("r"), p("sn"), ALU.mult, ["s5r", "s5sn"], ["s5ai"])
        self.ts(p("ar"), p("ar"), -1.0, None, ALU.add, None, ["s5ar"], ["s5ar"])
        self.tt(p("den"), p("lr"), p("lr"), ALU.mult, ["s5lr"], ["s5den"])
        self.tt(p("t0"), lim, lim, ALU.mult, ["eprm", "s5r"], ["s5t0"])
        self.tt(p("den"), p("den"), p("t0"), ALU.add, ["s5den", "s5t0"], ["s5den"])
        self.kb.op("dve", lambda e_: e_.reciprocal(out=p("den"), in_=p("den")), ["s5den"], ["s5den"])
        self.tt(p("t0"), p("ar"), p("lr"), ALU.mult, ["s5ar", "s5lr"], ["s5t0"])
        self.tt(p("t1"), p("ai"), lim, ALU.mult, ["s5ai", "eprm"], ["s5t1"])
        self.tt(p("t0"), p("t0"), p("t1"), ALU.add, ["s5t0", "s5t1"], ["s5t0"])
        self.tt(p("cr"), p("t0"), p("den"), ALU.mult, ["s5t0", "s5den"], ["s5cr"])
        self.tt(p("t0"), p("ai"), p("lr"), ALU.mult, ["s5ai", "s5lr", "s5cr"], ["s5t0"])
        self.tt(p("t1"), p("ar"), lim, ALU.mult, ["s5ar", "eprm"], ["s5t1"])
        self.tt(p("t0"), p("t0"), p("t1"), ALU.subtract, ["s5t0", "s5t1"], ["s5t0"])
        self.tt(p("ci"), p("t0"), p("den"), ALU.mult, ["s5t0", "s5den"], ["s5ci"])
        esp.close()
        with self.scope() as es3:
            CR = self.sb(es3, f"s5CR{e}", [128, 64, 16])
            CI = self.sb(es3, f"s5CI{e}", [128, 64, 16])
            Cr_ = self.sb(es3, f"s5Cr_{e}", [128, 64, 16])
            Ci_ = self.sb(es3, f"s5Ci_{e}", [128, 64, 16])
            tq = self.sb(es3, f"s5tq{e}", [128, 64, 16])
            self.dma(CR[:].rearrange("p a b -> p (a b)"), self.d_pe[e][:, PE_CR:PE_CR + 1024], (), ["s5CR"])
            self.dma(CI[:].rearrange("p a b -> p (a b)"), self.d_pe[e][:, PE_CI:PE_CI + 1024], (), ["s5CI"])
            crb = p("cr").unsqueeze(2).to_broadcast([128, 64, 16])
            cib = p("ci").unsqueeze(2).to_broadcast([128, 64, 16])
            self.tt(Cr_[:], CR[:], crb, ALU.mult, ["s5CR", "s5cr"], ["s5Cr_"])
            self.tt(tq[:], CI[:], cib, ALU.mult, ["s5CI", "s5ci"], ["s5tq"])
            self.tt(Cr_[:], Cr_[:], tq[:], ALU.subtract, ["s5Cr_", "s5tq"], ["s5Cr_"])
            self.tt(Ci_[:], CR[:], cib, ALU.mult, ["s5CR", "s5ci"], ["s5Ci_"])
            self.tt(tq[:], CI[:], crb, ALU.mult, ["s5CI", "s5cr", "s5Cr_"], ["s5tq"])
            self.tt(Ci_[:], Ci_[:], tq[:], ALU.add, ["s5Ci_", "s5tq"], ["s5Ci_"])
            mlo, mhi = self.cf[:, CF_MLO:CF_MLO + 1], self.cf[:, CF_MHI:CF_MHI + 1]
            self.ts(tq[:], Ci_[:], mhi, None, ALU.mult, None, ["s5Ci_", "cf", "s5Ci_"], ["s5tq"])
            self.stt(L1f[:], Cr_[:], mlo, tq[:], ALU.mult, ALU.subtract, ["s5Cr_", "s5tq", "cf"], ["s5L1f"])
            self.ts(tq[:], Cr_[:], mhi, -1.0, ALU.mult, ALU.mult, ["s5Cr_", "cf", "s5L1f"], ["s5tq"])
            self.ts(Ci_[:], Ci_[:], mlo, None, ALU.mult, None, ["s5Ci_", "cf"], ["s5Ci_"])
            self.tt(L2f[:], tq[:], Ci_[:], ALU.subtract, ["s5tq", "s5Ci_"], ["s5L2f"])
        esc.close()

        u = S("u", [128, NT], BF16)
        y = S("y", [128, NT])
        BW1 = S("BW1", [128, 8, 128], BF16)
        BW2 = S("BW2", [128, 8, 128], BF16)
        LR1 = S("LR1", [128, 8, 128], BF16)
        LR2 = S("LR2", [128, 8, 128], BF16)
        Ec = [S(f"Ec{i}", [128, 512]) for i in range(2)]
        Es = [S(f"Es{i}", [128, 512]) for i in range(2)]
        iot = S("iot", [128, 512])
        self.dma(iot[:], self.d_c2[:, C2_IOTA:C2_IOTA + 512], (), ["s5iot"])
        tib = S("tib", [128, 512], I32)
        R512 = [S(f"R512{i}", [128, 128]) for i in range(2)]
        R256 = [S(f"R256{i}", [128, 128]) for i in range(2)]
        Ta = [S(f"Ta{i}", [128, 512]) for i in range(2)]
        Tb = [S(f"Tb{i}", [128, 512]) for i in range(2)]
        cg = [S(f"cg{i}", [128, 512], BF16) for i in range(2)]
        sg = [S(f"sg{i}", [128, 512], BF16) for i in range(2)]
        carry = S("carry", [128, 2])
        qtr = S("qtr", [128, 1])
        self.memset(qtr[:], 0.25, ["s5qtr"])
        iota = iot[:]
        Jm = self.cf[:, CF_J:CF_J + 128]

        def diag(t):
            a = t[:]
            return bass.AP(a.tensor, a.offset, [list(a.ap[0]), [144, 8], [1, 16]])

        order = {0: [0, 1, 2, 3, 4], 1: [0, 4, 3, 2, 1]}
        step = 0
        for c4 in range(LIM.get('c4', 4)):
            w, wk = self.wload(self.d_ab_w_in[e][:, c4 * 128:(c4 + 1) * 128], 8, 128)

            def evac_u(pst, pk, b, t0, n, v, c4=c4):
                self.act(u[:, t0:t0 + n], pst[:, :n], AF.Copy, [pk], [f"s5u.{b}"])
                self.act(y[:, t0:t0 + n], pst[:, :n], AF.Copy, [pk, "eprm"], [f"s5y.{b}"],
                         scale=prm[:, PE_SD + c4:PE_SD + c4 + 1])
            self.proj_chunk(w, wk, 0, 128, evac_u)
            for d in range(LIM.get('d', 2)):
                bwk = [f"s5BW1.{j}" for j in range(8)]
                self.memset(BW1[:], 0.0, bwk, eng="pool")
                for j in range(8):
                    self.dma(BW1[16 * j:16 * j + 16, j, :], self.d_bt[e, d, c4 * 8 + j], (), [bwk[j]], q="pool")
                self.cp(BW2[:, :, 0:64], BW1[:, :, 64:128], bwk, ["s5BW2"], eng="pool")
                self.ts(BW2[:, :, 64:128], BW1[:, :, 0:64], -1.0, None, ALU.mult, None, bwk, ["s5BW2"], eng="pool")
                base = (d * 32 + c4 * 8) * 16
                for LR, Lf, key in ((LR1, L1f, "s5L1f"), (LR2, L2f, "s5L2f")):
                    self.memset(LR[:], 0.0, ["s5" + ("LR1" if LR is LR1 else "LR2")], eng="pool")
                    src = Lf[:].rearrange("p a b -> p (a b)")[:, base:base + 128].rearrange("p (j c) -> p j c", c=16)
                    self.cp(diag(LR), src, [key], ["s5" + ("LR1" if LR is LR1 else "LR2")], eng="pool")
                self.ps_avail = [0, 1]
                blocks = order[d][:LIM.get('b', 5)]
                for jp in range(0, LIM.get('j', 8), 2):
                    chains = []
                    for jj in range(2):
                        j = jp + jj
                        col = d * 32 + c4 * 8 + j
                        cs_ = lambda k, col=col: P64[k][:, col:col + 1]
                        for tbl, add, tk in ((Es[jj], 0.0, f"s5Es{jj}"), (Ec[jj], 0.25, f"s5Ec{jj}")):
                            if add == 0.0:
                                self.act(tbl[:], iota, AF.Copy, ["s5iot", "s5q1"], [tk], scale=cs_("q1"))
                            else:
                                self.act(tbl[:], iota, AF.Identity, ["s5iot", "s5q1", "s5qtr"], [tk], scale=cs_("q1"), bias=qtr[:])
                            self.cp(tib[:], tbl[:], [tk], ["s5yi"])
                            self.tt(tbl[:], tbl[:], tib[:], ALU.subtract, [tk, "s5yi"], [tk])
                            self.act(tbl[:], tbl[:], AF.Sin, [tk], [tk], scale=TWO_PI)
                        for Rm, rk, ck, sk in ((R512[jj], f"s5R512{jj}", "c512", "s512"), (R256[jj], f"s5R256{jj}", "c256", "s256")):
                            self.act(Rm[:], self.ident, AF.Copy, ["cf", "s5" + ck], [rk], scale=cs_(ck))
                            self.stt(Rm[:], Jm, cs_(sk), Rm[:], ALU.mult, ALU.add, ["cf", "s5" + sk, rk], [rk])
                        chains.append((jj, j, cs_))

                    def mmM(jj, j, bi):
                        t0, n, v = TB[blocks[bi]]
                        ub = u[:, t0:t0 + n]
                        if d == 1:
                            ub = rev(ub, n)
                        M1, k1, M2, k2 = self.psum[2 + 2 * jj], f"ps{2 + 2 * jj}", self.psum[3 + 2 * jj], f"ps{3 + 2 * jj}"
                        self.mm(M1[:, :n], BW1[:, j, :], ub, True, True, [f"s5BW1.{j}", f"s5u.{blocks[bi]}"], [k1])
                        self.mm(M2[:, :n], BW2[:, j, :], ub, True, True, ["s5BW2", f"s5u.{blocks[bi]}"], [k2])

                    for jj, j, cs_ in chains:
                        mmM(jj, j, 0)
                    pend = None
                    for bi, b in enumerate(blocks):
                        t0, n, v = TB[b]
                        for jj, j, cs_ in chains:
                            M1, k1, M2, k2 = self.psum[2 + 2 * jj], f"ps{2 + 2 * jj}", self.psum[3 + 2 * jj], f"ps{3 + 2 * jj}"
                            self.tt(Ta[jj][:, :n], M1[:, :n], Ec[jj][:, :n], ALU.mult, [k1, f"s5Ec{jj}"], [f"s5Ta{jj}"])
                            self.tt(Tb[jj][:, :n], M2[:, :n], Es[jj][:, :n], ALU.mult, [k2, f"s5Es{jj}"], [f"s5Tb{jj}"])
                            if bi < len(blocks) - 1:
                                mmM(jj, j, bi + 1)
                            self.tt(Ta[jj][:, :n], Ta[jj][:, :n], Tb[jj][:, :n], ALU.add, [f"s5Ta{jj}", f"s5Tb{jj}"], [f"s5Ta{jj}"])
                            init = 0.0 if bi == 0 else carry[:, jj:jj + 1]
                            self.scan(Tb[jj][:, :n], cs_("r").to_broadcast([128, n]), Ta[jj][:, :n], init,
                                      [f"s5Ta{jj}", "s5r", f"s5carry{jj}"], [f"s5Tb{jj}"])
                            self.tt(cg[jj][:, :n], Ec[jj][:, :n], Tb[jj][:, :n], ALU.mult, [f"s5Ec{jj}", f"s5Tb{jj}"], [f"s5cg{jj}"], eng="pool")
                            self.tt(sg[jj][:, :n], Es[jj][:, :n], Tb[jj][:, :n], ALU.mult, [f"s5Es{jj}", f"s5Tb{jj}"], [f"s5sg{jj}"], eng="pool")
                            if bi < len(blocks) - 1:
                                Rm, rk = (R256[jj], f"s5R256{jj}") if n == 256 else (R512[jj], f"s5R512{jj}")
                                pc, pck = self.psum[6 + jj], f"ps{6 + jj}"
                                self.mm(pc[:, 0:1], Rm[:], Tb[jj][:, n - 1:n], True, True, [rk, f"s5Tb{jj}"], [pck])
                                self.cp(carry[:, jj:jj + 1], pc[:, 0:1], [pck], [f"s5carry{jj}"], eng="act")
                        if pend is not None:
                            pend()
                        yb, ybk = self.ps()
                        for jj, j, cs_ in chains:
                            self.mm(yb[:, :n], LR1[:, j, :], cg[jj][:, :n], jj == 0, False, ["s5LR1", f"s5cg{jj}"], [ybk])
                            self.mm(yb[:, :n], LR2[:, j, :], sg[jj][:, :n], False, jj == 1, ["s5LR2", f"s5sg{jj}"], [ybk])

                        def pend(yb=yb, ybk=ybk, t0=t0, n=n, b=b):
                            src = yb[:, :n]
                            if d == 1:
                                src = rev(src, n)
                            self.tt(y[:, t0:t0 + n], y[:, t0:t0 + n], src, ALU.add, [f"s5y.{b}", ybk], [f"s5y.{b}"])
                    pend()
                self.ps_avail = list(range(8))
            for b, (t0, n, v) in enumerate(TB):
                yb_ = y[:, t0:t0 + n]
                self.tt(Ta[0][:, :n], yb_, yb_, ALU.mult, [f"s5y.{b}"], ["s5Ta0"])
                self.ts(Ta[0][:, :n], Ta[0][:, :n], 0.044715, 1.0, ALU.mult, ALU.add, ["s5Ta0"], ["s5Ta0"])
                self.tt(Ta[0][:, :n], Ta[0][:, :n], yb_, ALU.mult, ["s5Ta0", f"s5y.{b}"], ["s5Ta0"])
                self.act(Tb[0][:, :n], Ta[0][:, :n], AF.Sigmoid, ["s5Ta0"], ["s5Tb0"], scale=1.5957691216057308)
                self.tt(self.Ma[:, c4, t0:t0 + n], yb_, Tb[0][:, :n], ALU.mult, [f"s5y.{b}", "s5Tb0"], [f"M{c4}.{b}"])
        self.dump(f"s5u{e}", u[:], [128, NT], BF16, [f"s5u.{b}" for b in range(5)])
        self.dump(f"s5y{e}", y[:], [128, NT], F32, [f"s5y.{b}" for b in range(5)])
        self.dump(f"s5prm{e}", prm[:], [128, PE_CR], F32, ["eprm"])
        self.dump(f"s5L1f{e}", L1f[:], [128, 64, 16], BF16, ["s5L1f"])
        self.dump(f"s5Tb{e}", Tb[0][:], [128, 512], F32, ["s5Tb0"])
        self.dump(f"s5r{e}", P64["r"][:], [128, 64], F32, ["s5r"])
        self.dump(f"s5BW1{e}", BW1[:], [128, 8, 128], BF16, [f"s5BW1.{j}" for j in range(8)])
        es.close()
        wg, wgk = self.wload(self.d_glu_w[e], 4, 512)
        SG = self.sb(es_outer, f"s5SG{e}", [128, 4, 512], BF16)
        for b, (t0, n, v) in enumerate(TB):
            for c in range(4):
                pst, pk = self.ps()
                for kc in range(4):
                    self.mm(pst[:, :n], wg[:, kc, c * 128:(c + 1) * 128], self.Ma[:, kc, t0:t0 + n], kc == 0, kc == 3,
                            [wgk, f"M{kc}.{b}"], [pk])
                self.act(SG[:, c, :n], pst[:, :n], AF.Sigmoid, [pk, "eprm"], [f"s5SG{c}"], bias=prm[:, PE_GB + c:PE_GB + c + 1])
            for c in range(4):
                self.tt(self.Ma[:, c, t0:t0 + n], self.Ma[:, c, t0:t0 + n], SG[:, c, :n], ALU.mult,
                        [f"M{c}.{b}", f"s5SG{c}"], [f"M{c}.{b}"])

    def hgrn(self, e, prm, es):
        S = lambda name, shape, dt=F32: self.sb(es, f"hg{name}{e}", shape, dt)
        E0, E1, Ss, LB, OML, NOML = (S(k, [128, 4]) for k in ("E0", "E1", "Ss", "LB", "OML", "NOML"))
        self.act(E0[:], prm[:, PE_LBL:PE_LBL + 4], AF.Exp, ["eprm"], ["hgE0"])
        self.act(E1[:], prm[:, PE_LBL + 4:PE_LBL + 8], AF.Exp, ["eprm"], ["hgE1"])
        self.tt(Ss[:], E0[:], E1[:], ALU.add, ["hgE0", "hgE1"], ["hgSs"])
        self.kb.op("dve", lambda e_: e_.reciprocal(out=Ss[:], in_=Ss[:]), ["hgSs"], ["hgSs"])
        self.tt(E0[:], E0[:], Ss[:], ALU.mult, ["hgE0", "hgSs"], ["hgE0"])
        self.tt(E1[:], E1[:], Ss[:], ALU.mult, ["hgE1", "hgSs"], ["hgE1"])
        if e == 0:
            self.tt(LB[:], E0[:], E0[:], ALU.subtract, ["hgE0"], ["hgLB"])
        else:
            self.tt(LB[:], E0[:], E1[:], ALU.add, ["hgE0", "hgE1"], ["hgLB"])
            self.tt(LB[:], LB[:], E0[:], ALU.subtract, ["hgLB", "hgE0"], ["hgLB"])
        self.ts(OML[:], LB[:], -1.0, 1.0, ALU.mult, ALU.add, ["hgLB"], ["hgOML"])
        self.ts(NOML[:], LB[:], -1.0, None, ALU.add, None, ["hgLB"], ["hgNOML"])

        qs = S("qs", [128, NT], BF16)
        cmk = S("cmk", [128, 1024], BF16)
        self.dma(cmk[:], self.d_c2[:, C2_CMASK:C2_CMASK + 1024], (), ["hgcmk"], q="pool")
        CH = 32
        vtm = S("vtm", [CH, NT // CH, 128], BF16)
        O = S("O", [128, NT])
        T = {i: S(f"T{i}", [128, 512]) for i in (1, 2, 3, 5)}
        EB = S("EB", [128, 512])
        KK = S("KK", [128, 512])
        KH = T[3]
        QA = S("QA", [128, 512])
        QR = S("QR", [128, 512], BF16)
        KT = S("KT", [128, 512], BF16)
        sqb = QR
        PM = [S(f"PM{i}", [CH, CH], BF16) for i in range(3)]
        KHt = [S(f"KHt{i}", [CH, 128], BF16) for i in range(3)]
        St = [S(f"St{i}", [128, 128]) for i in range(2)]
        cm_f = cmk[:, 0:512]
        cm_b = cmk[:, 512:1024]
        mask = {0: self.cf[0:CH, CF_MF:CF_MF + CH], 1: self.cf[0:CH, CF_MB:CF_MB + CH]}
        order = {0: [0, 1, 2, 3, 4], 1: [0, 4, 3, 2, 1]}
        wcols = [512, 1024, 1536, 2048, 2560]
        for h in range(4):
            wl = lambda wi: self.wload(self.d_ab_w_in[e][:, wcols[wi] + 128 * h:wcols[wi] + 128 * h + 128], 8, 128)

            def evac_q(pst, pk, b, t0, n, v):
                self.act(qs[:, t0:t0 + n], pst[:, :n], AF.Silu, [pk], [f"hgqs.{b}"])
            wq, wqk = wl(0)
            self.proj_chunk(wq, wqk, 0, 128, evac_q)

            def evac_v(pst, pk, b, t0, n, v):
                self.cp(T[1][:, :n], pst[:, :n], [pk], ["hgT1"], eng="act")
                for ci in range(n // CH):
                    pt, ptk = self.ps()
                    self.tr(pt[0:CH, 0:128], T[1][:, ci * CH:ci * CH + CH], self.ident, ["hgT1", "cf"], [ptk])
                    self.cp(vtm[:, t0 // CH + ci, :], pt[0:CH, 0:128], [ptk], [f"hgvtm.{b}"])
            wv, wvk = wl(3)
            self.proj_chunk(wv, wvk, 0, 128, evac_v)

            for d in range(2):
                self.memset(St[0][:], 0.0, ["hgSt0"])
                kst = [0]
                lc = CH - 1 if d == 0 else 0
                wf, wfk = wl(1 + d)
                def part1(b):
                    t0, n, v = TB[b]
                    pf, pfk = self.ps()
                    for kc in range(8):
                        self.mm(pf[:, :n], wf[:, kc, :], self.A[:, kc, t0:t0 + n], kc == 0, kc == 7,
                                [wfk, f"A{kc}.{b}"], [pfk])
                    t = lambda i: T[i][:, :n]
                    self.act(t(1), pf[:, :n], AF.Exp, [pfk], ["hgT1"], scale=-1.0)
                    self.ts(t(1), t(1), 1.1420073898156842e26, None, ALU.min, None, ["hgT1"], ["hgT1"])
                    self.act(t(2), t(1), AF.Ln, ["hgT1", "oneT"], ["hgT2"], bias=self.oneT[:])
                    self.act(t(5), t(2), AF.Exp, ["hgT2"], ["hgT5"], scale=-1.0)
                    self.ts(KK[:, :n], t(5), NOML[:, h:h + 1], OML[:, h:h + 1], ALU.mult, ALU.add, ["hgT5", "hgNOML", "hgOML"], ["hgKK"])
                    self.act(t(5), t(1), AF.Ln, ["hgT1", "oneT", "hgLB", "hgKK"], ["hgT5"], bias=self.oneT[:], scale=LB[:, h:h + 1])
                    self.tt(t(5), t(5), t(2), ALU.subtract, ["hgT5", "hgT2"], ["hgT5"])
                    if d == 0:
                        self.scan(t(2), cm_f[:, :n], t(5), 0.0, ["hgT5", "hgcmk"], ["hgT2"])
                    else:
                        self.scan(rev(t(2), n), rev(cm_b[:, :n], n), rev(t(5), n), 0.0, ["hgT5", "hgcmk"], ["hgT2"])

                def part2(b):
                    t0, n, v = TB[b]
                    nch = n // CH
                    t = lambda i: T[i][:, :n]
                    v3 = lambda ap: ap.rearrange("p (c s) -> p c s", s=CH)
                    bc = lambda ap, col: v3(ap)[:, :, col:col + 1].to_broadcast([128, nch, CH])
                    self.act(EB[:, :n], t(2), AF.Exp, ["hgT2"], ["hgEB"])
                    self.tt(QA[:, :n], qs[:, t0:t0 + n], EB[:, :n], ALU.mult, [f"hgqs.{b}", "hgEB"], ["hgQA"])
                    self.tt(v3(t(5)), v3(t(2)), bc(t(2), CH // 2), ALU.subtract, ["hgT2"], ["hgT5"])
                    self.act(t(1), t(5), AF.Exp, ["hgT5"], ["hgT1"])
                    self.tt(QR[:, :n], qs[:, t0:t0 + n], t(1), ALU.mult, [f"hgqs.{b}", "hgT1"], ["hgQR"])
                    self.act(t(1), t(5), AF.Exp, ["hgT5", "hgQR"], ["hgT1"], scale=-1.0)
                    self.tt(KT[:, :n], KK[:, :n], t(1), ALU.mult, ["hgKK", "hgT1"], ["hgKT"])
                    self.tt(v3(t(5)), bc(t(2), lc), v3(t(2)), ALU.subtract, ["hgT2", "hgKT"], ["hgT5"])
                    self.act(t(5), t(5), AF.Exp, ["hgT5"], ["hgT5"])
                    self.tt(KH[:, :n], KK[:, :n], t(5), ALU.mult, ["hgKK", "hgT5", "hgT3"], ["hgT3"])

                blocks_ = order[d]
                part1(blocks_[0])
                for bi_, b in enumerate(blocks_):
                    t0, n, v = TB[b]
                    nch = n // CH
                    nxt_b = blocks_[bi_ + 1] if bi_ + 1 < len(blocks_) else None
                    part2(b)
                    clist = list(range(nch)) if d == 0 else list(range(nch - 1, -1, -1))
                    st1 = {}

                    def stage1a(ci):
                        c0 = ci * CH
                        gch = t0 // CH + ci
                        par = gch % 3
                        pS, pSk = self.ps()
                        self.mm(pS[0:CH, 0:CH], KT[:, c0:c0 + CH], QR[:, c0:c0 + CH], True, True, ["hgKT", "hgQR"], [pSk])
                        self.tt(PM[par][:], pS[0:CH, 0:CH], mask[d], ALU.mult, [pSk, "cf"], [f"hgPM{par}"])
                        pT, pTk = self.ps()
                        self.tr(pT[0:CH, 0:128], KH[:, c0:c0 + CH], self.ident, ["hgT3", "cf"], [pTk])
                        self.cp(KHt[par][:], pT[0:CH, 0:128], [pTk], [f"hgKHt{par}"], eng="act")

                    def stage1b(ci):
                        gch = t0 // CH + ci
                        par = gch % 3
                        pD, pDk = self.ps()
                        self.mm(pD[:, 0:128], KHt[par][:], vtm[:, gch, :], True, True, [f"hgKHt{par}", f"hgvtm.{b}"], [pDk])
                        st1[ci] = (pD, pDk)

                    def stage2(ci):
                        c0 = ci * CH
                        gch = t0 // CH + ci
                        par = gch % 3
                        pD, pDk = st1.pop(ci)
                        sp, sn = kst[0] % 2, (kst[0] + 1) % 2
                        kst[0] += 1
                        pO, pOk = self.ps()
                        self.mm(pO[:, 0:CH], St[sp][:], QA[:, c0:c0 + CH], True, False, [f"hgSt{sp}", "hgQA"], [pOk])
                        self.mm(pO[:, 0:CH], vtm[:, gch, :], PM[par][:], False, True, [f"hgvtm.{b}", f"hgPM{par}"], [pOk])
                        self.stt(St[sn][:], St[sp][:], EB[:, c0 + lc:c0 + lc + 1], pD[:, 0:128], ALU.mult, ALU.add,
                                 [f"hgSt{sp}", "hgEB", pDk], [f"hgSt{sn}"])
                        if d == 0:
                            self.cp(O[:, t0 + c0:t0 + c0 + CH], pO[:, 0:CH], [pOk], [f"hgO.{b}"], eng="act")
                        else:
                            self.tt(O[:, t0 + c0:t0 + c0 + CH], O[:, t0 + c0:t0 + c0 + CH], pO[:, 0:CH], ALU.add,
                                    [pOk, f"hgO.{b}"], [f"hgO.{b}"])

                    stage1a(clist[0])
                    if len(clist) > 1:
                        stage1a(clist[1])
                    stage1b(clist[0])
                    for i_, ci in enumerate(clist):
                        if i_ + 2 < len(clist):
                            stage1a(clist[i_ + 2])
                        if i_ + 1 < len(clist):
                            stage1b(clist[i_ + 1])
                        stage2(ci)
                        if i_ == 3 and nxt_b is not None:
                            part1(nxt_b)
            if h == 3:
                self.dump(f"hgqs{e}", qs[:], [128, NT], BF16, [f"hgqs.{b}" for b in range(5)])
                self.dump(f"hgO{e}", O[:], [128, NT], F32, [f"hgO.{b}" for b in range(5)])
                self.dump(f"hgvtm{e}", vtm[:], [32, 72, 128], BF16, [f"hgvtm.{b}" for b in range(5)])
                self.dump(f"hgEB{e}", EB[:], [128, 512], F32, ["hgEB"])
                self.dump(f"hgKK{e}", KK[:], [128, 512], F32, ["hgKK"])
                self.dump(f"hgT5{e}", T[5][:], [128, 512], F32, ["hgT5"])
                self.dump(f"hgLB{e}", LB[:], [128, 4], F32, ["hgLB"])
            wgt, wgtk = wl(4)
            for b, (t0, n, v) in enumerate(TB):
                self.act(sqb[:, :n], O[:, t0:t0 + n], AF.Square, [f"hgO.{b}"], ["hgQR"])
                pR, pRk = self.ps()
                self.mm(pR[:, :n], self.onesb[:], sqb[:, :n], True, True, ["onesb", "hgQR"], [pRk])
                self.act(T[1][:, :n], pR[:, :n], AF.Ln, [pRk, "epsT"], ["hgT1"], bias=self.epsT[:], scale=1.0 / 128.0)
                self.act(T[1][:, :n], T[1][:, :n], AF.Exp, ["hgT1"], ["hgT1"], scale=-0.5)
                pg, pgk = self.ps()
                for kc in range(8):
                    self.mm(pg[:, :n], wgt[:, kc, :], self.A[:, kc, t0:t0 + n], kc == 0, kc == 7, [wgtk, f"A{kc}.{b}"], [pgk])
                self.act(T[2][:, :n], pg[:, :n], AF.Silu, [pgk], ["hgT2"])
                self.tt(T[1][:, :n], T[1][:, :n], O[:, t0:t0 + n], ALU.mult, ["hgT1", f"hgO.{b}"], ["hgT1"])
                self.tt(T[1][:, :n], T[1][:, :n], T[2][:, :n], ALU.mult, ["hgT1", "hgT2"], ["hgT1"])
                self.act(self.Mb[:, h, t0:t0 + n], T[1][:, :n], AF.Identity, ["hgT1", "eprm"], [f"M{4 + h}.{b}"],
                         scale=prm[:, PE_ON:PE_ON + 1])

    def norm_rope(self, pq, pqk, rows, gm, gmk, inv_dim, gain, t0, n, is_x, dest, destk, tm, tag="", alt=False):
        sq, rs, qn, qb, t1, t2, cosT, sinT = tm
        ksq, krs = f"nr{tag}_sq", f"nr{tag}_rs"
        if alt:
            assert not is_x
            sq, rs, ksq, krs = qb, t1, f"nr{tag}_qb", f"nr{tag}_t1"
        self.act(sq[:rows, :n], pq, AF.Square, [pqk], [ksq])
        pn, pnk = self.ps()
        self.mm(pn[:rows, :n], gm, sq[:rows, :n], True, True, [gmk, ksq], [pnk])
        self.act(rs[:rows, :n], pn[:rows, :n], AF.Ln, [pnk, "epsT"], [krs], bias=self.epsT[:rows, :], scale=inv_dim)
        self.act(rs[:rows, :n], rs[:rows, :n], AF.Exp, [krs], [krs], scale=-0.5)
        if not is_x:
            self.stt(dest, pq, gain, rs[:rows, :n], ALU.mult, ALU.mult, [pqk, krs, "oprm"], destk)
            return
        self.stt(qn[:rows, :n], pq, gain, rs[:rows, :n], ALU.mult, ALU.mult, [pqk, krs, "oprm"], [f"nr{tag}_qn"])
        self.cp(qb[:rows, :n], qn[:rows, :n], [f"nr{tag}_qn"], [f"nr{tag}_qb"], eng="act")
        pr, prk = self.ps()
        self.mm(pr[:rows, :n], self.permb[:rows, :rows], qb[:rows, :n], True, True, ["permb", f"nr{tag}_qb"], [prk])
        x0 = t0 - CTX
        self.dma(cosT[:rows, :n], self.d_cos[0:rows, x0:x0 + n], (), [f"nr{tag}_cos"])
        self.dma(sinT[:rows, :n], self.d_sin[0:rows, x0:x0 + n], (), [f"nr{tag}_sin"])
        self.tt(t1[:rows, :n], qn[:rows, :n], cosT[:rows, :n], ALU.mult, [f"nr{tag}_qn", f"nr{tag}_cos"], [f"nr{tag}_t1"])
        self.tt(t2[:rows, :n], pr[:rows, :n], sinT[:rows, :n], ALU.mult, [prk, f"nr{tag}_sin"], [f"nr{tag}_t2"])
        self.tt(dest, t1[:rows, :n], t2[:rows, :n], ALU.add, [f"nr{tag}_t1", f"nr{tag}_t2"], destk)

    def odd_mixer(self, o, l, es):
        lam_init = 0.8 - 0.6 * math.exp(-0.3 * l)
        prm = self.sb(es, f"oprm{o}", [128, PO_N])
        self.dma(prm[:], self.d_po[o], (), ["oprm"])
        self.permb = self.sb(es, f"permb{o}", [128, 128], BF16)
        self.dma(self.permb[:], self.d_perm, (), ["permb"], q="pool")
        self.bdb = self.sb(es, f"bdb{o}", [128, 128], BF16)
        self.cp(self.bdb[:], self.cf[:, CF_BD:CF_BD + 128], ["cf"], ["bdb"])
        S0 = lambda name, shape, dt=F32: self.sb(es, f"od{name}{o}", shape, dt)
        lp = S0("lp", [128, 2])
        nlam = S0("nlam", [128, 1])
        subg = S0("subg", [128, 1])
        with self.scope() as esl:
            onesf = self.sb(esl, f"onesf{o}", [128, 128])
            self.memset(onesf[:], 1.0, ["onesf"])
            self.memset(lp[:], 0.0, ["odlp"])
            self.tt(lp[0:64, 0:1], prm[0:64, PO_LAM:PO_LAM + 1], prm[0:64, PO_LAM + 1:PO_LAM + 2], ALU.mult, ["oprm", "odlp"], ["odlp"])
            self.tt(lp[0:64, 1:2], prm[0:64, PO_LAM + 2:PO_LAM + 3], prm[0:64, PO_LAM + 3:PO_LAM + 4], ALU.mult, ["oprm", "odlp"], ["odlp"])
            pl, plk = self.ps()
            self.mm(pl[:, 0:2], onesf[:], lp[:], True, True, ["onesf", "odlp"], [plk])
            self.act(lp[:], pl[:, 0:2], AF.Exp, [plk], ["odlp"])
            self.tt(nlam[:], lp[:, 1:2], lp[:, 0:1], ALU.subtract, ["odlp"], ["odnlam"])
            self.ts(nlam[:], nlam[:], -lam_init, None, ALU.add, None, ["odnlam"], ["odnlam"])
            self.ts(subg[:], prm[:, PO_SUB:PO_SUB + 1], 1.0 - lam_init, None, ALU.mult, None, ["oprm"], ["odsubg"])
        tm = (S0("sq", [128, 512], BF16), S0("rs", [128, 512]), S0("qn", [128, 512]), S0("qb", [128, 512], BF16),
              S0("t1", [128, 512]), S0("t2", [128, 512]), S0("cosT", [128, 512]), S0("sinT", [128, 512]))
        Pt = [S0(f"P{i}", [128, 512], BF16) for i in range(4)]
        orec = S0("orec", [128, 512])
        oacc = S0("oacc", [128, 512])

        def attend(qblk, ktiles, score_fn, v_fn, nacc, scale, finish, zacc=None):
            t0, n, v = TB[qblk]
            accs = [(self.psum[2 * i], f"ps{2 * i}", self.psum[2 * i + 1], f"ps{2 * i + 1}") for i in range(nacc)]
            self.ps_avail = list(range(2 * nacc, 8))
            nk = len(ktiles)
            depth = 1 if nacc == 2 else 3
            sc_ = {}

            def do_scores(ki):
                for i in range(nacc):
                    pS, pSk = self.ps()
                    score_fn(i, pS, pSk, ktiles[ki], t0, n)
                    sc_[(ki, i)] = (pS, pSk)

            for ki in range(min(depth, nk)):
                do_scores(ki)
            for ki, kt in enumerate(ktiles):
                if ki + depth < nk:
                    do_scores(ki + depth)
                for i in range(nacc):
                    pS, pSk = sc_.pop((ki, i))
                    P = Pt[(nacc * ki + i) % 4]
                    Pk = f"odP{(nacc * ki + i) % 4}"
                    self.act(P[:, :n], pS[:, :n], AF.Exp, [pSk], [Pk], scale=scale)
                    O_, Ok, Z_, Zk = accs[i]
                    vl, vk = v_fn(kt)
                    self.mm(O_[:, :n], vl, P[:, :n], ki == 0, ki == nk - 1, [vk, Pk], [Ok])
                    if zacc is None or i != 0:
                        self.mm(Z_[:, :n], self.onesb[:], P[:, :n], ki == 0, ki == nk - 1, ["onesb", Pk], [Zk])
                    else:
                        Zf, _ = zacc
                        if ki == 0:
                            self.cp(Zf[:, :n], P[:, :n], [Pk], ["odZf"])
                        else:
                            self.tt(Zf[:, :n], Zf[:, :n], P[:, :n], ALU.add, ["odZf", Pk], ["odZf"])
            if zacc is not None:
                Zf, ones_f = zacc
                O_, Ok, Z_, Zk = accs[0]
                self.mm(Z_[:, :n], ones_f[:], Zf[:, :n], True, True, ["odonesf", "odZf"], [Zk])
            finish(accs, t0, n, qblk)
            self.ps_avail = list(range(8))

        with self.scope() as es2:
            self.Ma = self.sb(es2, f"Ma{self.uid}", [128, 4, NT], BF16)
            self.uid += 1
            S = lambda name, shape, dt=F32: self.sb(es2, f"df{name}{o}", shape, dt)
            QD = S("QD", [128, NT], BF16)
            KD = S("KD", [128, NT], BF16)
            Vt = S("Vt", [128, 18, 128], BF16)
            sqo = S("sqo", [128, 512], BF16)
            tmB = (S("sqB", [128, 512], BF16), S("rsB", [128, 512]), S("qnB", [128, 512]), S("qbB", [128, 512], BF16),
                   S("t1B", [128, 512]), S("t2B", [128, 512]), S("cosB", [128, 512]), S("sinB", [128, 512]))
            tms = [(tm, ""), (tmB, "B")]
            Zf = S("Zf", [128, 512])
            ones_f = S("onesf2", [128, 128])
            self.memset(ones_f[:], 1.0, ["odonesf"])
            zacc = (Zf, ones_f)
            ncall = [0]
            for h in range(4):
                for which, col0, dst, dk, gcol in ((0, 0, QD, "dfQD", PO_QG), (1, 512, KD, "dfKD", PO_KG)):
                    w, wk = self.wload(self.d_cd_w_in[o][:, col0 + 128 * h:col0 + 128 * h + 128], 8, 128)

                    def evac(pst, pk, b, t0, n, v, dst=dst, dk=dk, gcol=gcol):
                        tmx, tagx = tms[ncall[0] % 2]
                        ncall[0] += 1
                        self.norm_rope(pst[:, :n], pk, 128, self.bdb[:], "bdb", 1.0 / 64.0, prm[:, gcol:gcol + 1], t0, n, v == 0,
                                       dst[:, t0:t0 + n], [f"{dk}.{b}"], tmx, tagx)
                    self.proj_chunk(w, wk, 0, 128, evac)
                wv, wvk = self.wload(self.d_cd_w_in[o][:, 1024 + 128 * h:1024 + 128 * h + 128], 8, 128)
                for tt_ in range(18):
                    b = 0 if tt_ < 2 else 1 + (tt_ - 2) // 4
                    pv_, pvk = self.ps()
                    for kc in range(8):
                        self.mm(pv_[:, 0:128], self.A[:, kc, tt_ * 128:(tt_ + 1) * 128], wv[:, kc, :], kc == 0, kc == 7,
                                [wvk, f"A{kc}.{b}"], [pvk])
                    self.cp(Vt[:, tt_, :], pv_[:, 0:128], [pvk], [f"dfVt.{tt_}"], eng="act")

                def score(i, pS, pSk, kt, t0, n):
                    bq = [b for b, tb in enumerate(TB) if tb[0] == t0][0]
                    bk = 0 if kt < 2 else 1 + (kt - 2) // 4
                    self.mm(pS[:, :n], KD[64 * i:64 * i + 64, kt * 128:(kt + 1) * 128], QD[64 * i:64 * i + 64, t0:t0 + n], True, True,
                            [f"dfKD.{bk}", f"dfQD.{bq}"], [pSk])

                def vfn(kt):
                    return Vt[:, kt, :], f"dfVt.{kt}"

                def finish(accs, t0, n, qblk, h=h):
                    (O1, O1k, Z1, Z1k), (O2, O2k, Z2, Z2k) = accs
                    r2 = tm[4]
                    self.act(orec[:, :n], Z1[:, :n], AF.Ln, [Z1k], ["odorec"])
                    self.act(orec[:, :n], orec[:, :n], AF.Exp, ["odorec"], ["odorec"], scale=-1.0)
                    self.act(r2[:, :n], Z2[:, :n], AF.Ln, [Z2k], ["nr_t1"])
                    self.act(r2[:, :n], r2[:, :n], AF.Exp, ["nr_t1"], ["nr_t1"], scale=-1.0)
                    self.tt(oacc[:, :n], O1[:, :n], orec[:, :n], ALU.mult, [O1k, "odorec"], ["odoacc"])
                    self.tt(orec[:, :n], O2[:, :n], r2[:, :n], ALU.mult, [O2k, "nr_t1", "odoacc"], ["odorec"])
                    self.stt(oacc[:, :n], orec[:, :n], nlam[:, 0:1], oacc[:, :n], ALU.mult, ALU.add, ["odorec", "odnlam", "odoacc"], ["odoacc"])
                    self.act(sqo[:, :n], oacc[:, :n], AF.Square, ["odoacc"], ["dfsqo"])
                    pn, pnk = self.ps()
                    self.mm(pn[:, :n], self.onesb[:], sqo[:, :n], True, True, ["onesb", "dfsqo"], [pnk])
                    self.act(orec[:, :n], pn[:, :n], AF.Ln, [pnk, "epsT"], ["odorec"], bias=self.epsT[:], scale=1.0 / 128.0)
                    self.act(orec[:, :n], orec[:, :n], AF.Exp, ["odorec"], ["odorec"], scale=-0.5)
                    self.stt(self.Ma[:, h, t0:t0 + n], oacc[:, :n], subg[:, 0:1], orec[:, :n], ALU.mult, ALU.mult,
                             ["odoacc", "odsubg", "odorec"], [f"M{h}.{qblk}"])
                if not self.skip_ctx:
                    attend(0, [0, 1], score, vfn, 2, 0.125, finish, zacc)
                for qblk in range(1, 5):
                    attend(qblk, list(range(18)), score, vfn, 2, 0.125, finish, zacc)
            self.dump(f"Mo{o}a", self.Ma[:], [128, 4, NT], BF16, [f"M{c}.{b}" for c in range(4) for b in range(5)])
            self.out_proj(l, [0, 1, 2, 3], lambda kc: self.Ma[:, kc, :])

        with self.scope() as es2:
            S = lambda name, shape, dt=F32: self.sb(es2, f"ml{name}{o}", shape, dt)
            CQn = S("CQn", [128, 3, NT], BF16)
            CKVn = S("CKVn", [128, 2, NT], BF16)
            KR = S("KR", [64, NT], BF16)
            Mh = S("Mh", [128, NT], BF16)
            VMh = S("VMh", [128, 18, 128], BF16)
            QN = S("QN", [128, NT], BF16)
            QR = S("QR", [64, NT], BF16)
            KN = S("KN", [128, NT], BF16)
            rawt = [tm[2], tm[4], tm[5]]
            rawk = ["nr_qn", "nr_t1", "nr_t2"]
            sq3, rs3 = tm[0], tm[1]
            for (col0, nch, dst, dk, gofs) in ((1536, 3, CQn, "mlCQn", PO_QA), (1920, 2, CKVn, "mlCKVn", PO_KVA)):
                w, wk = self.wload(self.d_cd_w_in[o][:, col0:col0 + nch * 128], 8, nch * 128)
                for b, (t0, n, v) in enumerate(TB):
                    pn, pnk = self.ps()
                    for c in range(nch):
                        pst, pk = self.ps()
                        for kc in range(8):
                            self.mm(pst[:, :n], w[:, kc, c * 128:(c + 1) * 128], self.A[:, kc, t0:t0 + n], kc == 0, kc == 7,
                                    [wk, f"A{kc}.{b}"], [pk])
                        self.cp(rawt[c][:, :n], pst[:, :n], [pk], [rawk[c]], eng="act")
                        self.act(sq3[:, :n], pst[:, :n], AF.Square, [pk], ["nr_sq"])
                        self.mm(pn[:, :n], self.onesb[:], sq3[:, :n], c == 0, c == nch - 1, ["onesb", "nr_sq"], [pnk])
                    self.act(rs3[:, :n], pn[:, :n], AF.Ln, [pnk, "epsT"], ["nr_rs"], bias=self.epsT[:], scale=1.0 / (nch * 128.0))
                    self.act(rs3[:, :n], rs3[:, :n], AF.Exp, ["nr_rs"], ["nr_rs"], scale=-0.5)
                    for c in range(nch):
                        self.stt(dst[:, c, t0:t0 + n], rawt[c][:, :n], prm[:, gofs + c:gofs + c + 1], rs3[:, :n], ALU.mult, ALU.mult,
                                 [rawk[c], "oprm", "nr_rs"], [f"{dk}{c}.{b}"])
            w, wk = self.wload(self.d_cd_w_in[o][:, 2176:2240], 8, 64)

            def evac_kr(pst, pk, b, t0, n, v):
                self.norm_rope(pst[0:64, :n], pk, 64, self.onesb[0:64, 0:64], "onesb", 1.0 / 64.0, prm[0:64, PO_RK:PO_RK + 1], t0, n,
                               v == 0, KR[:, t0:t0 + n], [f"mlKR.{b}"], tm)
            self.proj_chunk(w, wk, 0, 64, evac_kr)
            mscale = 192.0 ** -0.5
            for h in range(4):
                wuq, wuqk = self.wload(self.d_w_uq[o], 3, 768)
                wukv, wukvk = self.wload(self.d_w_ukv[o], 2, 1024)
                for b, (t0, n, v) in enumerate(TB):
                    pq, pqk = self.ps()
                    for kc in range(3):
                        self.mm(pq[:, :n], wuq[:, kc, h * 192:h * 192 + 128], CQn[:, kc, t0:t0 + n], kc == 0, kc == 2,
                                [wuqk, f"mlCQn{kc}.{b}"], [pqk])
                    self.norm_rope(pq[:, :n], pqk, 128, self.onesb[:], "onesb", 1.0 / 128.0, prm[:, PO_NQ:PO_NQ + 1], t0, n, False,
                                   QN[:, t0:t0 + n], [f"mlQN.{b}"], tm)
                    pk_, pkk_ = self.ps()
                    for kc in range(2):
                        self.mm(pk_[:, :n], wukv[:, kc, h * 256:h * 256 + 128], CKVn[:, kc, t0:t0 + n], kc == 0, kc == 1,
                                [wukvk, f"mlCKVn{kc}.{b}"], [pkk_])
                    self.norm_rope(pk_[:, :n], pkk_, 128, self.onesb[:], "onesb", 1.0 / 128.0, prm[:, PO_NK:PO_NK + 1], t0, n, False,
                                   KN[:, t0:t0 + n], [f"mlKN.{b}"], tm, "", True)
                    pr_, prk_ = self.ps()
                    for kc in range(3):
                        self.mm(pr_[0:64, :n], wuq[:, kc, h * 192 + 128:h * 192 + 192], CQn[:, kc, t0:t0 + n], kc == 0, kc == 2,
                                [wuqk, f"mlCQn{kc}.{b}"], [prk_])
                    self.norm_rope(pr_[0:64, :n], prk_, 64, self.onesb[0:64, 0:64], "onesb", 1.0 / 64.0, prm[0:64, PO_RQ:PO_RQ + 1],
                                   t0, n, v == 0, QR[:, t0:t0 + n], [f"mlQR.{b}"], tm)
                for tt_ in range(18):
                    b = 0 if tt_ < 2 else 1 + (tt_ - 2) // 4
                    pv_, pvk = self.ps()
                    for kc in range(2):
                        self.mm(pv_[:, 0:128], CKVn[:, kc, tt_ * 128:(tt_ + 1) * 128], wukv[:, kc, h * 256 + 128:h * 256 + 256],
                                kc == 0, kc == 1, [wukvk, f"mlCKVn{kc}.{b}"], [pvk])
                    self.cp(VMh[:, tt_, :], pv_[:, 0:128], [pvk], [f"mlVM.{tt_}"], eng="act")

                def score(i, pS, pSk, kt, t0, n):
                    bq = [b for b, tb in enumerate(TB) if tb[0] == t0][0]
                    bk = 0 if kt < 2 else 1 + (kt - 2) // 4
                    self.mm(pS[:, :n], KN[:, kt * 128:(kt + 1) * 128], QN[:, t0:t0 + n], True, False, [f"mlKN.{bk}", f"mlQN.{bq}"], [pSk])
                    self.mm(pS[:, :n], KR[:, kt * 128:(kt + 1) * 128], QR[:, t0:t0 + n], False, True, [f"mlKR.{bk}", f"mlQR.{bq}"], [pSk])

                def vfn(kt):
                    return VMh[:, kt, :], f"mlVM.{kt}"

                def finish(accs, t0, n, qblk, h=h):
                    ((O1, O1k, Z1, Z1k),) = accs
                    self.act(orec[:, :n], Z1[:, :n], AF.Ln, [Z1k], ["odorec"])
                    self.act(orec[:, :n], orec[:, :n], AF.Exp, ["odorec"], ["odorec"], scale=-1.0)
                    self.tt(Mh[:, t0:t0 + n], O1[:, :n], orec[:, :n], ALU.mult, [O1k, "odorec"], [f"mlMh.{qblk}"])
                if not self.skip_ctx:
                    attend(0, [0, 1], score, vfn, 1, mscale, finish)
                for qblk in range(1, 5):
                    attend(qblk, list(range(18)), score, vfn, 1, mscale, finish)
                self.dump(f"Mo{o}b{h}", Mh[:], [128, NT], BF16, [f"mlMh.{b}" for b in range(5)])
                self.out_proj(l, [4 + h], lambda kc: Mh[:, :], keyf=lambda kc, b: f"mlMh.{b}")


def make_in_maps(inp, batches):
    cf, c2 = host_consts()
    cos2, sin2, perm = host_rope()
    pv, pe, bt, po = pack_inputs(inp)
    f = lambda a: np.ascontiguousarray(np.asarray(a, np.float32))
    shared = {
        "cf": cf, "c2": c2, "ropecos": cos2, "ropesin": sin2, "ropeperm": perm, "pv": pv, "pe": pe, "bt": bt, "po": po,
        "ada_w": f(inp["ada_w"]), "w_out": f(inp["w_out"]), "ffn_w_in": f(inp["ffn_w_in"]), "ffn_w_out": f(inp["ffn_w_out"]),
        "ab_w_in": f(inp["ab_w_in"]), "s5_glu_w": f(inp["s5_glu_w"]), "cd_w_in": f(inp["cd_w_in"]),
        "mla_w_uq": f(inp["mla_w_uq"]), "mla_w_ukv": f(inp["mla_w_ukv"]),
    }
    maps = []
    cc = colmaj(f(inp["c_ctx"]), 8)
    for b in batches:
        m = dict(shared)
        m["h0"] = np.ascontiguousarray(np.concatenate([f(inp["ctx"])[b].T, f(inp["x"])[b].T], axis=1))
        m["sc"] = np.ascontiguousarray(np.stack([colmaj(f(inp["c"])[b], 8), cc], axis=-1))
        maps.append(m)
    return maps


def kernel(**inputs):
    nb = 8
    b = Builder(list(range(DEPTH)))
    nc = b.build()
    maps = make_in_maps(inputs, list(range(nb)))
    res = run_bass_kernel_spmd(nc, maps, core_ids=list(range(nb)))
    out = np.stack([np.asarray(res.results[i]["out"], np.float32).T for i in range(nb)], axis=0)
    return np.ascontiguousarray(out)
```

```python
from contextlib import ExitStack
import math
import numpy as np
import concourse.bass as bass
import concourse.mybir as mybir
from concourse.bass_utils import run_bass_kernel_spmd

F32 = mybir.dt.float32
BF16 = mybir.dt.bfloat16
I32 = mybir.dt.int32
ALU = mybir.AluOpType
AF = mybir.ActivationFunctionType

D = 1024
DEPTH = 4
NT = 2304
CTX = 256
SEQ = 2048
FFH = 2816
EPS = 1e-6
TB = [(0, 256, 1), (256, 512, 0), (768, 512, 0), (1280, 512, 0), (1792, 512, 0)]
TWO_PI = 2.0 * math.pi

ENGS = ("pe", "dve", "act", "pool", "sp")
NDSEM = 12
FENCE_DMA = True
FENCE_ON = True
PREFETCH_ADA = True
SKIP = None
LIM = {}


class Op:
    __slots__ = ("eng", "fn", "deps", "sig", "cnt", "dma", "dsem", "dval", "idx")


class KB:
    def __init__(self, nc):
        self.nc = nc
        self.ops = []
        self.last_w = {}
        self.readers = {}
        self.known = {e: {} for e in ENGS}
        self.dma_cnt = {e: 0 for e in ENGS}
        self.dma_last = {}
        self.dma_n = {}
        self.pending = {e: [] for e in ENGS}
        self.last_op = {}

    def fence(self):
        toks = list(self.last_op.values()) + (list(self.dma_last.values()) if FENCE_DMA else [])
        for e in ENGS:
            self.pending[e] = list(toks)

    def _need(self, eng, tok, deps):
        if tok is None:
            return
        src = self.ops[tok]
        if src.dma:
            key = ("d", src.eng, src.dsem)
        else:
            key = src.eng
            if src.eng == "pe" and eng == "pe":
                return
        if self.known[eng].get(key, -1) >= tok:
            return
        self.known[eng][key] = tok
        deps.append(tok)

    def op(self, eng, fn, R=(), W=(), dma=False):
        o = Op()
        o.eng, o.fn, o.dma, o.sig, o.cnt = eng, fn, dma, False, 0
        o.idx = len(self.ops)
        deps = []
        if self.pending[eng]:
            for t in self.pending[eng]:
                self._need(eng, t, deps)
            self.pending[eng] = []
        for r in R:
            self._need(eng, self.last_w.get(r), deps)
            if isinstance(r, str) and r.startswith("ps"):
                for k, t in self.readers.get(r, {}).items():
                    if k != eng:
                        self._need(eng, t, deps)
        for w in W:
            self._need(eng, self.last_w.get(w), deps)
            for t in self.readers.get(w, {}).values():
                self._need(eng, t, deps)
        if dma:
            k = self.dma_cnt[eng] % NDSEM
            self.dma_cnt[eng] += 1
            o.dsem = k
            prev = self.dma_last.get((eng, k))
            if prev is not None:
                self._need(eng, prev, deps)
            self.dma_last[(eng, k)] = o.idx
            self.dma_n[(eng, k)] = self.dma_n.get((eng, k), 0) + 1
            o.dval = 16 * self.dma_n[(eng, k)]
        o.deps = deps
        self.ops.append(o)
        if not dma:
            self.last_op[eng] = o.idx
        for r in R:
            self.readers.setdefault(r, {})[eng if not dma else ("d", o.idx)] = o.idx
        for w in W:
            self.last_w[w] = o.idx
            self.readers[w] = {}
        return o.idx

    def emit(self, final_wait_ops=()):
        nc = self.nc
        for o in self.ops:
            for d in o.deps:
                self.ops[d].sig = True
        cnt = {e: 0 for e in ENGS}
        for o in self.ops:
            if o.dma:
                continue
            if o.sig:
                cnt[o.eng] += 1
            o.cnt = cnt[o.eng]
        engobj = {"pe": nc.tensor, "dve": nc.vector, "act": nc.scalar, "pool": nc.gpsimd, "sp": nc.sync}
        with ExitStack() as es:
            sem = {e: es.enter_context(nc.semaphore("s_" + e)) for e in ENGS}
            dsem = {}
            for e in ("sp", "act", "pool"):
                for k in range(NDSEM):
                    dsem[(e, k)] = es.enter_context(nc.semaphore(f"d_{e}{k}"))
            per = {e: [] for e in ENGS}
            for o in self.ops:
                per[o.eng].append(o)
            fin = [self.ops[i] for i in final_wait_ops]

            def run(e):
                eo = engobj[e]
                for o in per[e]:
                    for d in o.deps:
                        s = self.ops[d]
                        if s.dma:
                            eo.wait_ge(dsem[(s.eng, s.dsem)], s.dval)
                        else:
                            eo.wait_ge(sem[s.eng], s.cnt)
                    ins = o.fn(eo)
                    if o.dma:
                        ins.then_inc(dsem[(o.eng, o.dsem)], 16)
                    elif o.sig:
                        ins.then_inc(sem[o.eng], 1)
                if e == "sp":
                    for s in fin:
                        eo.wait_ge(dsem[(s.eng, s.dsem)], s.dval)

            with nc.Block() as block:
                @block.tensor
                def _(t):
                    run("pe")

                @block.vector
                def _(v):
                    run("dve")

                @block.scalar
                def _(s):
                    run("act")

                @block.gpsimd
                def _(g):
                    run("pool")

                @block.sync
                def _(s):
                    run("sp")


def rev(ap2d, n):
    a = [list(x) for x in ap2d.ap]
    assert len(a) == 2 and a[1][1] == n
    return bass.AP(ap2d.tensor, ap2d.offset + a[1][0] * (n - 1), [a[0], [-a[1][0], n]])


CF_ID, CF_J, CF_MF, CF_MB, CF_MLO, CF_MHI, CF_SGN, CF_BD, CF_N = (0, 128, 256, 320, 384, 385, 386, 387, 515)
C2_IOTA, C2_CMASK, C2_CMASKB, C2_N = 0, 512, 1024, 1536


def host_consts():
    cf = np.zeros((128, CF_N), np.float32)
    cf[:, CF_ID:CF_ID + 128] = np.eye(128, dtype=np.float32)
    J = np.zeros((128, 128), np.float32)
    for p in range(64):
        J[p, p + 64] = 1.0
        J[p + 64, p] = 1.0
    cf[:, CF_J:CF_J + 128] = J
    c2 = np.zeros((128, C2_N), np.float32)
    c2[:, C2_IOTA:C2_IOTA + 512] = np.arange(512, dtype=np.float32)[None, :]
    cm = np.ones(512, np.float32)
    cm[::32] = 0.0
    c2[:, C2_CMASK:C2_CMASK + 512] = cm[None, :]
    cmb = np.ones(512, np.float32)
    cmb[31::32] = 0.0
    c2[:, C2_CMASKB:C2_CMASKB + 512] = cmb[None, :]
    s = np.arange(64)
    cf[:64, CF_MF:CF_MF + 64] = (s[:, None] <= s[None, :]).astype(np.float32)
    cf[:64, CF_MB:CF_MB + 64] = (s[:, None] >= s[None, :]).astype(np.float32)
    cf[:64, CF_MLO] = 1.0
    cf[64:, CF_MHI] = 1.0
    cf[:64, CF_SGN] = 1.0
    cf[64:, CF_SGN] = -1.0
    bd = np.zeros((128, 128), np.float32)
    bd[:64, :64] = 1.0
    bd[64:, 64:] = 1.0
    cf[:, CF_BD:CF_BD + 128] = bd
    return cf, c2


ROPE_DIM = 64


def host_rope():
    n_freq = ROPE_DIM // 4
    inv = np.power(np.float32(10000.0), -np.arange(n_freq, dtype=np.float32) / np.float32(n_freq)).astype(np.float32)
    t = np.arange(SEQ)
    rows = (t // 64).astype(np.float32)
    cols = (t % 64).astype(np.float32)
    ang_r = rows[:, None] * inv[None, :]
    ang_c = cols[:, None] * inv[None, :]
    ang = np.concatenate([ang_r, ang_r, ang_c, ang_c], axis=-1).astype(np.float32)
    cos = np.cos(ang).astype(np.float32).T
    sin = np.sin(ang).astype(np.float32).T
    sgn = np.ones((64, 1), np.float32)
    sgn[0:16] = -1.0
    sgn[32:48] = -1.0
    sins = sin * sgn
    cos2 = np.concatenate([cos, cos], 0)
    sin2 = np.concatenate([sins, sins], 0)
    perm = np.zeros((128, 128), np.float32)
    for base in (0, 64):
        for m in range(64):
            seg = m // 32
            r = m % 32
            k = seg * 32 + (r + 16) % 32
            perm[base + k, base + m] = 1.0
    return np.ascontiguousarray(cos2), np.ascontiguousarray(sin2), perm


PV_ADAB, PV_NM, PV_NF, PV_N = 0, 48, 56, 64
PE_SD, PE_GB, PE_LBL, PE_ON, PE_LRE, PE_LIM, PE_LST, PE_CR, PE_CI, PE_N = 0, 4, 8, 16, 17, 81, 145, 209, 1233, 2257
PO_LAM, PO_QG, PO_KG, PO_SUB, PO_QA, PO_KVA, PO_NQ, PO_NK, PO_RQ, PO_RK, PO_N = 0, 4, 5, 6, 7, 10, 12, 13, 14, 15, 16


def colmaj(v, nch):
    return np.ascontiguousarray(np.asarray(v, np.float32).reshape(nch, 128).T)


def pack_inputs(inp):
    f = lambda a: np.asarray(a, np.float32)
    pv = np.zeros((DEPTH, 128, PV_N), np.float32)
    for l in range(DEPTH):
        pv[l, :, PV_ADAB:PV_ADAB + 48] = colmaj(f(inp["ada_b"])[l], 48)
        pv[l, :, PV_NM:PV_NM + 8] = colmaj(f(inp["norm_mix"])[l], 8)
        pv[l, :, PV_NF:PV_NF + 8] = colmaj(f(inp["norm_ffn"])[l], 8)
    ne = 2
    pe = np.zeros((ne, 128, PE_N), np.float32)
    bt = np.zeros((ne, 2, 32, 16, 128), np.float32)
    dup = lambda a: np.concatenate([a, a], 0)
    for e in range(ne):
        pe[e, :, PE_SD:PE_SD + 4] = colmaj(f(inp["s5_d"])[e], 4)
        pe[e, :, PE_GB:PE_GB + 4] = colmaj(f(inp["s5_glu_b"])[e], 4)
        for e2 in range(ne):
            pe[e, :, PE_LBL + 4 * e2:PE_LBL + 4 * e2 + 4] = colmaj(f(inp["hgrn_lb_logits"])[e2], 4)
        pe[e, :, PE_ON] = f(inp["hgrn_out_norm"])[e]
        for d in range(2):
            pe[e, :, PE_LRE + 32 * d:PE_LRE + 32 * d + 32] = dup(f(inp["s5_lambda_re"])[e, d].T)
            pe[e, :, PE_LIM + 32 * d:PE_LIM + 32 * d + 32] = dup(f(inp["s5_lambda_im"])[e, d].T)
            pe[e, :, PE_LST + 32 * d:PE_LST + 32 * d + 32] = f(inp["s5_log_step"])[e, d][None, :]
            cr = f(inp["s5_c_re"])[e, d]
            ci = f(inp["s5_c_im"])[e, d]
            crt = dup(cr.transpose(2, 0, 1).reshape(64, 32 * 16))
            cit = dup(ci.transpose(2, 0, 1).reshape(64, 32 * 16))
            pe[e, :, PE_CR + 512 * d:PE_CR + 512 * d + 512] = crt
            pe[e, :, PE_CI + 512 * d:PE_CI + 512 * d + 512] = cit
            br = f(inp["s5_b_re"])[e, d]
            bi = f(inp["s5_b_im"])[e, d]
            bt[e, d, :, :, 0:64] = br.transpose(0, 2, 1)
            bt[e, d, :, :, 64:128] = bi.transpose(0, 2, 1)
    no = 2
    po = np.zeros((no, 128, PO_N), np.float32)
    for o in range(no):
        po[o, :64, PO_LAM:PO_LAM + 4] = f(inp["diff_lambda"])[o].T
        po[o, :, PO_QG] = np.tile(f(inp["diff_qk_norm"])[o, 0], 2)
        po[o, :, PO_KG] = np.tile(f(inp["diff_qk_norm"])[o, 1], 2)
        po[o, :, PO_SUB] = f(inp["diff_subln"])[o]
        po[o, :, PO_QA:PO_QA + 3] = colmaj(f(inp["mla_q_a_norm"])[o], 3)
        po[o, :, PO_KVA:PO_KVA + 2] = colmaj(f(inp["mla_kv_a_norm"])[o], 2)
        po[o, :, PO_NQ] = f(inp["mla_nope_norm"])[o, 0]
        po[o, :, PO_NK] = f(inp["mla_nope_norm"])[o, 1]
        po[o, :64, PO_RQ] = f(inp["mla_rope_norm"])[o, 0]
        po[o, :64, PO_RK] = f(inp["mla_rope_norm"])[o, 1]
    return pv, pe, bt, po


class Builder:
    def __init__(self, layers, dbg=None):
        self.layers = layers
        self.dbg = dbg
        nc = bass.Bass("TRN2", target_bir_lowering=False)
        self.nc = nc
        self.kb = KB(nc)
        dt = lambda name, shape, kind="ExternalInput", dtype=F32: nc.dram_tensor(name, list(shape), dtype, kind=kind).ap()
        self.d_h0 = dt("h0", [D, NT])
        self.d_sc = dt("sc", [128, 8, 2])
        self.d_cf = dt("cf", [128, CF_N])
        self.d_c2 = dt("c2", [128, C2_N])
        self.d_cos = dt("ropecos", [128, SEQ])
        self.d_sin = dt("ropesin", [128, SEQ])
        self.d_perm = dt("ropeperm", [128, 128])
        self.d_pv = dt("pv", [DEPTH, 128, PV_N])
        self.d_pe = dt("pe", [2, 128, PE_N])
        self.d_bt = dt("bt", [2, 2, 32, 16, 128])
        self.d_po = dt("po", [2, 128, PO_N])
        self.d_ada_w = dt("ada_w", [DEPTH, D, 6 * D])
        self.d_w_out = dt("w_out", [DEPTH, D, D])
        self.d_ffn_w_in = dt("ffn_w_in", [DEPTH, D, 2 * FFH])
        self.d_ffn_w_out = dt("ffn_w_out", [DEPTH, FFH, D])
        self.d_ab_w_in = dt("ab_w_in", [2, D, 3072])
        self.d_glu_w = dt("s5_glu_w", [2, 512, 512])
        self.d_cd_w_in = dt("cd_w_in", [2, D, 2240])
        self.d_w_uq = dt("mla_w_uq", [2, 384, 768])
        self.d_w_ukv = dt("mla_w_ukv", [2, 256, 1024])
        self.d_out = dt("out", [D, SEQ], kind="ExternalOutput")
        self.final = []
        self.skip_ctx = False
        self.need_fence = False
        self.wb_i = 0
        self.ps_i = 0
        self.ps_avail = list(range(8))
        self.uid = 0

    def scope(self):
        b = self

        class _Scope(ExitStack):
            def __exit__(self, *a):
                r = ExitStack.__exit__(self, *a)
                b.need_fence = True
                return r

            def close(self):
                ExitStack.close(self)
                b.need_fence = True
        return _Scope()

    def sb(self, es, name, shape, dtype=F32):
        if self.need_fence and FENCE_ON:
            self.kb.fence()
            self.need_fence = False
        return es.enter_context(self.nc.sbuf_tensor("sb_" + name, list(shape), dtype))

    def mm(self, out, lhsT, rhs, start, stop, R, W):
        self.kb.op("pe", lambda e: e.matmul(out, lhsT=lhsT, rhs=rhs, start=start, stop=stop), R, W)

    def tr(self, out, in_, ident, R, W):
        self.kb.op("pe", lambda e: e.transpose(out, in_, ident), R, W)

    def act(self, out, in_, func, R, W, bias=None, scale=1.0):
        if bias is None:
            self.kb.op("act", lambda e: e.activation(out=out, in_=in_, func=func, scale=scale), R, W)
        else:
            self.kb.op("act", lambda e: e.activation(out=out, in_=in_, func=func, bias=bias, scale=scale), R, W)

    def tt(self, out, in0, in1, op, R, W, eng="dve"):
        self.kb.op(eng, lambda e: e.tensor_tensor(out=out, in0=in0, in1=in1, op=op), R, W)

    def ts(self, out, in0, s1, s2, op0, op1, R, W, eng="dve"):
        if s2 is None:
            self.kb.op(eng, lambda e: e.tensor_scalar(out=out, in0=in0, scalar1=s1, scalar2=None, op0=op0), R, W)
        else:
            self.kb.op(eng, lambda e: e.tensor_scalar(out=out, in0=in0, scalar1=s1, scalar2=s2, op0=op0, op1=op1), R, W)

    def stt(self, out, in0, scalar, in1, op0, op1, R, W, eng="dve"):
        self.kb.op(eng, lambda e: e.scalar_tensor_tensor(out=out, in0=in0, scalar=scalar, in1=in1, op0=op0, op1=op1), R, W)

    def cp(self, out, in_, R, W, eng="dve"):
        if eng == "act":
            self.kb.op("act", lambda e: e.copy(out=out, in_=in_), R, W)
        else:
            self.kb.op(eng, lambda e: e.tensor_copy(out=out, in_=in_), R, W)

    def memset(self, ap, val, W, eng="dve"):
        self.kb.op(eng, lambda e: e.memset(ap, val), (), W)

    def dma(self, out, in_, R, W, q="sp"):
        return self.kb.op(q, lambda e: e.dma_start(out=out, in_=in_), R, W, dma=True)

    def dump(self, name, ap, shape, dtype, R):
        if not self.dbg:
            return
        t = self.nc.dram_tensor("dbg_" + name, list(shape), dtype, kind="ExternalOutput").ap()
        self.final.append(self.dma(t, ap, R, (), q="sp"))

    def scan(self, out, d0, d1, init, R, W):
        self.kb.op("dve", lambda e: e.tensor_tensor_scan(out=out, data0=d0, data1=d1, initial=init, op0=ALU.mult, op1=ALU.add), R, W)

    def ps(self):
        k = self.ps_avail[self.ps_i % len(self.ps_avail)]
        self.ps_i += 1
        return self.psum[k], f"ps{k}"

    def wload(self, src, kc, n):
        i = self.wb_i % len(self.wb)
        self.wb_i += 1
        t = self.wb[i]
        key = f"wb{i}"
        assert kc * n <= 4096
        v = bass.AP(t.tensor, t.offset, [list(t.ap[0]), [n, kc], [1, n]])
        self.dma(v, src.rearrange("(kc p) n -> p kc n", p=128), (), [key], q="pool")
        return v, key

    def build(self):
        nc = self.nc
        with ExitStack() as es:
            self.H = self.sb(es, "H", [128, 8, NT])
            self.A = self.sb(es, "A", [128, 8, NT], BF16)
            self.cf = self.sb(es, "cf", [128, CF_N])
            self.onesb = self.sb(es, "onesb", [128, 128], BF16)
            self.s2 = self.sb(es, "s2", [128, 8, 2], BF16)
            self.s2f = self.sb(es, "s2f", [128, 8, 2])
            self.mod2 = [self.sb(es, f"mod{i}", [128, 48, 2]) for i in range(2)]
            self.gs2 = [self.sb(es, f"gs{i}", [128, 2, 8, 2]) for i in range(2)]
            pvt1 = self.sb(es, "pvt", [128, PV_N])
            self.pvt2 = [pvt1, pvt1]
            self.epsT = self.sb(es, "epsT", [128, 1])
            self.oneT = self.sb(es, "oneT", [128, 1])
            self.wbt = [self.sb(es, f"wb{i}", [128, 4096], BF16) for i in range(3)]
            self.wb = [t[:] for t in self.wbt]
            self.psum = [es.enter_context(nc.psum_tensor(f"ps{i}", [128, 512], F32)) for i in range(8)]
            self.ident = self.cf[:, CF_ID:CF_ID + 128]

            self.dma(self.cf[:], self.d_cf, (), ["cf"])
            self.dma(self.s2f[:], self.d_sc, (), ["s2f"])
            for c in range(8):
                self.dma(self.H[:, c, :], self.d_h0[c * 128:(c + 1) * 128, :], (), [f"H{c}.{b}" for b in range(5)], q="sp")
            self.memset(self.onesb[:], 1.0, ["onesb"])
            self.memset(self.epsT[:], EPS, ["epsT"])
            self.memset(self.oneT[:], 1.0, ["oneT"])
            self.act(self.s2[:], self.s2f[:], AF.Silu, ["s2f"], ["s2"])

            for l in self.layers:
                self.layer(l)

            for c in range(8):
                self.final.append(self.dma(self.d_out[c * 128:(c + 1) * 128, :], self.H[:, c, CTX:NT],
                                           [f"H{c}.{b}" for b in range(5)], (), q="sp"))
            self.kb.emit(final_wait_ops=self.final)
        return nc

    def ada_begin(self, l, bank):
        self.dma(self.pvt2[l % 2][:], self.d_pv[l], (), ["pvt"])
        self.ada_ps = (self.psum[bank], f"ps{bank}")

    def ada_piece(self, l, jg, loader):
        pst, pk = self.ada_ps
        w, wk = loader(self.d_ada_w[l][:, jg * 512:(jg + 1) * 512])
        for jj in range(4):
            j = jg * 4 + jj
            for kc in range(8):
                self.mm(pst[:, 2 * j:2 * j + 2], w[:, kc, jj * 128:(jj + 1) * 128], self.s2[:, kc, :], kc == 0, kc == 7,
                        [wk, "s2"], [pk])

    def ada_end(self, l):
        pst, pk = self.ada_ps
        p = l % 2
        mod, gs, pvt = self.mod2[p], self.gs2[p], self.pvt2[p]
        pv3 = pst[:, 0:96].rearrange("p (j v) -> p j v", v=2)
        self.tt(mod[:], pv3, pvt[:, PV_ADAB:PV_ADAB + 48].unsqueeze(2).to_broadcast([128, 48, 2]), ALU.add,
                [pk, "pvt"], [f"mod{p}"])
        for w_, (nofs, sofs) in enumerate(((PV_NM, 8), (PV_NF, 32))):
            self.ts(gs[:, w_, :, :], mod[:, sofs:sofs + 8, :], 1.0, None, ALU.add, None, [f"mod{p}"], [f"gs{p}"])
            self.tt(gs[:, w_, :, :], gs[:, w_, :, :],
                    pvt[:, nofs:nofs + 8].unsqueeze(2).to_broadcast([128, 8, 2]), ALU.mult, [f"gs{p}", "pvt"], [f"gs{p}"])

    def ada(self, l):
        self.ada_begin(l, 7)
        for jg in range(12):
            self.ada_piece(l, jg, lambda src: self.wload(src, 8, 512))
        self.ada_end(l)

    def norm_mod(self, w_, shift_ofs, es, skip0=False):
        sq = self.sb(es, f"nsq{self.uid}", [128, 2, 512], BF16)
        rstd = self.sb(es, f"nrs{self.uid}", [128, 512])
        tmp = self.sb(es, f"ntmp{self.uid}", [128, 2, 512])
        self.uid += 1
        for b, (t0, n, v) in enumerate(TB):
            if skip0 and b == 0:
                continue
            pst, pk = self.ps()
            for c in range(8):
                self.act(sq[:, c % 2, :n], self.H[:, c, t0:t0 + n], AF.Square, [f"H{c}.{b}"], [f"nsq{c % 2}"])
                self.mm(pst[:, :n], self.onesb[:], sq[:, c % 2, :n], c == 0, c == 7, [f"nsq{c % 2}", "onesb"], [pk])
            self.act(rstd[:, :n], pst[:, :n], AF.Ln, [pk, "epsT"], ["nrstd"], bias=self.epsT[:], scale=1.0 / D)
            self.act(rstd[:, :n], rstd[:, :n], AF.Exp, ["nrstd"], ["nrstd"], scale=-0.5)
            for c in range(8):
                self.tt(tmp[:, c % 2, :n], self.H[:, c, t0:t0 + n], rstd[:, :n], ALU.mult, [f"H{c}.{b}", "nrstd"], [f"ntmp{c % 2}"])
                self.act(self.A[:, c, t0:t0 + n], tmp[:, c % 2, :n], AF.Identity, [f"ntmp{c % 2}", self.kgs, self.kmod], [f"A{c}.{b}"],
                         bias=self.mod[:, shift_ofs + c, v:v + 1], scale=self.gs[:, w_, c, v:v + 1])

    def out_proj(self, l, kcs, src, keyf=None):
        nk = len(kcs)
        ws = []
        for og in range(2):
            ws.append(self.wload(self.d_w_out[l][kcs[0] * 128:(kcs[-1] + 1) * 128, og * 512:(og + 1) * 512], nk, 512))
        for o in range(8):
            w, wk = ws[o // 4]
            for b, (t0, n, v) in enumerate(TB):
                if self.skip_ctx and b == 0:
                    continue
                pst, pk = self.ps()
                for i, kc in enumerate(kcs):
                    self.mm(pst[:, :n], w[:, i, (o % 4) * 128:(o % 4 + 1) * 128], src(kc)[:, t0:t0 + n], i == 0, i == nk - 1,
                            [wk, keyf(kc, b) if keyf else f"M{kc}.{b}"], [pk])
                self.stt(self.H[:, o, t0:t0 + n], pst[:, :n], self.mod[:, 16 + o, v:v + 1], self.H[:, o, t0:t0 + n],
                         ALU.mult, ALU.add, [pk, self.kmod, f"H{o}.{b}"], [f"H{o}.{b}"])

    def ffn(self, l, es, nxt=None):
        hact = self.sb(es, f"hact{l}", [128, 4, NT], BF16)
        sg = self.sb(es, f"fsg{l}", [128, 2, 512])
        groups = [(g * 4, 4) for g in range(5)] + [(20, 2)]
        if nxt is not None:
            adaw = [self.sb(es, f"adaw{l}_{i}", [128, 4096], BF16) for i in range(2)]
            acnt = [0]

            def aload(src):
                i = acnt[0] % 2
                acnt[0] += 1
                t = adaw[i][:]
                v_ = bass.AP(t.tensor, t.offset, [list(t.ap[0]), [512, 8], [1, 512]])
                self.dma(v_, src.rearrange("(kc p) n -> p kc n", p=128), (), [f"adaw{i}"], q="pool")
                return v_, f"adaw{i}"
            self.ada_begin(nxt, 7)
            self.ps_avail = list(range(7))
        for gi, (hc0, ng) in enumerate(groups):
            if nxt is not None:
                for jg in (2 * gi, 2 * gi + 1):
                    self.ada_piece(nxt, jg, aload)
            wg, wgk = self.wload(self.d_ffn_w_in[l][:, hc0 * 128:(hc0 + ng) * 128], 8, ng * 128)
            wu, wuk = self.wload(self.d_ffn_w_in[l][:, FFH + hc0 * 128:FFH + (hc0 + ng) * 128], 8, ng * 128)
            wo, wok = self.wload(self.d_ffn_w_out[l][hc0 * 128:(hc0 + ng) * 128, :], ng, 1024)
            for j in range(ng):
                for b, (t0, n, v) in enumerate(TB):
                    if self.skip_ctx and b == 0:
                        continue
                    pg, pgk = self.ps()
                    pu, puk = self.ps()
                    for kc in range(8):
                        self.mm(pg[:, :n], wg[:, kc, j * 128:(j + 1) * 128], self.A[:, kc, t0:t0 + n], kc == 0, kc == 7,
                                [wgk, f"A{kc}.{b}"], [pgk])
                    for kc in range(8):
                        self.mm(pu[:, :n], wu[:, kc, j * 128:(j + 1) * 128], self.A[:, kc, t0:t0 + n], kc == 0, kc == 7,
                                [wuk, f"A{kc}.{b}"], [puk])
                    s = (j * 5 + b) % 2
                    self.act(sg[:, s, :n], pg[:, :n], AF.Silu, [pgk], [f"fsg{s}"])
                    self.tt(hact[:, j, t0:t0 + n], sg[:, s, :n], pu[:, :n], ALU.mult, [f"fsg{s}", puk], [f"hact{j}.{b}"])
            for o in range(8):
                for b, (t0, n, v) in enumerate(TB):
                    if self.skip_ctx and b == 0:
                        continue
                    pst, pk = self.ps()
                    for j in range(ng):
                        self.mm(pst[:, :n], wo[:, j, o * 128:(o + 1) * 128], hact[:, j, t0:t0 + n], j == 0, j == ng - 1,
                                [wok, f"hact{j}.{b}"], [pk])
                    self.stt(self.H[:, o, t0:t0 + n], pst[:, :n], self.mod[:, 40 + o, v:v + 1], self.H[:, o, t0:t0 + n],
                             ALU.mult, ALU.add, [pk, self.kmod, f"H{o}.{b}"], [f"H{o}.{b}"])
        if nxt is not None:
            self.ada_end(nxt)
            self.ps_avail = list(range(8))

    def layer(self, l):
        self.skip_ctx = (l == DEPTH - 1)
        p = l % 2
        self.mod, self.gs, self.pvt = self.mod2[p], self.gs2[p], self.pvt2[p]
        self.kmod, self.kgs = f"mod{p}", f"gs{p}"
        if l == self.layers[0] or not PREFETCH_ADA:
            self.ada(l)
        with self.scope() as es:
            self.norm_mod(0, 0, es)
        with self.scope() as es:
            if l % 2 == 0:
                self.even_mixer(l // 2, es)
            else:
                self.odd_mixer(l // 2, l, es)
        with self.scope() as es:
            self.norm_mod(1, 24, es, skip0=self.skip_ctx)
        with self.scope() as es:
            nxt = self.layers[self.layers.index(l) + 1] if self.layers.index(l) + 1 < len(self.layers) else None
            self.ffn(l, es, nxt if PREFETCH_ADA else None)

    def frac_sin(self, out, q, add, tf, ti, R, W, tag):
        self.ts(tf, q, float(add), None, ALU.add, None, R, [tag + "f"])
        self.cp(ti, tf, [tag + "f"], [tag + "i"])
        self.tt(tf, tf, ti, ALU.subtract, [tag + "f", tag + "i"], [tag + "f"])
        self.act(out, tf, AF.Sin, [tag + "f"], W, scale=TWO_PI)

    def proj_chunk(self, w, wk, col0, ncols, evac):
        for b, (t0, n, v) in enumerate(TB):
            pst, pk = self.ps()
            for kc in range(8):
                self.mm(pst[:ncols, :n], w[:, kc, col0:col0 + ncols], self.A[:, kc, t0:t0 + n], kc == 0, kc == 7,
                        [wk, f"A{kc}.{b}"], [pk])
            evac(pst, pk, b, t0, n, v)

    def even_mixer(self, e, es):
        l = 2 * e
        prm = self.sb(es, f"eprm{e}", [128, PE_CR])
        self.dma(prm[:], self.d_pe[e][:, 0:PE_CR], (), ["eprm"])
        with self.scope() as es2:
            self.Ma = self.sb(es2, f"Ma{self.uid}", [128, 4, NT], BF16)
            self.uid += 1
            if SKIP == "s5":
                self.memset(self.Ma[:], 0.0, [f"M{c}.{b}" for c in range(4) for b in range(5)], eng="pool")
            else:
                self.s5(e, prm, es2)
            self.dump(f"Ma{e}", self.Ma[:], [128, 4, NT], BF16, [f"M{c}.{b}" for c in range(4) for b in range(5)])
            self.out_proj(l, [0, 1, 2, 3], lambda kc: self.Ma[:, kc, :])
        with self.scope() as es2:
            self.Mb = self.sb(es2, f"Mb{self.uid}", [128, 4, NT], BF16)
            self.uid += 1
            if SKIP == "hgrn":
                self.memset(self.Mb[:], 0.0, [f"M{c}.{b}" for c in range(4, 8) for b in range(5)], eng="pool")
            else:
                self.hgrn(e, prm, es2)
            self.dump(f"Mb{e}", self.Mb[:], [128, 4, NT], BF16, [f"M{c}.{b}" for c in range(4, 8) for b in range(5)])
            self.out_proj(l, [4, 5, 6, 7], lambda kc: self.Mb[:, kc - 4, :])

    def mchunk(self, c):
        return self.Ma[:, c, :] if c < 4 else self.Mb[:, c - 4, :]

    def s5(self, e, prm, es_outer):
        nc = self.nc
        es = es_outer.enter_context(self.scope())
        S = lambda name, shape, dt=F32: self.sb(es, f"s5{name}{e}", shape, dt)
        P64 = {k: S(k, [128, 64]) for k in ("r", "q1", "c256", "s256", "c512", "s512")}
        L1f = S("L1f", [128, 64, 16], BF16)
        L2f = S("L2f", [128, 64, 16], BF16)
        esc = es.enter_context(self.scope())
        for k in ("cr", "ci"):
            P64[k] = self.sb(esc, f"s5{k}{e}", [128, 64])
        esp = es.enter_context(self.scope())
        for k in ("lr", "step", "cs", "sn", "ar", "ai", "den", "t0", "t1", "tf"):
            P64[k] = self.sb(esp, f"s5{k}{e}", [128, 64])
        ti64 = self.sb(esp, f"s5ti64{e}", [128, 64], I32)
        lre, lim, lst = prm[:, PE_LRE:PE_LRE + 64], prm[:, PE_LIM:PE_LIM + 64], prm[:, PE_LST:PE_LST + 64]
        p = lambda k: P64[k][:]
        self.ts(p("lr"), lre, -1e-4, None, ALU.min, None, ["eprm"], ["s5lr"])
        self.act(p("step"), lst, AF.Exp, ["eprm"], ["s5step"])
        self.tt(p("t0"), p("lr"), p("step"), ALU.mult, ["s5lr", "s5step"], ["s5t0"])
        self.act(p("r"), p("t0"), AF.Exp, ["s5t0"], ["s5r"])
        self.tt(p("q1"), lim, p("step"), ALU.mult, ["eprm", "s5step"], ["s5q1"])
        self.ts(p("q1"), p("q1"), 1.0 / TWO_PI, None, ALU.mult, None, ["s5q1"], ["s5q1"])
        self.frac_sin(p("cs"), p("q1"), 0.25, p("tf"), ti64[:], ["s5q1"], ["s5cs"], "s5x")
        self.frac_sin(p("sn"), p("q1"), 0.0, p("tf"), ti64[:], ["s5q1"], ["s5sn"], "s5x")
        for T, ck, sk in ((256.0, "c256", "s256"), (512.0, "c512", "s512")):
            self.ts(p("t1"), p("q1"), T, None, ALU.mult, None, ["s5q1"], ["s5t1"])
            self.frac_sin(p(ck), p("t1"), 0.25, p("tf"), ti64[:], ["s5t1"], ["s5" + ck], "s5x")
            self.frac_sin(p(sk), p("t1"), 0.0, p("tf"), ti64[:], ["s5t1"], ["s5" + sk], "s5x")
            self.ts(p(sk), p(sk), self.cf[:, CF_SGN:CF_SGN + 1], None, ALU.mult, None, ["s5" + sk, "cf"], ["s5" + sk])
        self.tt(p("ar"), p("r"), p("cs"), ALU.mult, ["s5r", "s5cs"], ["s5ar"])
        self.tt(p("ai"), p("r"), p("sn"), ALU.mult, ["s5r", "s5sn"], ["s5ai"])
        self.ts(p("ar"), p("ar"), -1.0, None, ALU.add, None, ["s5ar"], ["s5ar"])
        self.tt(p("den"), p("lr"), p("lr"), ALU.mult, ["s5lr"], ["s5den"])
        self.tt(p("t0"), lim, lim, ALU.mult, ["eprm", "s5r"], ["s5t0"])
        self.tt(p("den"), p("den"), p("t0"), ALU.add, ["s5den", "s5t0"], ["s5den"])
        self.kb.op("dve", lambda e_: e_.reciprocal(out=p("den"), in_=p("den")), ["s5den"], ["s5den"])
        self.tt(p("t0"), p("ar"), p("lr"), ALU.mult, ["s5ar", "s5lr"], ["s5t0"])
        self.tt(p("t1"), p("ai"), lim, ALU.mult, ["s5ai", "eprm"], ["s5t1"])
        self.tt(p("t0"), p("t0"), p("t1"), ALU.add, ["s5t0", "s5t1"], ["s5t0"])
        self.tt(p("cr"), p("t0"), p("den"), ALU.mult, ["s5t0", "s5den"], ["s5cr"])
        self.tt(p("t0"), p("ai"), p("lr"), ALU.mult, ["s5ai", "s5lr", "s5cr"], ["s5t0"])
        self.tt(p("t1"), p("ar"), lim, ALU.mult, ["s5ar", "eprm"], ["s5t1"])
        self.tt(p("t0"), p("t0"), p("t1"), ALU.subtract, ["s5t0", "s5t1"], ["s5t0"])
        self.tt(p("ci"), p("t0"), p("den"), ALU.mult, ["s5t0", "s5den"], ["s5ci"])
        esp.close()
        with self.scope() as es3:
            CR = self.sb(es3, f"s5CR{e}", [128, 64, 16])
            CI = self.sb(es3, f"s5CI{e}", [128, 64, 16])
            Cr_ = self.sb(es3, f"s5Cr_{e}", [128, 64, 16])
            Ci_ = self.sb(es3, f"s5Ci_{e}", [128, 64, 16])
            tq = self.sb(es3, f"s5tq{e}", [128, 64, 16])
            self.dma(CR[:].rearrange("p a b -> p (a b)"), self.d_pe[e][:, PE_CR:PE_CR + 1024], (), ["s5CR"])
            self.dma(CI[:].rearrange("p a b -> p (a b)"), self.d_pe[e][:, PE_CI:PE_CI + 1024], (), ["s5CI"])
            crb = p("cr").unsqueeze(2).to_broadcast([128, 64, 16])
            cib = p("ci").unsqueeze(2).to_broadcast([128, 64, 16])
            self.tt(Cr_[:], CR[:], crb, ALU.mult, ["s5CR", "s5cr"], ["s5Cr_"])
            self.tt(tq[:], CI[:], cib, ALU.mult, ["s5CI", "s5ci"], ["s5tq"])
            self.tt(Cr_[:], Cr_[:], tq[:], ALU.subtract, ["s5Cr_", "s5tq"], ["s5Cr_"])
            self.tt(Ci_[:], CR[:], cib, ALU.mult, ["s5CR", "s5ci"], ["s5Ci_"])
            self.tt(tq[:], CI[:], crb, ALU.mult, ["s5CI", "s5cr", "s5Cr_"], ["s5tq"])
            self.tt(Ci_[:], Ci_[:], tq[:], ALU.add, ["s5Ci_", "s5tq"], ["s5Ci_"])
            mlo, mhi = self.cf[:, CF_MLO:CF_MLO + 1], self.cf[:, CF_MHI:CF_MHI + 1]
            self.ts(tq[:], Ci_[:], mhi, None, ALU.mult, None, ["s5Ci_", "cf", "s5Ci_"], ["s5tq"])
            self.stt(L1f[:], Cr_[:], mlo, tq[:], ALU.mult, ALU.subtract, ["s5Cr_", "s5tq", "cf"], ["s5L1f"])
            self.ts(tq[:], Cr_[:], mhi, -1.0, ALU.mult, ALU.mult, ["s5Cr_", "cf", "s5L1f"], ["s5tq"])
            self.ts(Ci_[:], Ci_[:], mlo, None, ALU.mult, None, ["s5Ci_", "cf"], ["s5Ci_"])
            self.tt(L2f[:], tq[:], Ci_[:], ALU.subtract, ["s5tq", "s5Ci_"], ["s5L2f"])
        esc.close()

        u = S("u", [128, NT], BF16)
        y = S("y", [128, NT])
        BW1 = S("BW1", [128, 8, 128], BF16)
        BW2 = S("BW2", [128, 8, 128], BF16)
        LR1 = S("LR1", [128, 8, 128], BF16)
        LR2 = S("LR2", [128, 8, 128], BF16)
        Ec = [S(f"Ec{i}", [128, 512]) for i in range(2)]
        Es = [S(f"Es{i}", [128, 512]) for i in range(2)]
        iot = S("iot", [128, 512])
        self.dma(iot[:], self.d_c2[:, C2_IOTA:C2_IOTA + 512], (), ["s5iot"])
        tib = S("tib", [128, 512], I32)
        R512 = [S(f"R512{i}", [128, 128]) for i in range(2)]
        R256 = [S(f"R256{i}", [128, 128]) for i in range(2)]
        Ta = [S(f"Ta{i}", [128, 512]) for i in range(2)]
        Tb = [S(f"Tb{i}", [128, 512]) for i in range(2)]
        cg = [S(f"cg{i}", [128, 512], BF16) for i in range(2)]
        sg = [S(f"sg{i}", [128, 512], BF16) for i in range(2)]
        carry = S("carry", [128, 2])
        qtr = S("qtr", [128, 1])
        self.memset(qtr[:], 0.25, ["s5qtr"])
        iota = iot[:]
        Jm = self.cf[:, CF_J:CF_J + 128]

        def diag(t):
            a = t[:]
            return bass.AP(a.tensor, a.offset, [list(a.ap[0]), [144, 8], [1, 16]])

        order = {0: [0, 1, 2, 3, 4], 1: [0, 4, 3, 2, 1]}
        step = 0
        for c4 in range(LIM.get('c4', 4)):
            w, wk = self.wload(self.d_ab_w_in[e][:, c4 * 128:(c4 + 1) * 128], 8, 128)

            def evac_u(pst, pk, b, t0, n, v, c4=c4):
                self.act(u[:, t0:t0 + n], pst[:, :n], AF.Copy, [pk], [f"s5u.{b}"])
                self.act(y[:, t0:t0 + n], pst[:, :n], AF.Copy, [pk, "eprm"], [f"s5y.{b}"],
                         scale=prm[:, PE_SD + c4:PE_SD + c4 + 1])
            self.proj_chunk(w, wk, 0, 128, evac_u)
            for d in range(LIM.get('d', 2)):
                bwk = [f"s5BW1.{j}" for j in range(8)]
                self.memset(BW1[:], 0.0, bwk, eng="pool")
                for j in range(8):
                    self.dma(BW1[16 * j:16 * j + 16, j, :], self.d_bt[e, d, c4 * 8 + j], (), [bwk[j]], q="pool")
                self.cp(BW2[:, :, 0:64], BW1[:, :, 64:128], bwk, ["s5BW2"], eng="pool")
                self.ts(BW2[:, :, 64:128], BW1[:, :, 0:64], -1.0, None, ALU.mult, None, bwk, ["s5BW2"], eng="pool")
                base = (d * 32 + c4 * 8) * 16
                for LR, Lf, key in ((LR1, L1f, "s5L1f"), (LR2, L2f, "s5L2f")):
                    self.memset(LR[:], 0.0, ["s5" + ("LR1" if LR is LR1 else "LR2")], eng="pool")
                    src = Lf[:].rearrange("p a b -> p (a b)")[:, base:base + 128].rearrange("p (j c) -> p j c", c=16)
                    self.cp(diag(LR), src, [key], ["s5" + ("LR1" if LR is LR1 else "LR2")], eng="pool")
                self.ps_avail = [0, 1]
                blocks = order[d][:LIM.get('b', 5)]
                for jp in range(0, LIM.get('j', 8), 2):
                    chains = []
                    for jj in range(2):
                        j = jp + jj
                        col = d * 32 + c4 * 8 + j
                        cs_ = lambda k, col=col: P64[k][:, col:col + 1]
                        for tbl, add, tk in ((Es[jj], 0.0, f"s5Es{jj}"), (Ec[jj], 0.25, f"s5Ec{jj}")):
                            if add == 0.0:
                                self.act(tbl[:], iota, AF.Copy, ["s5iot", "s5q1"], [tk], scale=cs_("q1"))
                            else:
                                self.act(tbl[:], iota, AF.Identity, ["s5iot", "s5q1", "s5qtr"], [tk], scale=cs_("q1"), bias=qtr[:])
                            self.cp(tib[:], tbl[:], [tk], ["s5yi"])
                            self.tt(tbl[:], tbl[:], tib[:], ALU.subtract, [tk, "s5yi"], [tk])
                            self.act(tbl[:], tbl[:], AF.Sin, [tk], [tk], scale=TWO_PI)
                        for Rm, rk, ck, sk in ((R512[jj], f"s5R512{jj}", "c512", "s512"), (R256[jj], f"s5R256{jj}", "c256", "s256")):
                            self.act(Rm[:], self.ident, AF.Copy, ["cf", "s5" + ck], [rk], scale=cs_(ck))
                            self.stt(Rm[:], Jm, cs_(sk), Rm[:], ALU.mult, ALU.add, ["cf", "s5" + sk, rk], [rk])
                        chains.append((jj, j, cs_))

                    def mmM(jj, j, bi):
                        t0, n, v = TB[blocks[bi]]
                        ub = u[:, t0:t0 + n]
                        if d == 1:
                            ub = rev(ub, n)
                        M1, k1, M2, k2 = self.psum[2 + 2 * jj], f"ps{2 + 2 * jj}", self.psum[3 + 2 * jj], f"ps{3 + 2 * jj}"
                        self.mm(M1[:, :n], BW1[:, j, :], ub, True, True, [f"s5BW1.{j}", f"s5u.{blocks[bi]}"], [k1])
                        self.mm(M2[:, :n], BW2[:, j, :], ub, True, True, ["s5BW2", f"s5u.{blocks[bi]}"], [k2])

                    for jj, j, cs_ in chains:
                        mmM(jj, j, 0)
                    pend = None
                    for bi, b in enumerate(blocks):
                        t0, n, v = TB[b]
                        for jj, j, cs_ in chains:
                            M1, k1, M2, k2 = self.psum[2 + 2 * jj], f"ps{2 + 2 * jj}", self.psum[3 + 2 * jj], f"ps{3 + 2 * jj}"
                            self.tt(Ta[jj][:, :n], M1[:, :n], Ec[jj][:, :n], ALU.mult, [k1, f"s5Ec{jj}"], [f"s5Ta{jj}"])
                            self.tt(Tb[jj][:, :n], M2[:, :n], Es[jj][:, :n], ALU.mult, [k2, f"s5Es{jj}"], [f"s5Tb{jj}"])
                            if bi < len(blocks) - 1:
                                mmM(jj, j, bi + 1)
                            self.tt(Ta[jj][:, :n], Ta[jj][:, :n], Tb[jj][:, :n], ALU.add, [f"s5Ta{jj}", f"s5Tb{jj}"], [f"s5Ta{jj}"])
                            init = 0.0 if bi == 0 else carry[:, jj:jj + 1]
                            self.scan(Tb[jj][:, :n], cs_("r").to_broadcast([128, n]), Ta[jj][:, :n], init,
                                      [f"s5Ta{jj}", "s5r", f"s5carry{jj}"], [f"s5Tb{jj}"])
                            self.tt(cg[jj][:, :n], Ec[jj][:, :n], Tb[jj][:, :n], ALU.mult, [f"s5Ec{jj}", f"s5Tb{jj}"], [f"s5cg{jj}"], eng="pool")
                            self.tt(sg[jj][:, :n], Es[jj][:, :n], Tb[jj][:, :n], ALU.mult, [f"s5Es{jj}", f"s5Tb{jj}"], [f"s5sg{jj}"], eng="pool")
                            if bi < len(blocks) - 1:
                                Rm, rk = (R256[jj], f"s5R256{jj}") if n == 256 else (R512[jj], f"s5R512{jj}")
                                pc, pck = self.psum[6 + jj], f"ps{6 + jj}"
                                self.mm(pc[:, 0:1], Rm[:], Tb[jj][:, n - 1:n], True, True, [rk, f"s5Tb{jj}"], [pck])
                                self.cp(carry[:, jj:jj + 1], pc[:, 0:1], [pck], [f"s5carry{jj}"], eng="act")
                        if pend is not None:
                            pend()
                        yb, ybk = self.ps()
                        for jj, j, cs_ in chains:
                            self.mm(yb[:, :n], LR1[:, j, :], cg[jj][:, :n], jj == 0, False, ["s5LR1", f"s5cg{jj}"], [ybk])
                            self.mm(yb[:, :n], LR2[:, j, :], sg[jj][:, :n], False, jj == 1, ["s5LR2", f"s5sg{jj}"], [ybk])

                        def pend(yb=yb, ybk=ybk, t0=t0, n=n, b=b):
                            src = yb[:, :n]
                            if d == 1:
                                src = rev(src, n)
                            self.tt(y[:, t0:t0 + n], y[:, t0:t0 + n], src, ALU.add, [f"s5y.{b}", ybk], [f"s5y.{b}"])
                    pend()
                self.ps_avail = list(range(8))
            for b, (t0, n, v) in enumerate(TB):
                yb_ = y[:, t0:t0 + n]
                self.tt(Ta[0][:, :n], yb_, yb_, ALU.mult, [f"s5y.{b}"], ["s5Ta0"])
                self.ts(Ta[0][:, :n], Ta[0][:, :n], 0.044715, 1.0, ALU.mult, ALU.add, ["s5Ta0"], ["s5Ta0"])
                self.tt(Ta[0][:, :n], Ta[0][:, :n], yb_, ALU.mult, ["s5Ta0", f"s5y.{b}"], ["s5Ta0"])
                self.act(Tb[0][:, :n], Ta[0][:, :n], AF.Sigmoid, ["s5Ta0"], ["s5Tb0"], scale=1.5957691216057308)
                self.tt(self.Ma[:, c4, t0:t0 + n], yb_, Tb[0][:, :n], ALU.mult, [f"s5y.{b}", "s5Tb0"], [f"M{c4}.{b}"])
        self.dump(f"s5u{e}", u[:], [128, NT], BF16, [f"s5u.{b}" for b in range(5)])
        self.dump(f"s5y{e}", y[:], [128, NT], F32, [f"s5y.{b}" for b in range(5)])
        self.dump(f"s5prm{e}", prm[:], [128, PE_CR], F32, ["eprm"])
        self.dump(f"s5L1f{e}", L1f[:], [128, 64, 16], BF16, ["s5L1f"])
        self.dump(f"s5Tb{e}", Tb[0][:], [128, 512], F32, ["s5Tb0"])
        self.dump(f"s5r{e}", P64["r"][:], [128, 64], F32, ["s5r"])
        self.dump(f"s5BW1{e}", BW1[:], [128, 8, 128], BF16, [f"s5BW1.{j}" for j in range(8)])
        es.close()
        wg, wgk = self.wload(self.d_glu_w[e], 4, 512)
        SG = self.sb(es_outer, f"s5SG{e}", [128, 4, 512], BF16)
        for b, (t0, n, v) in enumerate(TB):
            for c in range(4):
                pst, pk = self.ps()
                for kc in range(4):
                    self.mm(pst[:, :n], wg[:, kc, c * 128:(c + 1) * 128], self.Ma[:, kc, t0:t0 + n], kc == 0, kc == 3,
                            [wgk, f"M{kc}.{b}"], [pk])
                self.act(SG[:, c, :n], pst[:, :n], AF.Sigmoid, [pk, "eprm"], [f"s5SG{c}"], bias=prm[:, PE_GB + c:PE_GB + c + 1])
            for c in range(4):
                self.tt(self.Ma[:, c, t0:t0 + n], self.Ma[:, c, t0:t0 + n], SG[:, c, :n], ALU.mult,
                        [f"M{c}.{b}", f"s5SG{c}"], [f"M{c}.{b}"])

    def hgrn(self, e, prm, es):
        S = lambda name, shape, dt=F32: self.sb(es, f"hg{name}{e}", shape, dt)
        E0, E1, Ss, LB, OML, NOML = (S(k, [128, 4]) for k in ("E0", "E1", "Ss", "LB", "OML", "NOML"))
        self.act(E0[:], prm[:, PE_LBL:PE_LBL + 4], AF.Exp, ["eprm"], ["hgE0"])
        self.act(E1[:], prm[:, PE_LBL + 4:PE_LBL + 8], AF.Exp, ["eprm"], ["hgE1"])
        self.tt(Ss[:], E0[:], E1[:], ALU.add, ["hgE0", "hgE1"], ["hgSs"])
        self.kb.op("dve", lambda e_: e_.reciprocal(out=Ss[:], in_=Ss[:]), ["hgSs"], ["hgSs"])
        self.tt(E0[:], E0[:], Ss[:], ALU.mult, ["hgE0", "hgSs"], ["hgE0"])
        self.tt(E1[:], E1[:], Ss[:], ALU.mult, ["hgE1", "hgSs"], ["hgE1"])
        if e == 0:
            self.tt(LB[:], E0[:], E0[:], ALU.subtract, ["hgE0"], ["hgLB"])
        else:
            self.tt(LB[:], E0[:], E1[:], ALU.add, ["hgE0", "hgE1"], ["hgLB"])
            self.tt(LB[:], LB[:], E0[:], ALU.subtract, ["hgLB", "hgE0"], ["hgLB"])
        self.ts(OML[:], LB[:], -1.0, 1.0, ALU.mult, ALU.add, ["hgLB"], ["hgOML"])
        self.ts(NOML[:], LB[:], -1.0, None, ALU.add, None, ["hgLB"], ["hgNOML"])

        qs = S("qs", [128, NT], BF16)
        cmk = S("cmk", [128, 1024], BF16)
        self.dma(cmk[:], self.d_c2[:, C2_CMASK:C2_CMASK + 1024], (), ["hgcmk"], q="pool")
        CH = 32
        vtm = S("vtm", [CH, NT // CH, 128], BF16)
        O = S("O", [128, NT])
        T = {i: S(f"T{i}", [128, 512]) for i in (1, 2, 3, 5)}
        EB = S("EB", [128, 512])
        KK = S("KK", [128, 512])
        KH = T[3]
        QA = S("QA", [128, 512])
        QR = S("QR", [128, 512], BF16)
        KT = S("KT", [128, 512], BF16)
        sqb = QR
        PM = [S(f"PM{i}", [CH, CH], BF16) for i in range(3)]
        KHt = [S(f"KHt{i}", [CH, 128], BF16) for i in range(3)]
        St = [S(f"St{i}", [128, 128]) for i in range(2)]
        cm_f = cmk[:, 0:512]
        cm_b = cmk[:, 512:1024]
        mask = {0: self.cf[0:CH, CF_MF:CF_MF + CH], 1: self.cf[0:CH, CF_MB:CF_MB + CH]}
        order = {0: [0, 1, 2, 3, 4], 1: [0, 4, 3, 2, 1]}
        wcols = [512, 1024, 1536, 2048, 2560]
        for h in range(4):
            wl = lambda wi: self.wload(self.d_ab_w_in[e][:, wcols[wi] + 128 * h:wcols[wi] + 128 * h + 128], 8, 128)

            def evac_q(pst, pk, b, t0, n, v):
                self.act(qs[:, t0:t0 + n], pst[:, :n], AF.Silu, [pk], [f"hgqs.{b}"])
            wq, wqk = wl(0)
            self.proj_chunk(wq, wqk, 0, 128, evac_q)

            def evac_v(pst, pk, b, t0, n, v):
                self.cp(T[1][:, :n], pst[:, :n], [pk], ["hgT1"], eng="act")
                for ci in range(n // CH):
                    pt, ptk = self.ps()
                    self.tr(pt[0:CH, 0:128], T[1][:, ci * CH:ci * CH + CH], self.ident, ["hgT1", "cf"], [ptk])
                    self.cp(vtm[:, t0 // CH + ci, :], pt[0:CH, 0:128], [ptk], [f"hgvtm.{b}"])
            wv, wvk = wl(3)
            self.proj_chunk(wv, wvk, 0, 128, evac_v)

            for d in range(2):
                self.memset(St[0][:], 0.0, ["hgSt0"])
                kst = [0]
                lc = CH - 1 if d == 0 else 0
                wf, wfk = wl(1 + d)
                def part1(b):
                    t0, n, v = TB[b]
                    pf, pfk = self.ps()
                    for kc in range(8):
                        self.mm(pf[:, :n], wf[:, kc, :], self.A[:, kc, t0:t0 + n], kc == 0, kc == 7,
                                [wfk, f"A{kc}.{b}"], [pfk])
                    t = lambda i: T[i][:, :n]
                    self.act(t(1), pf[:, :n], AF.Exp, [pfk], ["hgT1"], scale=-1.0)
                    self.ts(t(1), t(1), 1.1420073898156842e26, None, ALU.min, None, ["hgT1"], ["hgT1"])
                    self.act(t(2), t(1), AF.Ln, ["hgT1", "oneT"], ["hgT2"], bias=self.oneT[:])
                    self.act(t(5), t(2), AF.Exp, ["hgT2"], ["hgT5"], scale=-1.0)
                    self.ts(KK[:, :n], t(5), NOML[:, h:h + 1], OML[:, h:h + 1], ALU.mult, ALU.add, ["hgT5", "hgNOML", "hgOML"], ["hgKK"])

                def part2(b):
                    t0, n, v = TB[b]
                    nch = n // CH
                    t = lambda i: T[i][:, :n]
                    v3 = lambda ap: ap.rearrange("p (c s) -> p c s", s=CH)
                    bc = lambda ap, col: v3(ap)[:, :, col:col + 1].to_broadcast([128, nch, CH])
                    self.act(t(3), t(1), AF.Ln, ["hgT1", "oneT", "hgLB"], ["hgT3"], bias=self.oneT[:], scale=LB[:, h:h + 1])
                    self.tt(t(3), t(3), t(2), ALU.subtract, ["hgT3", "hgT2"], ["hgT3"])
                    if d == 0:
                        self.scan(t(5), cm_f[:, :n], t(3), 0.0, ["hgT3", "hgcmk", "hgKK"], ["hgT5"])
                    else:
                        self.scan(rev(t(5), n), rev(cm_b[:, :n], n), rev(t(3), n), 0.0, ["hgT3", "hgcmk", "hgKK"], ["hgT5"])
                    self.act(EB[:, :n], t(5), AF.Exp, ["hgT5"], ["hgEB"])
                    self.tt(QA[:, :n], qs[:, t0:t0 + n], EB[:, :n], ALU.mult, [f"hgqs.{b}", "hgEB"], ["hgQA"])
                    self.tt(v3(t(2)), v3(t(5)), bc(t(5), CH // 2), ALU.subtract, ["hgT5"], ["hgT2"])
                    self.act(t(1), t(2), AF.Exp, ["hgT2"], ["hgT1"])
                    self.tt(QR[:, :n], qs[:, t0:t0 + n], t(1), ALU.mult, [f"hgqs.{b}", "hgT1"], ["hgQR"])
                    self.act(t(1), t(2), AF.Exp, ["hgT2", "hgQR"], ["hgT1"], scale=-1.0)
                    self.tt(KT[:, :n], KK[:, :n], t(1), ALU.mult, ["hgKK", "hgT1"], ["hgKT"])
                    self.tt(v3(t(2)), bc(t(5), lc), v3(t(5)), ALU.subtract, ["hgT5", "hgKT"], ["hgT2"])
                    self.act(t(2), t(2), AF.Exp, ["hgT2"], ["hgT2"])
                    self.tt(KH[:, :n], KK[:, :n], t(2), ALU.mult, ["hgKK", "hgT2", "hgT3"], ["hgT3"])

                blocks_ = order[d]
                part1(blocks_[0])
                for bi_, b in enumerate(blocks_):
                    t0, n, v = TB[b]
                    nch = n // CH
                    nxt_b = blocks_[bi_ + 1] if bi_ + 1 < len(blocks_) else None
                    part2(b)
                    clist = list(range(nch)) if d == 0 else list(range(nch - 1, -1, -1))
                    st1 = {}

                    def stage1a(ci):
                        c0 = ci * CH
                        gch = t0 // CH + ci
                        par = gch % 3
                        pS, pSk = self.ps()
                        self.mm(pS[0:CH, 0:CH], KT[:, c0:c0 + CH], QR[:, c0:c0 + CH], True, True, ["hgKT", "hgQR"], [pSk])
                        self.tt(PM[par][:], pS[0:CH, 0:CH], mask[d], ALU.mult, [pSk, "cf"], [f"hgPM{par}"])
                        pT, pTk = self.ps()
                        self.tr(pT[0:CH, 0:128], KH[:, c0:c0 + CH], self.ident, ["hgT3", "cf"], [pTk])
                        self.cp(KHt[par][:], pT[0:CH, 0:128], [pTk], [f"hgKHt{par}"], eng="act")

                    def stage1b(ci):
                        gch = t0 // CH + ci
                        par = gch % 3
                        pD, pDk = self.ps()
                        self.mm(pD[:, 0:128], KHt[par][:], vtm[:, gch, :], True, True, [f"hgKHt{par}", f"hgvtm.{b}"], [pDk])
                        st1[ci] = (pD, pDk)

                    def stage2(ci):
                        c0 = ci * CH
                        gch = t0 // CH + ci
                        par = gch % 3
                        pD, pDk = st1.pop(ci)
                        sp, sn = kst[0] % 2, (kst[0] + 1) % 2
                        kst[0] += 1
                        pO, pOk = self.ps()
                        self.mm(pO[:, 0:CH], St[sp][:], QA[:, c0:c0 + CH], True, False, [f"hgSt{sp}", "hgQA"], [pOk])
                        self.mm(pO[:, 0:CH], vtm[:, gch, :], PM[par][:], False, True, [f"hgvtm.{b}", f"hgPM{par}"], [pOk])
                        self.stt(St[sn][:], St[sp][:], EB[:, c0 + lc:c0 + lc + 1], pD[:, 0:128], ALU.mult, ALU.add,
                                 [f"hgSt{sp}", "hgEB", pDk], [f"hgSt{sn}"])
                        if d == 0:
                            self.cp(O[:, t0 + c0:t0 + c0 + CH], pO[:, 0:CH], [pOk], [f"hgO.{b}"], eng="act")
                        else:
                            self.tt(O[:, t0 + c0:t0 + c0 + CH], O[:, t0 + c0:t0 + c0 + CH], pO[:, 0:CH], ALU.add,
                                    [pOk, f"hgO.{b}"], [f"hgO.{b}"])

                    stage1a(clist[0])
                    if len(clist) > 1:
                        stage1a(clist[1])
                    stage1b(clist[0])
                    for i_, ci in enumerate(clist):
                        if i_ + 2 < len(clist):
                            stage1a(clist[i_ + 2])
                        if i_ + 1 < len(clist):
                            stage1b(clist[i_ + 1])
                        stage2(ci)
                        if i_ == 3 and nxt_b is not None:
                            part1(nxt_b)
            if h == 3:
                self.dump(f"hgqs{e}", qs[:], [128, NT], BF16, [f"hgqs.{b}" for b in range(5)])
                self.dump(f"hgO{e}", O[:], [128, NT], F32, [f"hgO.{b}" for b in range(5)])
                self.dump(f"hgvtm{e}", vtm[:], [32, 72, 128], BF16, [f"hgvtm.{b}" for b in range(5)])
                self.dump(f"hgEB{e}", EB[:], [128, 512], F32, ["hgEB"])
                self.dump(f"hgKK{e}", KK[:], [128, 512], F32, ["hgKK"])
                self.dump(f"hgT5{e}", T[5][:], [128, 512], F32, ["hgT5"])
                self.dump(f"hgLB{e}", LB[:], [128, 4], F32, ["hgLB"])
            wgt, wgtk = wl(4)
            for b, (t0, n, v) in enumerate(TB):
                self.act(sqb[:, :n], O[:, t0:t0 + n], AF.Square, [f"hgO.{b}"], ["hgQR"])
                pR, pRk = self.ps()
                self.mm(pR[:, :n], self.onesb[:], sqb[:, :n], True, True, ["onesb", "hgQR"], [pRk])
                self.act(T[1][:, :n], pR[:, :n], AF.Ln, [pRk, "epsT"], ["hgT1"], bias=self.epsT[:], scale=1.0 / 128.0)
                self.act(T[1][:, :n], T[1][:, :n], AF.Exp, ["hgT1"], ["hgT1"], scale=-0.5)
                pg, pgk = self.ps()
                for kc in range(8):
                    self.mm(pg[:, :n], wgt[:, kc, :], self.A[:, kc, t0:t0 + n], kc == 0, kc == 7, [wgtk, f"A{kc}.{b}"], [pgk])
                self.act(T[2][:, :n], pg[:, :n], AF.Silu, [pgk], ["hgT2"])
                self.tt(T[1][:, :n], T[1][:, :n], O[:, t0:t0 + n], ALU.mult, ["hgT1", f"hgO.{b}"], ["hgT1"])
                self.tt(T[1][:, :n], T[1][:, :n], T[2][:, :n], ALU.mult, ["hgT1", "hgT2"], ["hgT1"])
                self.act(self.Mb[:, h, t0:t0 + n], T[1][:, :n], AF.Identity, ["hgT1", "eprm"], [f"M{4 + h}.{b}"],
                         scale=prm[:, PE_ON:PE_ON + 1])

    def norm_rope(self, pq, pqk, rows, gm, gmk, inv_dim, gain, t0, n, is_x, dest, destk, tm, tag="", alt=False):
        sq, rs, qn, qb, t1, t2, cosT, sinT = tm
        ksq, krs = f"nr{tag}_sq", f"nr{tag}_rs"
        if alt:
            assert not is_x
            sq, rs, ksq, krs = qb, t1, f"nr{tag}_qb", f"nr{tag}_t1"
        self.act(sq[:rows, :n], pq, AF.Square, [pqk], [ksq])
        pn, pnk = self.ps()
        self.mm(pn[:rows, :n], gm, sq[:rows, :n], True, True, [gmk, ksq], [pnk])
        self.act(rs[:rows, :n], pn[:rows, :n], AF.Ln, [pnk, "epsT"], [krs], bias=self.epsT[:rows, :], scale=inv_dim)
        self.act(rs[:rows, :n], rs[:rows, :n], AF.Exp, [krs], [krs], scale=-0.5)
        if not is_x:
            self.stt(dest, pq, gain, rs[:rows, :n], ALU.mult, ALU.mult, [pqk, krs, "oprm"], destk)
            return
        self.stt(qn[:rows, :n], pq, gain, rs[:rows, :n], ALU.mult, ALU.mult, [pqk, krs, "oprm"], [f"nr{tag}_qn"])
        self.cp(qb[:rows, :n], qn[:rows, :n], [f"nr{tag}_qn"], [f"nr{tag}_qb"], eng="act")
        pr, prk = self.ps()
        self.mm(pr[:rows, :n], self.permb[:rows, :rows], qb[:rows, :n], True, True, ["permb", f"nr{tag}_qb"], [prk])
        x0 = t0 - CTX
        self.dma(cosT[:rows, :n], self.d_cos[0:rows, x0:x0 + n], (), [f"nr{tag}_cos"])
        self.dma(sinT[:rows, :n], self.d_sin[0:rows, x0:x0 + n], (), [f"nr{tag}_sin"])
        self.tt(t1[:rows, :n], qn[:rows, :n], cosT[:rows, :n], ALU.mult, [f"nr{tag}_qn", f"nr{tag}_cos"], [f"nr{tag}_t1"])
        self.tt(t2[:rows, :n], pr[:rows, :n], sinT[:rows, :n], ALU.mult, [prk, f"nr{tag}_sin"], [f"nr{tag}_t2"])
        self.tt(dest, t1[:rows, :n], t2[:rows, :n], ALU.add, [f"nr{tag}_t1", f"nr{tag}_t2"], destk)

    def odd_mixer(self, o, l, es):
        lam_init = 0.8 - 0.6 * math.exp(-0.3 * l)
        prm = self.sb(es, f"oprm{o}", [128, PO_N])
        self.dma(prm[:], self.d_po[o], (), ["oprm"])
        self.permb = self.sb(es, f"permb{o}", [128, 128], BF16)
        self.dma(self.permb[:], self.d_perm, (), ["permb"], q="pool")
        self.bdb = self.sb(es, f"bdb{o}", [128, 128], BF16)
        self.cp(self.bdb[:], self.cf[:, CF_BD:CF_BD + 128], ["cf"], ["bdb"])
        S0 = lambda name, shape, dt=F32: self.sb(es, f"od{name}{o}", shape, dt)
        lp = S0("lp", [128, 2])
        nlam = S0("nlam", [128, 1])
        subg = S0("subg", [128, 1])
        with self.scope() as esl:
            onesf = self.sb(esl, f"onesf{o}", [128, 128])
            self.memset(onesf[:], 1.0, ["onesf"])
            self.memset(lp[:], 0.0, ["odlp"])
            self.tt(lp[0:64, 0:1], prm[0:64, PO_LAM:PO_LAM + 1], prm[0:64, PO_LAM + 1:PO_LAM + 2], ALU.mult, ["oprm", "odlp"], ["odlp"])
            self.tt(lp[0:64, 1:2], prm[0:64, PO_LAM + 2:PO_LAM + 3], prm[0:64, PO_LAM + 3:PO_LAM + 4], ALU.mult, ["oprm", "odlp"], ["odlp"])
            pl, plk = self.ps()
            self.mm(pl[:, 0:2], onesf[:], lp[:], True, True, ["onesf", "odlp"], [plk])
            self.act(lp[:], pl[:, 0:2], AF.Exp, [plk], ["odlp"])
            self.tt(nlam[:], lp[:, 1:2], lp[:, 0:1], ALU.subtract, ["odlp"], ["odnlam"])
            self.ts(nlam[:], nlam[:], -lam_init, None, ALU.add, None, ["odnlam"], ["odnlam"])
            self.ts(subg[:], prm[:, PO_SUB:PO_SUB + 1], 1.0 - lam_init, None, ALU.mult, None, ["oprm"], ["odsubg"])
        tm = (S0("sq", [128, 512], BF16), S0("rs", [128, 512]), S0("qn", [128, 512]), S0("qb", [128, 512], BF16),
              S0("t1", [128, 512]), S0("t2", [128, 512]), S0("cosT", [128, 512]), S0("sinT", [128, 512]))
        Pt = [S0(f"P{i}", [128, 512], BF16) for i in range(4)]
        orec = S0("orec", [128, 512])
        oacc = S0("oacc", [128, 512])

        def attend(qblk, ktiles, score_fn, v_fn, nacc, scale, finish, zacc=None):
            t0, n, v = TB[qblk]
            accs = [(self.psum[2 * i], f"ps{2 * i}", self.psum[2 * i + 1], f"ps{2 * i + 1}") for i in range(nacc)]
            self.ps_avail = list(range(2 * nacc, 8))
            if zacc is not None:
                self.ps_avail = [1] + self.ps_avail
            nk = len(ktiles)
            depth = 1 if nacc == 2 else 3
            sc_ = {}

            def do_scores(ki):
                for i in range(nacc):
                    pS, pSk = self.ps()
                    score_fn(i, pS, pSk, ktiles[ki], t0, n)
                    sc_[(ki, i)] = (pS, pSk)

            for ki in range(min(depth, nk)):
                do_scores(ki)
            for ki, kt in enumerate(ktiles):
                if ki + depth < nk:
                    do_scores(ki + depth)
                for i in range(nacc):
                    pS, pSk = sc_.pop((ki, i))
                    P = Pt[(nacc * ki + i) % 4]
                    Pk = f"odP{(nacc * ki + i) % 4}"
                    self.act(P[:, :n], pS[:, :n], AF.Exp, [pSk], [Pk], scale=scale)
                    O_, Ok, Z_, Zk = accs[i]
                    vl, vk = v_fn(kt)
                    self.mm(O_[:, :n], vl, P[:, :n], ki == 0, ki == nk - 1, [vk, Pk], [Ok])
                    if zacc is None or i != 0:
                        self.mm(Z_[:, :n], self.onesb[:], P[:, :n], ki == 0, ki == nk - 1, ["onesb", Pk], [Zk])
                    else:
                        Zf, _ = zacc
                        if ki == 0:
                            self.cp(Zf[:, :n], P[:, :n], [Pk], ["odZf"])
                        else:
                            self.tt(Zf[:, :n], Zf[:, :n], P[:, :n], ALU.add, ["odZf", Pk], ["odZf"])
            if zacc is not None:
                Zf, ones_f = zacc
                O_, Ok, Z_, Zk = accs[0]
                self.mm(Z_[:, :n], ones_f[:], Zf[:, :n], True, True, ["odonesf", "odZf"], [Zk])
            finish(accs, t0, n, qblk)
            self.ps_avail = list(range(8))

        with self.scope() as es2:
            self.Ma = self.sb(es2, f"Ma{self.uid}", [128, 4, NT], BF16)
            self.uid += 1
            S = lambda name, shape, dt=F32: self.sb(es2, f"df{name}{o}", shape, dt)
            QD = S("QD", [128, NT], BF16)
            KD = S("KD", [128, NT], BF16)
            Vt = S("Vt", [128, 18, 128], BF16)
            sqo = S("sqo", [128, 512], BF16)
            tmB = (S("sqB", [128, 512], BF16), S("rsB", [128, 512]), S("qnB", [128, 512]), S("qbB", [128, 512], BF16),
                   S("t1B", [128, 512]), S("t2B", [128, 512]), S("cosB", [128, 512]), S("sinB", [128, 512]))
            tms = [(tm, ""), (tmB, "B")]
            Zf = S("Zf", [128, 512])
            ones_f = S("onesf2", [128, 128])
            self.memset(ones_f[:], 1.0, ["odonesf"])
            zacc = (Zf, ones_f)
            ncall = [0]
            for h in range(4):
                for which, col0, dst, dk, gcol in ((0, 0, QD, "dfQD", PO_QG), (1, 512, KD, "dfKD", PO_KG)):
                    w, wk = self.wload(self.d_cd_w_in[o][:, col0 + 128 * h:col0 + 128 * h + 128], 8, 128)

                    def evac(pst, pk, b, t0, n, v, dst=dst, dk=dk, gcol=gcol):
                        tmx, tagx = tms[ncall[0] % 2]
                        ncall[0] += 1
                        self.norm_rope(pst[:, :n], pk, 128, self.bdb[:], "bdb", 1.0 / 64.0, prm[:, gcol:gcol + 1], t0, n, v == 0,
                                       dst[:, t0:t0 + n], [f"{dk}.{b}"], tmx, tagx)
                    self.proj_chunk(w, wk, 0, 128, evac)
                wv, wvk = self.wload(self.d_cd_w_in[o][:, 1024 + 128 * h:1024 + 128 * h + 128], 8, 128)
                for tt_ in range(18):
                    b = 0 if tt_ < 2 else 1 + (tt_ - 2) // 4
                    pv_, pvk = self.ps()
                    for kc in range(8):
                        self.mm(pv_[:, 0:128], self.A[:, kc, tt_ * 128:(tt_ + 1) * 128], wv[:, kc, :], kc == 0, kc == 7,
                                [wvk, f"A{kc}.{b}"], [pvk])
                    self.cp(Vt[:, tt_, :], pv_[:, 0:128], [pvk], [f"dfVt.{tt_}"], eng="act")

                def score(i, pS, pSk, kt, t0, n):
                    bq = [b for b, tb in enumerate(TB) if tb[0] == t0][0]
                    bk = 0 if kt < 2 else 1 + (kt - 2) // 4
                    self.mm(pS[:, :n], KD[64 * i:64 * i + 64, kt * 128:(kt + 1) * 128], QD[64 * i:64 * i + 64, t0:t0 + n], True, True,
                            [f"dfKD.{bk}", f"dfQD.{bq}"], [pSk])

                def vfn(kt):
                    return Vt[:, kt, :], f"dfVt.{kt}"

                def finish(accs, t0, n, qblk, h=h):
                    (O1, O1k, Z1, Z1k), (O2, O2k, Z2, Z2k) = accs
                    r2 = tm[4]
                    self.act(orec[:, :n], Z1[:, :n], AF.Ln, [Z1k], ["odorec"])
                    self.act(orec[:, :n], orec[:, :n], AF.Exp, ["odorec"], ["odorec"], scale=-1.0)
                    self.act(r2[:, :n], Z2[:, :n], AF.Ln, [Z2k], ["nr_t1"])
                    self.act(r2[:, :n], r2[:, :n], AF.Exp, ["nr_t1"], ["nr_t1"], scale=-1.0)
                    self.tt(oacc[:, :n], O1[:, :n], orec[:, :n], ALU.mult, [O1k, "odorec"], ["odoacc"])
                    self.tt(orec[:, :n], O2[:, :n], r2[:, :n], ALU.mult, [O2k, "nr_t1", "odoacc"], ["odorec"])
                    self.stt(oacc[:, :n], orec[:, :n], nlam[:, 0:1], oacc[:, :n], ALU.mult, ALU.add, ["odorec", "odnlam", "odoacc"], ["odoacc"])
                    self.act(sqo[:, :n], oacc[:, :n], AF.Square, ["odoacc"], ["dfsqo"])
                    pn, pnk = self.ps()
                    self.mm(pn[:, :n], self.onesb[:], sqo[:, :n], True, True, ["onesb", "dfsqo"], [pnk])
                    self.act(orec[:, :n], pn[:, :n], AF.Ln, [pnk, "epsT"], ["odorec"], bias=self.epsT[:], scale=1.0 / 128.0)
                    self.act(orec[:, :n], orec[:, :n], AF.Exp, ["odorec"], ["odorec"], scale=-0.5)
                    self.stt(self.Ma[:, h, t0:t0 + n], oacc[:, :n], subg[:, 0:1], orec[:, :n], ALU.mult, ALU.mult,
                             ["odoacc", "odsubg", "odorec"], [f"M{h}.{qblk}"])
                if not self.skip_ctx:
                    attend(0, [0, 1], score, vfn, 2, 0.125, finish, zacc)
                for qblk in range(1, 5):
                    attend(qblk, list(range(18)), score, vfn, 2, 0.125, finish, zacc)
            self.dump(f"Mo{o}a", self.Ma[:], [128, 4, NT], BF16, [f"M{c}.{b}" for c in range(4) for b in range(5)])
            self.out_proj(l, [0, 1, 2, 3], lambda kc: self.Ma[:, kc, :])

        with self.scope() as es2:
            S = lambda name, shape, dt=F32: self.sb(es2, f"ml{name}{o}", shape, dt)
            CQn = S("CQn", [128, 3, NT], BF16)
            CKVn = S("CKVn", [128, 2, NT], BF16)
            KR = S("KR", [64, NT], BF16)
            Mh = S("Mh", [128, NT], BF16)
            VMh = S("VMh", [128, 18, 128], BF16)
            QN = S("QN", [128, NT], BF16)
            QR = S("QR", [64, NT], BF16)
            KN = S("KN", [128, NT], BF16)
            rawt = [tm[2], tm[4], tm[5]]
            rawk = ["nr_qn", "nr_t1", "nr_t2"]
            sq3, rs3 = tm[0], tm[1]
            for (col0, nch, dst, dk, gofs) in ((1536, 3, CQn, "mlCQn", PO_QA), (1920, 2, CKVn, "mlCKVn", PO_KVA)):
                w, wk = self.wload(self.d_cd_w_in[o][:, col0:col0 + nch * 128], 8, nch * 128)
                for b, (t0, n, v) in enumerate(TB):
                    pn, pnk = self.ps()
                    for c in range(nch):
                        pst, pk = self.ps()
                        for kc in range(8):
                            self.mm(pst[:, :n], w[:, kc, c * 128:(c + 1) * 128], self.A[:, kc, t0:t0 + n], kc == 0, kc == 7,
                                    [wk, f"A{kc}.{b}"], [pk])
                        self.cp(rawt[c][:, :n], pst[:, :n], [pk], [rawk[c]], eng="act")
                        self.act(sq3[:, :n], pst[:, :n], AF.Square, [pk], ["nr_sq"])
                        self.mm(pn[:, :n], self.onesb[:], sq3[:, :n], c == 0, c == nch - 1, ["onesb", "nr_sq"], [pnk])
                    self.act(rs3[:, :n], pn[:, :n], AF.Ln, [pnk, "epsT"], ["nr_rs"], bias=self.epsT[:], scale=1.0 / (nch * 128.0))
                    self.act(rs3[:, :n], rs3[:, :n], AF.Exp, ["nr_rs"], ["nr_rs"], scale=-0.5)
                    for c in range(nch):
                        self.stt(dst[:, c, t0:t0 + n], rawt[c][:, :n], prm[:, gofs + c:gofs + c + 1], rs3[:, :n], ALU.mult, ALU.mult,
                                 [rawk[c], "oprm", "nr_rs"], [f"{dk}{c}.{b}"])
            w, wk = self.wload(self.d_cd_w_in[o][:, 2176:2240], 8, 64)

            def evac_kr(pst, pk, b, t0, n, v):
                self.norm_rope(pst[0:64, :n], pk, 64, self.onesb[0:64, 0:64], "onesb", 1.0 / 64.0, prm[0:64, PO_RK:PO_RK + 1], t0, n,
                               v == 0, KR[:, t0:t0 + n], [f"mlKR.{b}"], tm)
            self.proj_chunk(w, wk, 0, 64, evac_kr)
            mscale = 192.0 ** -0.5
            for h in range(4):
                wuq, wuqk = self.wload(self.d_w_uq[o], 3, 768)
                wukv, wukvk = self.wload(self.d_w_ukv[o], 2, 1024)
                for b, (t0, n, v) in enumerate(TB):
                    pq, pqk = self.ps()
                    for kc in range(3):
                        self.mm(pq[:, :n], wuq[:, kc, h * 192:h * 192 + 128], CQn[:, kc, t0:t0 + n], kc == 0, kc == 2,
                                [wuqk, f"mlCQn{kc}.{b}"], [pqk])
                    self.norm_rope(pq[:, :n], pqk, 128, self.onesb[:], "onesb", 1.0 / 128.0, prm[:, PO_NQ:PO_NQ + 1], t0, n, False,
                                   QN[:, t0:t0 + n], [f"mlQN.{b}"], tm)
                    pk_, pkk_ = self.ps()
                    for kc in range(2):
                        self.mm(pk_[:, :n], wukv[:, kc, h * 256:h * 256 + 128], CKVn[:, kc, t0:t0 + n], kc == 0, kc == 1,
                                [wukvk, f"mlCKVn{kc}.{b}"], [pkk_])
                    self.norm_rope(pk_[:, :n], pkk_, 128, self.onesb[:], "onesb", 1.0 / 128.0, prm[:, PO_NK:PO_NK + 1], t0, n, False,
                                   KN[:, t0:t0 + n], [f"mlKN.{b}"], tm, "", True)
                    pr_, prk_ = self.ps()
                    for kc in range(3):
                        self.mm(pr_[0:64, :n], wuq[:, kc, h * 192 + 128:h * 192 + 192], CQn[:, kc, t0:t0 + n], kc == 0, kc == 2,
                                [wuqk, f"mlCQn{kc}.{b}"], [prk_])
                    self.norm_rope(pr_[0:64, :n], prk_, 64, self.onesb[0:64, 0:64], "onesb", 1.0 / 64.0, prm[0:64, PO_RQ:PO_RQ + 1],
                                   t0, n, v == 0, QR[:, t0:t0 + n], [f"mlQR.{b}"], tm)
                for tt_ in range(18):
                    b = 0 if tt_ < 2 else 1 + (tt_ - 2) // 4
                    pv_, pvk = self.ps()
                    for kc in range(2):
                        self.mm(pv_[:, 0:128], CKVn[:, kc, tt_ * 128:(tt_ + 1) * 128], wukv[:, kc, h * 256 + 128:h * 256 + 256],
                                kc == 0, kc == 1, [wukvk, f"mlCKVn{kc}.{b}"], [pvk])
                    self.cp(VMh[:, tt_, :], pv_[:, 0:128], [pvk], [f"mlVM.{tt_}"], eng="act")

                def score(i, pS, pSk, kt, t0, n):
                    bq = [b for b, tb in enumerate(TB) if tb[0] == t0][0]
                    bk = 0 if kt < 2 else 1 + (kt - 2) // 4
                    self.mm(pS[:, :n], KN[:, kt * 128:(kt + 1) * 128], QN[:, t0:t0 + n], True, False, [f"mlKN.{bk}", f"mlQN.{bq}"], [pSk])
                    self.mm(pS[:, :n], KR[:, kt * 128:(kt + 1) * 128], QR[:, t0:t0 + n], False, True, [f"mlKR.{bk}", f"mlQR.{bq}"], [pSk])

                def vfn(kt):
                    return VMh[:, kt, :], f"mlVM.{kt}"

                def finish(accs, t0, n, qblk, h=h):
                    ((O1, O1k, Z1, Z1k),) = accs
                    self.act(orec[:, :n], Z1[:, :n], AF.Ln, [Z1k], ["odorec"])
                    self.act(orec[:, :n], orec[:, :n], AF.Exp, ["odorec"], ["odorec"], scale=-1.0)
                    self.tt(Mh[:, t0:t0 + n], O1[:, :n], orec[:, :n], ALU.mult, [O1k, "odorec"], [f"mlMh.{qblk}"])
                if not self.skip_ctx:
                    attend(0, [0, 1], score, vfn, 1, mscale, finish)
                for qblk in range(1, 5):
                    attend(qblk, list(range(18)), score, vfn, 1, mscale, finish)
                self.dump(f"Mo{o}b{h}", Mh[:], [128, NT], BF16, [f"mlMh.{b}" for b in range(5)])
                self.out_proj(l, [4 + h], lambda kc: Mh[:, :], keyf=lambda kc, b: f"mlMh.{b}")


def make_in_maps(inp, batches):
    cf, c2 = host_consts()
    cos2, sin2, perm = host_rope()
    pv, pe, bt, po = pack_inputs(inp)
    f = lambda a: np.ascontiguousarray(np.asarray(a, np.float32))
    shared = {
        "cf": cf, "c2": c2, "ropecos": cos2, "ropesin": sin2, "ropeperm": perm, "pv": pv, "pe": pe, "bt": bt, "po": po,
        "ada_w": f(inp["ada_w"]), "w_out": f(inp["w_out"]), "ffn_w_in": f(inp["ffn_w_in"]), "ffn_w_out": f(inp["ffn_w_out"]),
        "ab_w_in": f(inp["ab_w_in"]), "s5_glu_w": f(inp["s5_glu_w"]), "cd_w_in": f(inp["cd_w_in"]),
        "mla_w_uq": f(inp["mla_w_uq"]), "mla_w_ukv": f(inp["mla_w_ukv"]),
    }
    maps = []
    cc = colmaj(f(inp["c_ctx"]), 8)
    for b in batches:
        m = dict(shared)
        m["h0"] = np.ascontiguousarray(np.concatenate([f(inp["ctx"])[b].T, f(inp["x"])[b].T], axis=1))
        m["sc"] = np.ascontiguousarray(np.stack([colmaj(f(inp["c"])[b], 8), cc], axis=-1))
        maps.append(m)
    return maps


def kernel(**inputs):
    nb = 8
    b = Builder(list(range(DEPTH)))
    nc = b.build()
    maps = make_in_maps(inputs, list(range(nb)))
    res = run_bass_kernel_spmd(nc, maps, core_ids=list(range(nb)))
    out = np.stack([np.asarray(res.results[i]["out"], np.float32).T for i in range(nb)], axis=0)
    return np.ascontiguousarray(out)
```

```python
from contextlib import ExitStack
import math
import numpy as np
import concourse.bass as bass
import concourse.mybir as mybir
from concourse.bass_utils import run_bass_kernel_spmd

F32 = mybir.dt.float32
BF16 = mybir.dt.bfloat16
I32 = mybir.dt.int32
ALU = mybir.AluOpType
AF = mybir.ActivationFunctionType

D = 1024
DEPTH = 4
NT = 2304
CTX = 256
SEQ = 2048
FFH = 2816
EPS = 1e-6
TB = [(0, 256, 1), (256, 512, 0), (768, 512, 0), (1280, 512, 0), (1792, 512, 0)]
TWO_PI = 2.0 * math.pi

ENGS = ("pe", "dve", "act", "pool", "sp")
NDSEM = 12
FENCE_DMA = True
FENCE_ON = True
PREFETCH_ADA = True
SKIP = None
LIM = {}


class Op:
    __slots__ = ("eng", "fn", "deps", "sig", "cnt", "dma", "dsem", "dval", "idx")


class KB:
    def __init__(self, nc):
        self.nc = nc
        self.ops = []
        self.last_w = {}
        self.readers = {}
        self.known = {e: {} for e in ENGS}
        self.dma_cnt = {e: 0 for e in ENGS}
        self.dma_last = {}
        self.dma_n = {}
        self.pending = {e: [] for e in ENGS}
        self.last_op = {}

    def fence(self):
        toks = list(self.last_op.values()) + (list(self.dma_last.values()) if FENCE_DMA else [])
        for e in ENGS:
            self.pending[e] = list(toks)

    def _need(self, eng, tok, deps):
        if tok is None:
            return
        src = self.ops[tok]
        if src.dma:
            key = ("d", src.eng, src.dsem)
        else:
            key = src.eng
            if src.eng == "pe" and eng == "pe":
                return
        if self.known[eng].get(key, -1) >= tok:
            return
        self.known[eng][key] = tok
        deps.append(tok)

    def op(self, eng, fn, R=(), W=(), dma=False):
        o = Op()
        o.eng, o.fn, o.dma, o.sig, o.cnt = eng, fn, dma, False, 0
        o.idx = len(self.ops)
        deps = []
        if self.pending[eng]:
            for t in self.pending[eng]:
                self._need(eng, t, deps)
            self.pending[eng] = []
        for r in R:
            self._need(eng, self.last_w.get(r), deps)
            if isinstance(r, str) and r.startswith("ps"):
                for k, t in self.readers.get(r, {}).items():
                    if k != eng:
                        self._need(eng, t, deps)
        for w in W:
            self._need(eng, self.last_w.get(w), deps)
            for t in self.readers.get(w, {}).values():
                self._need(eng, t, deps)
        if dma:
            k = self.dma_cnt[eng] % NDSEM
            self.dma_cnt[eng] += 1
            o.dsem = k
            prev = self.dma_last.get((eng, k))
            if prev is not None:
                self._need(eng, prev, deps)
            self.dma_last[(eng, k)] = o.idx
            self.dma_n[(eng, k)] = self.dma_n.get((eng, k), 0) + 1
            o.dval = 16 * self.dma_n[(eng, k)]
        o.deps = deps
        self.ops.append(o)
        if not dma:
            self.last_op[eng] = o.idx
        for r in R:
            self.readers.setdefault(r, {})[eng if not dma else ("d", o.idx)] = o.idx
        for w in W:
            self.last_w[w] = o.idx
            self.readers[w] = {}
        return o.idx

    def emit(self, final_wait_ops=()):
        nc = self.nc
        for o in self.ops:
            for d in o.deps:
                self.ops[d].sig = True
        cnt = {e: 0 for e in ENGS}
        for o in self.ops:
            if o.dma:
                continue
            if o.sig:
                cnt[o.eng] += 1
            o.cnt = cnt[o.eng]
        engobj = {"pe": nc.tensor, "dve": nc.vector, "act": nc.scalar, "pool": nc.gpsimd, "sp": nc.sync}
        with ExitStack() as es:
            sem = {e: es.enter_context(nc.semaphore("s_" + e)) for e in ENGS}
            dsem = {}
            for e in ("sp", "act", "pool"):
                for k in range(NDSEM):
                    dsem[(e, k)] = es.enter_context(nc.semaphore(f"d_{e}{k}"))
            per = {e: [] for e in ENGS}
            for o in self.ops:
                per[o.eng].append(o)
            fin = [self.ops[i] for i in final_wait_ops]

            def run(e):
                eo = engobj[e]
                for o in per[e]:
                    for d in o.deps:
                        s = self.ops[d]
                        if s.dma:
                            eo.wait_ge(dsem[(s.eng, s.dsem)], s.dval)
                        else:
                            eo.wait_ge(sem[s.eng], s.cnt)
                    ins = o.fn(eo)
                    if o.dma:
                        ins.then_inc(dsem[(o.eng, o.dsem)], 16)
                    elif o.sig:
                        ins.then_inc(sem[o.eng], 1)
                if e == "sp":
                    for s in fin:
                        eo.wait_ge(dsem[(s.eng, s.dsem)], s.dval)

            with nc.Block() as block:
                @block.tensor
                def _(t):
                    run("pe")

                @block.vector
                def _(v):
                    run("dve")

                @block.scalar
                def _(s):
                    run("act")

                @block.gpsimd
                def _(g):
                    run("pool")

                @block.sync
                def _(s):
                    run("sp")


def rev(ap2d, n):
    a = [list(x) for x in ap2d.ap]
    assert len(a) == 2 and a[1][1] == n
    return bass.AP(ap2d.tensor, ap2d.offset + a[1][0] * (n - 1), [a[0], [-a[1][0], n]])


CF_ID, CF_J, CF_MF, CF_MB, CF_MLO, CF_MHI, CF_SGN, CF_BD, CF_N = (0, 128, 256, 320, 384, 385, 386, 387, 515)
C2_IOTA, C2_CMASK, C2_CMASKB, C2_N = 0, 512, 1024, 1536


def host_consts():
    cf = np.zeros((128, CF_N), np.float32)
    cf[:, CF_ID:CF_ID + 128] = np.eye(128, dtype=np.float32)
    J = np.zeros((128, 128), np.float32)
    for p in range(64):
        J[p, p + 64] = 1.0
        J[p + 64, p] = 1.0
    cf[:, CF_J:CF_J + 128] = J
    c2 = np.zeros((128, C2_N), np.float32)
    c2[:, C2_IOTA:C2_IOTA + 512] = np.arange(512, dtype=np.float32)[None, :]
    cm = np.ones(512, np.float32)
    cm[::32] = 0.0
    c2[:, C2_CMASK:C2_CMASK + 512] = cm[None, :]
    cmb = np.ones(512, np.float32)
    cmb[31::32] = 0.0
    c2[:, C2_CMASKB:C2_CMASKB + 512] = cmb[None, :]
    s = np.arange(64)
    cf[:64, CF_MF:CF_MF + 64] = (s[:, None] <= s[None, :]).astype(np.float32)
    cf[:64, CF_MB:CF_MB + 64] = (s[:, None] >= s[None, :]).astype(np.float32)
    cf[:64, CF_MLO] = 1.0
    cf[64:, CF_MHI] = 1.0
    cf[:64, CF_SGN] = 1.0
    cf[64:, CF_SGN] = -1.0
    bd = np.zeros((128, 128), np.float32)
    bd[:64, :64] = 1.0
    bd[64:, 64:] = 1.0
    cf[:, CF_BD:CF_BD + 128] = bd
    return cf, c2


ROPE_DIM = 64


def host_rope():
    n_freq = ROPE_DIM // 4
    inv = np.power(np.float32(10000.0), -np.arange(n_freq, dtype=np.float32) / np.float32(n_freq)).astype(np.float32)
    t = np.arange(SEQ)
    rows = (t // 64).astype(np.float32)
    cols = (t % 64).astype(np.float32)
    ang_r = rows[:, None] * inv[None, :]
    ang_c = cols[:, None] * inv[None, :]
    ang = np.concatenate([ang_r, ang_r, ang_c, ang_c], axis=-1).astype(np.float32)
    cos = np.cos(ang).astype(np.float32).T
    sin = np.sin(ang).astype(np.float32).T
    sgn = np.ones((64, 1), np.float32)
    sgn[0:16] = -1.0
    sgn[32:48] = -1.0
    sins = sin * sgn
    cos2 = np.concatenate([cos, cos], 0)
    sin2 = np.concatenate([sins, sins], 0)
    perm = np.zeros((128, 128), np.float32)
    for base in (0, 64):
        for m in range(64):
            seg = m // 32
            r = m % 32
            k = seg * 32 + (r + 16) % 32
            perm[base + k, base + m] = 1.0
    return np.ascontiguousarray(cos2), np.ascontiguousarray(sin2), perm


PV_ADAB, PV_NM, PV_NF, PV_N = 0, 48, 56, 64
PE_SD, PE_GB, PE_LBL, PE_ON, PE_LRE, PE_LIM, PE_LST, PE_CR, PE_CI, PE_N = 0, 4, 8, 16, 17, 81, 145, 209, 1233, 2257
PO_LAM, PO_QG, PO_KG, PO_SUB, PO_QA, PO_KVA, PO_NQ, PO_NK, PO_RQ, PO_RK, PO_N = 0, 4, 5, 6, 7, 10, 12, 13, 14, 15, 16


def colmaj(v, nch):
    return np.ascontiguousarray(np.asarray(v, np.float32).reshape(nch, 128).T)


def pack_inputs(inp):
    f = lambda a: np.asarray(a, np.float32)
    pv = np.zeros((DEPTH, 128, PV_N), np.float32)
    for l in range(DEPTH):
        pv[l, :, PV_ADAB:PV_ADAB + 48] = colmaj(f(inp["ada_b"])[l], 48)
        pv[l, :, PV_NM:PV_NM + 8] = colmaj(f(inp["norm_mix"])[l], 8)
        pv[l, :, PV_NF:PV_NF + 8] = colmaj(f(inp["norm_ffn"])[l], 8)
    ne = 2
    pe = np.zeros((ne, 128, PE_N), np.float32)
    bt = np.zeros((ne, 2, 32, 16, 128), np.float32)
    dup = lambda a: np.concatenate([a, a], 0)
    for e in range(ne):
        pe[e, :, PE_SD:PE_SD + 4] = colmaj(f(inp["s5_d"])[e], 4)
        pe[e, :, PE_GB:PE_GB + 4] = colmaj(f(inp["s5_glu_b"])[e], 4)
        for e2 in range(ne):
            pe[e, :, PE_LBL + 4 * e2:PE_LBL + 4 * e2 + 4] = colmaj(f(inp["hgrn_lb_logits"])[e2], 4)
        pe[e, :, PE_ON] = f(inp["hgrn_out_norm"])[e]
        for d in range(2):
            pe[e, :, PE_LRE + 32 * d:PE_LRE + 32 * d + 32] = dup(f(inp["s5_lambda_re"])[e, d].T)
            pe[e, :, PE_LIM + 32 * d:PE_LIM + 32 * d + 32] = dup(f(inp["s5_lambda_im"])[e, d].T)
            pe[e, :, PE_LST + 32 * d:PE_LST + 32 * d + 32] = f(inp["s5_log_step"])[e, d][None, :]
            cr = f(inp["s5_c_re"])[e, d]
            ci = f(inp["s5_c_im"])[e, d]
            crt = dup(cr.transpose(2, 0, 1).reshape(64, 32 * 16))
            cit = dup(ci.transpose(2, 0, 1).reshape(64, 32 * 16))
            pe[e, :, PE_CR + 512 * d:PE_CR + 512 * d + 512] = crt
            pe[e, :, PE_CI + 512 * d:PE_CI + 512 * d + 512] = cit
            br = f(inp["s5_b_re"])[e, d]
            bi = f(inp["s5_b_im"])[e, d]
            bt[e, d, :, :, 0:64] = br.transpose(0, 2, 1)
            bt[e, d, :, :, 64:128] = bi.transpose(0, 2, 1)
    no = 2
    po = np.zeros((no, 128, PO_N), np.float32)
    for o in range(no):
        po[o, :64, PO_LAM:PO_LAM + 4] = f(inp["diff_lambda"])[o].T
        po[o, :, PO_QG] = np.tile(f(inp["diff_qk_norm"])[o, 0], 2)
        po[o, :, PO_KG] = np.tile(f(inp["diff_qk_norm"])[o, 1], 2)
        po[o, :, PO_SUB] = f(inp["diff_subln"])[o]
        po[o, :, PO_QA:PO_QA + 3] = colmaj(f(inp["mla_q_a_norm"])[o], 3)
        po[o, :, PO_KVA:PO_KVA + 2] = colmaj(f(inp["mla_kv_a_norm"])[o], 2)
        po[o, :, PO_NQ] = f(inp["mla_nope_norm"])[o, 0]
        po[o, :, PO_NK] = f(inp["mla_nope_norm"])[o, 1]
        po[o, :64, PO_RQ] = f(inp["mla_rope_norm"])[o, 0]
        po[o, :64, PO_RK] = f(inp["mla_rope_norm"])[o, 1]
    return pv, pe, bt, po


class Builder:
    def __init__(self, layers, dbg=None):
        self.layers = layers
        self.dbg = dbg
        nc = bass.Bass("TRN2", target_bir_lowering=False)
        self.nc = nc
        self.kb = KB(nc)
        dt = lambda name, shape, kind="ExternalInput", dtype=F32: nc.dram_tensor(name, list(shape), dtype, kind=kind).ap()
        self.d_h0 = dt("h0", [D, NT])
        self.d_sc = dt("sc", [128, 8, 2])
        self.d_cf = dt("cf", [128, CF_N])
        self.d_c2 = dt("c2", [128, C2_N])
        self.d_cos = dt("ropecos", [128, SEQ])
        self.d_sin = dt("ropesin", [128, SEQ])
        self.d_perm = dt("ropeperm", [128, 128])
        self.d_pv = dt("pv", [DEPTH, 128, PV_N])
        self.d_pe = dt("pe", [2, 128, PE_N])
        self.d_bt = dt("bt", [2, 2, 32, 16, 128])
        self.d_po = dt("po", [2, 128, PO_N])
        self.d_ada_w = dt("ada_w", [DEPTH, D, 6 * D])
        self.d_w_out = dt("w_out", [DEPTH, D, D])
        self.d_ffn_w_in = dt("ffn_w_in", [DEPTH, D, 2 * FFH])
        self.d_ffn_w_out = dt("ffn_w_out", [DEPTH, FFH, D])
        self.d_ab_w_in = dt("ab_w_in", [2, D, 3072])
        self.d_glu_w = dt("s5_glu_w", [2, 512, 512])
        self.d_cd_w_in = dt("cd_w_in", [2, D, 2240])
        self.d_w_uq = dt("mla_w_uq", [2, 384, 768])
        self.d_w_ukv = dt("mla_w_ukv", [2, 256, 1024])
        self.d_out = dt("out", [D, SEQ], kind="ExternalOutput")
        self.final = []
        self.skip_ctx = False
        self.need_fence = False
        self.wb_i = 0
        self.ps_i = 0
        self.ps_avail = list(range(8))
        self.uid = 0

    def scope(self):
        b = self

        class _Scope(ExitStack):
            def __exit__(self, *a):
                r = ExitStack.__exit__(self, *a)
                b.need_fence = True
                return r

            def close(self):
                ExitStack.close(self)
                b.need_fence = True
        return _Scope()

    def sb(self, es, name, shape, dtype=F32):
        if self.need_fence and FENCE_ON:
            self.kb.fence()
            self.need_fence = False
        return es.enter_context(self.nc.sbuf_tensor("sb_" + name, list(shape), dtype))

    def mm(self, out, lhsT, rhs, start, stop, R, W):
        self.kb.op("pe", lambda e: e.matmul(out, lhsT=lhsT, rhs=rhs, start=start, stop=stop), R, W)

    def tr(self, out, in_, ident, R, W):
        self.kb.op("pe", lambda e: e.transpose(out, in_, ident), R, W)

    def act(self, out, in_, func, R, W, bias=None, scale=1.0):
        if bias is None:
            self.kb.op("act", lambda e: e.activation(out=out, in_=in_, func=func, scale=scale), R, W)
        else:
            self.kb.op("act", lambda e: e.activation(out=out, in_=in_, func=func, bias=bias, scale=scale), R, W)

    def tt(self, out, in0, in1, op, R, W, eng="dve"):
        self.kb.op(eng, lambda e: e.tensor_tensor(out=out, in0=in0, in1=in1, op=op), R, W)

    def ts(self, out, in0, s1, s2, op0, op1, R, W, eng="dve"):
        if s2 is None:
            self.kb.op(eng, lambda e: e.tensor_scalar(out=out, in0=in0, scalar1=s1, scalar2=None, op0=op0), R, W)
        else:
            self.kb.op(eng, lambda e: e.tensor_scalar(out=out, in0=in0, scalar1=s1, scalar2=s2, op0=op0, op1=op1), R, W)

    def stt(self, out, in0, scalar, in1, op0, op1, R, W, eng="dve"):
        self.kb.op(eng, lambda e: e.scalar_tensor_tensor(out=out, in0=in0, scalar=scalar, in1=in1, op0=op0, op1=op1), R, W)

    def cp(self, out, in_, R, W, eng="dve"):
        if eng == "act":
            self.kb.op("act", lambda e: e.copy(out=out, in_=in_), R, W)
        else:
            self.kb.op(eng, lambda e: e.tensor_copy(out=out, in_=in_), R, W)

    def memset(self, ap, val, W, eng="dve"):
        self.kb.op(eng, lambda e: e.memset(ap, val), (), W)

    def dma(self, out, in_, R, W, q="sp"):
        return self.kb.op(q, lambda e: e.dma_start(out=out, in_=in_), R, W, dma=True)

    def dump(self, name, ap, shape, dtype, R):
        if not self.dbg:
            return
        t = self.nc.dram_tensor("dbg_" + name, list(shape), dtype, kind="ExternalOutput").ap()
        self.final.append(self.dma(t, ap, R, (), q="sp"))

    def scan(self, out, d0, d1, init, R, W):
        self.kb.op("dve", lambda e: e.tensor_tensor_scan(out=out, data0=d0, data1=d1, initial=init, op0=ALU.mult, op1=ALU.add), R, W)

    def ps(self):
        k = self.ps_avail[self.ps_i % len(self.ps_avail)]
        self.ps_i += 1
        return self.psum[k], f"ps{k}"

    def wload(self, src, kc, n):
        i = self.wb_i % len(self.wb)
        self.wb_i += 1
        t = self.wb[i]
        key = f"wb{i}"
        assert kc * n <= 4096
        v = bass.AP(t.tensor, t.offset, [list(t.ap[0]), [n, kc], [1, n]])
        self.dma(v, src.rearrange("(kc p) n -> p kc n", p=128), (), [key], q="pool")
        return v, key

    def build(self):
        nc = self.nc
        with ExitStack() as es:
            self.H = self.sb(es, "H", [128, 8, NT])
            self.A = self.sb(es, "A", [128, 8, NT], BF16)
            self.cf = self.sb(es, "cf", [128, CF_N])
            self.onesb = self.sb(es, "onesb", [128, 128], BF16)
            self.s2 = self.sb(es, "s2", [128, 8, 2], BF16)
            self.s2f = self.sb(es, "s2f", [128, 8, 2])
            self.mod2 = [self.sb(es, f"mod{i}", [128, 48, 2]) for i in range(2)]
            self.gs2 = [self.sb(es, f"gs{i}", [128, 2, 8, 2]) for i in range(2)]
            pvt1 = self.sb(es, "pvt", [128, PV_N])
            self.pvt2 = [pvt1, pvt1]
            self.epsT = self.sb(es, "epsT", [128, 1])
            self.oneT = self.sb(es, "oneT", [128, 1])
            self.wbt = [self.sb(es, f"wb{i}", [128, 4096], BF16) for i in range(3)]
            self.wb = [t[:] for t in self.wbt]
            self.psum = [es.enter_context(nc.psum_tensor(f"ps{i}", [128, 512], F32)) for i in range(8)]
            self.ident = self.cf[:, CF_ID:CF_ID + 128]

            self.dma(self.cf[:], self.d_cf, (), ["cf"])
            self.dma(self.s2f[:], self.d_sc, (), ["s2f"])
            for c in range(8):
                self.dma(self.H[:, c, :], self.d_h0[c * 128:(c + 1) * 128, :], (), [f"H{c}.{b}" for b in range(5)], q="sp")
            self.memset(self.onesb[:], 1.0, ["onesb"])
            self.memset(self.epsT[:], EPS, ["epsT"])
            self.memset(self.oneT[:], 1.0, ["oneT"])
            self.act(self.s2[:], self.s2f[:], AF.Silu, ["s2f"], ["s2"])

            for l in self.layers:
                self.layer(l)

            for c in range(8):
                self.final.append(self.dma(self.d_out[c * 128:(c + 1) * 128, :], self.H[:, c, CTX:NT],
                                           [f"H{c}.{b}" for b in range(5)], (), q="sp"))
            self.kb.emit(final_wait_ops=self.final)
        return nc

    def ada_begin(self, l, bank):
        self.dma(self.pvt2[l % 2][:], self.d_pv[l], (), ["pvt"])
        self.ada_ps = (self.psum[bank], f"ps{bank}")

    def ada_piece(self, l, jg, loader):
        pst, pk = self.ada_ps
        w, wk = loader(self.d_ada_w[l][:, jg * 512:(jg + 1) * 512])
        for jj in range(4):
            j = jg * 4 + jj
            for kc in range(8):
                self.mm(pst[:, 2 * j:2 * j + 2], w[:, kc, jj * 128:(jj + 1) * 128], self.s2[:, kc, :], kc == 0, kc == 7,
                        [wk, "s2"], [pk])

    def ada_end(self, l):
        pst, pk = self.ada_ps
        p = l % 2
        mod, gs, pvt = self.mod2[p], self.gs2[p], self.pvt2[p]
        pv3 = pst[:, 0:96].rearrange("p (j v) -> p j v", v=2)
        self.tt(mod[:], pv3, pvt[:, PV_ADAB:PV_ADAB + 48].unsqueeze(2).to_broadcast([128, 48, 2]), ALU.add,
                [pk, "pvt"], [f"mod{p}"])
        for w_, (nofs, sofs) in enumerate(((PV_NM, 8), (PV_NF, 32))):
            self.ts(gs[:, w_, :, :], mod[:, sofs:sofs + 8, :], 1.0, None, ALU.add, None, [f"mod{p}"], [f"gs{p}"])
            self.tt(gs[:, w_, :, :], gs[:, w_, :, :],
                    pvt[:, nofs:nofs + 8].unsqueeze(2).to_broadcast([128, 8, 2]), ALU.mult, [f"gs{p}", "pvt"], [f"gs{p}"])

    def ada(self, l):
        self.ada_begin(l, 7)
        for jg in range(12):
            self.ada_piece(l, jg, lambda src: self.wload(src, 8, 512))
        self.ada_end(l)

    def norm_mod(self, w_, shift_ofs, es, skip0=False):
        sq = self.sb(es, f"nsq{self.uid}", [128, 2, 512], BF16)
        rstd = self.sb(es, f"nrs{self.uid}", [128, 512])
        tmp = self.sb(es, f"ntmp{self.uid}", [128, 2, 512])
        self.uid += 1
        for b, (t0, n, v) in enumerate(TB):
            if skip0 and b == 0:
                continue
            pst, pk = self.ps()
            for c in range(8):
                self.act(sq[:, c % 2, :n], self.H[:, c, t0:t0 + n], AF.Square, [f"H{c}.{b}"], [f"nsq{c % 2}"])
                self.mm(pst[:, :n], self.onesb[:], sq[:, c % 2, :n], c == 0, c == 7, [f"nsq{c % 2}", "onesb"], [pk])
            self.act(rstd[:, :n], pst[:, :n], AF.Ln, [pk, "epsT"], ["nrstd"], bias=self.epsT[:], scale=1.0 / D)
            self.act(rstd[:, :n], rstd[:, :n], AF.Exp, ["nrstd"], ["nrstd"], scale=-0.5)
            for c in range(8):
                self.tt(tmp[:, c % 2, :n], self.H[:, c, t0:t0 + n], rstd[:, :n], ALU.mult, [f"H{c}.{b}", "nrstd"], [f"ntmp{c % 2}"])
                self.act(self.A[:, c, t0:t0 + n], tmp[:, c % 2, :n], AF.Identity, [f"ntmp{c % 2}", self.kgs, self.kmod], [f"A{c}.{b}"],
                         bias=self.mod[:, shift_ofs + c, v:v + 1], scale=self.gs[:, w_, c, v:v + 1])

    def out_proj(self, l, kcs, src, keyf=None):
        nk = len(kcs)
        ws = []
        for og in range(2):
            ws.append(self.wload(self.d_w_out[l][kcs[0] * 128:(kcs[-1] + 1) * 128, og * 512:(og + 1) * 512], nk, 512))
        for o in range(8):
            w, wk = ws[o // 4]
            for b, (t0, n, v) in enumerate(TB):
                if self.skip_ctx and b == 0:
                    continue
                pst, pk = self.ps()
                for i, kc in enumerate(kcs):
                    self.mm(pst[:, :n], w[:, i, (o % 4) * 128:(o % 4 + 1) * 128], src(kc)[:, t0:t0 + n], i == 0, i == nk - 1,
                            [wk, keyf(kc, b) if keyf else f"M{kc}.{b}"], [pk])
                self.stt(self.H[:, o, t0:t0 + n], pst[:, :n], self.mod[:, 16 + o, v:v + 1], self.H[:, o, t0:t0 + n],
                         ALU.mult, ALU.add, [pk, self.kmod, f"H{o}.{b}"], [f"H{o}.{b}"])

    def ffn(self, l, es, nxt=None):
        hact = self.sb(es, f"hact{l}", [128, 4, NT], BF16)
        sg = self.sb(es, f"fsg{l}", [128, 2, 512])
        groups = [(g * 4, 4) for g in range(5)] + [(20, 2)]
        if nxt is not None:
            adaw = [self.sb(es, f"adaw{l}_{i}", [128, 4096], BF16) for i in range(2)]
            acnt = [0]

            def aload(src):
                i = acnt[0] % 2
                acnt[0] += 1
                t = adaw[i][:]
                v_ = bass.AP(t.tensor, t.offset, [list(t.ap[0]), [512, 8], [1, 512]])
                self.dma(v_, src.rearrange("(kc p) n -> p kc n", p=128), (), [f"adaw{i}"], q="pool")
                return v_, f"adaw{i}"
            self.ada_begin(nxt, 7)
            self.ps_avail = list(range(7))
        for gi, (hc0, ng) in enumerate(groups):
            if nxt is not None:
                for jg in (2 * gi, 2 * gi + 1):
                    self.ada_piece(nxt, jg, aload)
            wg, wgk = self.wload(self.d_ffn_w_in[l][:, hc0 * 128:(hc0 + ng) * 128], 8, ng * 128)
            wu, wuk = self.wload(self.d_ffn_w_in[l][:, FFH + hc0 * 128:FFH + (hc0 + ng) * 128], 8, ng * 128)
            wo, wok = self.wload(self.d_ffn_w_out[l][hc0 * 128:(hc0 + ng) * 128, :], ng, 1024)
            for j in range(ng):
                for b, (t0, n, v) in enumerate(TB):
                    if self.skip_ctx and b == 0:
                        continue
                    pg, pgk = self.ps()
                    pu, puk = self.ps()
                    for kc in range(8):
                        self.mm(pg[:, :n], wg[:, kc, j * 128:(j + 1) * 128], self.A[:, kc, t0:t0 + n], kc == 0, kc == 7,
                                [wgk, f"A{kc}.{b}"], [pgk])
                    for kc in range(8):
                        self.mm(pu[:, :n], wu[:, kc, j * 128:(j + 1) * 128], self.A[:, kc, t0:t0 + n], kc == 0, kc == 7,
                                [wuk, f"A{kc}.{b}"], [puk])
                    s = (j * 5 + b) % 2
                    self.act(sg[:, s, :n], pg[:, :n], AF.Silu, [pgk], [f"fsg{s}"])
                    self.tt(hact[:, j, t0:t0 + n], sg[:, s, :n], pu[:, :n], ALU.mult, [f"fsg{s}", puk], [f"hact{j}.{b}"])
            for o in range(8):
                for b, (t0, n, v) in enumerate(TB):
                    if self.skip_ctx and b == 0:
                        continue
                    pst, pk = self.ps()
                    for j in range(ng):
                        self.mm(pst[:, :n], wo[:, j, o * 128:(o + 1) * 128], hact[:, j, t0:t0 + n], j == 0, j == ng - 1,
                                [wok, f"hact{j}.{b}"], [pk])
                    self.stt(self.H[:, o, t0:t0 + n], pst[:, :n], self.mod[:, 40 + o, v:v + 1], self.H[:, o, t0:t0 + n],
                             ALU.mult, ALU.add, [pk, self.kmod, f"H{o}.{b}"], [f"H{o}.{b}"])
        if nxt is not None:
            self.ada_end(nxt)
            self.ps_avail = list(range(8))

    def layer(self, l):
        self.skip_ctx = (l == DEPTH - 1)
        p = l % 2
        self.mod, self.gs, self.pvt = self.mod2[p], self.gs2[p], self.pvt2[p]
        self.kmod, self.kgs = f"mod{p}", f"gs{p}"
        if l == self.layers[0] or not PREFETCH_ADA:
            self.ada(l)
        with self.scope() as es:
            self.norm_mod(0, 0, es)
        with self.scope() as es:
            if l % 2 == 0:
                self.even_mixer(l // 2, es)
            else:
                self.odd_mixer(l // 2, l, es)
        with self.scope() as es:
            self.norm_mod(1, 24, es, skip0=self.skip_ctx)
        with self.scope() as es:
            nxt = self.layers[self.layers.index(l) + 1] if self.layers.index(l) + 1 < len(self.layers) else None
            self.ffn(l, es, nxt if PREFETCH_ADA else None)

    def frac_sin(self, out, q, add, tf, ti, R, W, tag):
        self.ts(tf, q, float(add), None, ALU.add, None, R, [tag + "f"])
        self.cp(ti, tf, [tag + "f"], [tag + "i"])
        self.tt(tf, tf, ti, ALU.subtract, [tag + "f", tag + "i"], [tag + "f"])
        self.act(out, tf, AF.Sin, [tag + "f"], W, scale=TWO_PI)

    def proj_chunk(self, w, wk, col0, ncols, evac):
        for b, (t0, n, v) in enumerate(TB):
            pst, pk = self.ps()
            for kc in range(8):
                self.mm(pst[:ncols, :n], w[:, kc, col0:col0 + ncols], self.A[:, kc, t0:t0 + n], kc == 0, kc == 7,
                        [wk, f"A{kc}.{b}"], [pk])
            evac(pst, pk, b, t0, n, v)

    def even_mixer(self, e, es):
        l = 2 * e
        prm = self.sb(es, f"eprm{e}", [128, PE_CR])
        self.dma(prm[:], self.d_pe[e][:, 0:PE_CR], (), ["eprm"])
        with self.scope() as es2:
            self.Ma = self.sb(es2, f"Ma{self.uid}", [128, 4, NT], BF16)
            self.uid += 1
            if SKIP == "s5":
                self.memset(self.Ma[:], 0.0, [f"M{c}.{b}" for c in range(4) for b in range(5)], eng="pool")
            else:
                self.s5(e, prm, es2)
            self.dump(f"Ma{e}", self.Ma[:], [128, 4, NT], BF16, [f"M{c}.{b}" for c in range(4) for b in range(5)])
            self.out_proj(l, [0, 1, 2, 3], lambda kc: self.Ma[:, kc, :])
        with self.scope() as es2:
            self.Mb = self.sb(es2, f"Mb{self.uid}", [128, 4, NT], BF16)
            self.uid += 1
            if SKIP == "hgrn":
                self.memset(self.Mb[:], 0.0, [f"M{c}.{b}" for c in range(4, 8) for b in range(5)], eng="pool")
            else:
                self.hgrn(e, prm, es2)
            self.dump(f"Mb{e}", self.Mb[:], [128, 4, NT], BF16, [f"M{c}.{b}" for c in range(4, 8) for b in range(5)])
            self.out_proj(l, [4, 5, 6, 7], lambda kc: self.Mb[:, kc - 4, :])

    def mchunk(self, c):
        return self.Ma[:, c, :] if c < 4 else self.Mb[:, c - 4, :]

    def s5(self, e, prm, es_outer):
        nc = self.nc
        es = es_outer.enter_context(self.scope())
        S = lambda name, shape, dt=F32: self.sb(es, f"s5{name}{e}", shape, dt)
        P64 = {k: S(k, [128, 64]) for k in ("r", "q1", "c256", "s256", "c512", "s512")}
        L1f = S("L1f", [128, 64, 16], BF16)
        L2f = S("L2f", [128, 64, 16], BF16)
        esc = es.enter_context(self.scope())
        for k in ("cr", "ci"):
            P64[k] = self.sb(esc, f"s5{k}{e}", [128, 64])
        esp = es.enter_context(self.scope())
        for k in ("lr", "step", "cs", "sn", "ar", "ai", "den", "t0", "t1", "tf"):
            P64[k] = self.sb(esp, f"s5{k}{e}", [128, 64])
        ti64 = self.sb(esp, f"s5ti64{e}", [128, 64], I32)
        lre, lim, lst = prm[:, PE_LRE:PE_LRE + 64], prm[:, PE_LIM:PE_LIM + 64], prm[:, PE_LST:PE_LST + 64]
        p = lambda k: P64[k][:]
        self.ts(p("lr"), lre, -1e-4, None, ALU.min, None, ["eprm"], ["s5lr"])
        self.act(p("step"), lst, AF.Exp, ["eprm"], ["s5step"])
        self.tt(p("t0"), p("lr"), p("step"), ALU.mult, ["s5lr", "s5step"], ["s5t0"])
        self.act(p("r"), p("t0"), AF.Exp, ["s5t0"], ["s5r"])
        self.tt(p("q1"), lim, p("step"), ALU.mult, ["eprm", "s5step"], ["s5q1"])
        self.ts(p("q1"), p("q1"), 1.0 / TWO_PI, None, ALU.mult, None, ["s5q1"], ["s5q1"])
        self.frac_sin(p("cs"), p("q1"), 0.25, p("tf"), ti64[:], ["s5q1"], ["s5cs"], "s5x")
        self.frac_sin(p("sn"), p("q1"), 0.0, p("tf"), ti64[:], ["s5q1"], ["s5sn"], "s5x")
        for T, ck, sk in ((256.0, "c256", "s256"), (512.0, "c512", "s512")):
            self.ts(p("t1"), p("q1"), T, None, ALU.mult, None, ["s5q1"], ["s5t1"])
            self.frac_sin(p(ck), p("t1"), 0.25, p("tf"), ti64[:], ["s5t1"], ["s5" + ck], "s5x")
            self.frac_sin(p(sk), p("t1"), 0.0, p("tf"), ti64[:], ["s5t1"], ["s5" + sk], "s5x")
            self.ts(p(sk), p(sk), self.cf[:, CF_SGN:CF_SGN + 1], None, ALU.mult, None, ["s5" + sk, "cf"], ["s5" + sk])
        self.tt(p("ar"), p("r"), p("cs"), ALU.mult, ["s5r", "s5cs"], ["s5ar"])
        self.tt(p("ai"), p("r"), p("sn"), ALU.mult, ["s5r", "s5sn"], ["s5ai"])
        self.ts(p("ar"), p("ar"), -1.0, None, ALU.add, None, ["s5ar"], ["s5ar"])
        self.tt(p("den"), p("lr"), p("lr"), ALU.mult, ["s5lr"], ["s5den"])
        self.tt(p("t0"), lim, lim, ALU.mult, ["eprm", "s5r"], ["s5t0"])
        self.tt(p("den"), p("den"), p("t0"), ALU.add, ["s5den", "s5t0"], ["s5den"])
        self.kb.op("dve", lambda e_: e_.reciprocal(out=p("den"), in_=p("den")), ["s5den"], ["s5den"])
        self.tt(p("t0"), p("ar"), p("lr"), ALU.mult, ["s5ar", "s5lr"], ["s5t0"])
        self.tt(p("t1"), p("ai"), lim, ALU.mult, ["s5ai", "eprm"], ["s5t1"])
        self.tt(p("t0"), p("t0"), p("t1"), ALU.add, ["s5t0", "s5t1"], ["s5t0"])
        self.tt(p("cr"), p("t0"), p("den"), ALU.mult, ["s5t0", "s5den"], ["s5cr"])
        self.tt(p("t0"), p("ai"), p("lr"), ALU.mult, ["s5ai", "s5lr", "s5cr"], ["s5t0"])
        self.tt(p("t1"), p("ar"), lim, ALU.mult, ["s5ar", "eprm"], ["s5t1"])
        self.tt(p("t0"), p("t0"), p("t1"), ALU.subtract, ["s5t0", "s5t1"], ["s5t0"])
        self.tt(p("ci"), p("t0"), p("den"), ALU.mult, ["s5t0", "s5den"], ["s5ci"])
        esp.close()
        with self.scope() as es3:
            CR = self.sb(es3, f"s5CR{e}", [128, 64, 16])
            CI = self.sb(es3, f"s5CI{e}", [128, 64, 16])
            Cr_ = self.sb(es3, f"s5Cr_{e}", [128, 64, 16])
            Ci_ = self.sb(es3, f"s5Ci_{e}", [128, 64, 16])
            tq = self.sb(es3, f"s5tq{e}", [128, 64, 16])
            self.dma(CR[:].rearrange("p a b -> p (a b)"), self.d_pe[e][:, PE_CR:PE_CR + 1024], (), ["s5CR"])
            self.dma(CI[:].rearrange("p a b -> p (a b)"), self.d_pe[e][:, PE_CI:PE_CI + 1024], (), ["s5CI"])
            crb = p("cr").unsqueeze(2).to_broadcast([128, 64, 16])
            cib = p("ci").unsqueeze(2).to_broadcast([128, 64, 16])
            self.tt(Cr_[:], CR[:], crb, ALU.mult, ["s5CR", "s5cr"], ["s5Cr_"])
            self.tt(tq[:], CI[:], cib, ALU.mult, ["s5CI", "s5ci"], ["s5tq"])
            self.tt(Cr_[:], Cr_[:], tq[:], ALU.subtract, ["s5Cr_", "s5tq"], ["s5Cr_"])
            self.tt(Ci_[:], CR[:], cib, ALU.mult, ["s5CR", "s5ci"], ["s5Ci_"])
            self.tt(tq[:], CI[:], crb, ALU.mult, ["s5CI", "s5cr", "s5Cr_"], ["s5tq"])
            self.tt(Ci_[:], Ci_[:], tq[:], ALU.add, ["s5Ci_", "s5tq"], ["s5Ci_"])
            mlo, mhi = self.cf[:, CF_MLO:CF_MLO + 1], self.cf[:, CF_MHI:CF_MHI + 1]
            self.ts(tq[:], Ci_[:], mhi, None, ALU.mult, None, ["s5Ci_", "cf", "s5Ci_"], ["s5tq"])
            self.stt(L1f[:], Cr_[:], mlo, tq[:], ALU.mult, ALU.subtract, ["s5Cr_", "s5tq", "cf"], ["s5L1f"])
            self.ts(tq[:], Cr_[:], mhi, -1.0, ALU.mult, ALU.mult, ["s5Cr_", "cf", "s5L1f"], ["s5tq"])
            self.ts(Ci_[:], Ci_[:], mlo, None, ALU.mult, None, ["s5Ci_", "cf"], ["s5Ci_"])
            self.tt(L2f[:], tq[:], Ci_[:], ALU.subtract, ["s5tq", "s5Ci_"], ["s5L2f"])
        esc.close()

        u = S("u", [128, NT], BF16)
        y = S("y", [128, NT])
        BW1 = S("BW1", [128, 8, 128], BF16)
        BW2 = S("BW2", [128, 8, 128], BF16)
        LR1 = S("LR1", [128, 8, 128], BF16)
        LR2 = S("LR2", [128, 8, 128], BF16)
        Ec = [S(f"Ec{i}", [128, 512]) for i in range(2)]
        Es = [S(f"Es{i}", [128, 512]) for i in range(2)]
        iot = S("iot", [128, 512])
        self.dma(iot[:], self.d_c2[:, C2_IOTA:C2_IOTA + 512], (), ["s5iot"])
        tib = S("tib", [128, 512], I32)
        R512 = [S(f"R512{i}", [128, 128]) for i in range(2)]
        R256 = [S(f"R256{i}", [128, 128]) for i in range(2)]
        Ta = [S(f"Ta{i}", [128, 512]) for i in range(2)]
        Tb = [S(f"Tb{i}", [128, 512]) for i in range(2)]
        cg = [S(f"cg{i}", [128, 512], BF16) for i in range(2)]
        sg = [S(f"sg{i}", [128, 512], BF16) for i in range(2)]
        carry = S("carry", [128, 2])
        qtr = S("qtr", [128, 1])
        self.memset(qtr[:], 0.25, ["s5qtr"])
        iota = iot[:]
        Jm = self.cf[:, CF_J:CF_J + 128]

        def diag(t):
            a = t[:]
            return bass.AP(a.tensor, a.offset, [list(a.ap[0]), [144, 8], [1, 16]])

        order = {0: [0, 1, 2, 3, 4], 1: [0, 4, 3, 2, 1]}
        step = 0
        for c4 in range(LIM.get('c4', 4)):
            w, wk = self.wload(self.d_ab_w_in[e][:, c4 * 128:(c4 + 1) * 128], 8, 128)

            def evac_u(pst, pk, b, t0, n, v, c4=c4):
                self.act(u[:, t0:t0 + n], pst[:, :n], AF.Copy, [pk], [f"s5u.{b}"])
                self.act(y[:, t0:t0 + n], pst[:, :n], AF.Copy, [pk, "eprm"], [f"s5y.{b}"],
                         scale=prm[:, PE_SD + c4:PE_SD + c4 + 1])
            self.proj_chunk(w, wk, 0, 128, evac_u)
            for d in range(LIM.get('d', 2)):
                bwk = [f"s5BW1.{j}" for j in range(8)]
                self.memset(BW1[:], 0.0, bwk, eng="pool")
                for j in range(8):
                    self.dma(BW1[16 * j:16 * j + 16, j, :], self.d_bt[e, d, c4 * 8 + j], (), [bwk[j]], q="pool")
                self.cp(BW2[:, :, 0:64], BW1[:, :, 64:128], bwk, ["s5BW2"], eng="pool")
                self.ts(BW2[:, :, 64:128], BW1[:, :, 0:64], -1.0, None, ALU.mult, None, bwk, ["s5BW2"], eng="pool")
                base = (d * 32 + c4 * 8) * 16
                for LR, Lf, key in ((LR1, L1f, "s5L1f"), (LR2, L2f, "s5L2f")):
                    self.memset(LR[:], 0.0, ["s5" + ("LR1" if LR is LR1 else "LR2")], eng="pool")
                    src = Lf[:].rearrange("p a b -> p (a b)")[:, base:base + 128].rearrange("p (j c) -> p j c", c=16)
                    self.cp(diag(LR), src, [key], ["s5" + ("LR1" if LR is LR1 else "LR2")], eng="pool")
                self.ps_avail = [0, 1]
                blocks = order[d][:LIM.get('b', 5)]
                for jp in range(0, LIM.get('j', 8), 2):
                    chains = []
                    for jj in range(2):
                        j = jp + jj
                        col = d * 32 + c4 * 8 + j
                        cs_ = lambda k, col=col: P64[k][:, col:col + 1]
                        for tbl, add, tk in ((Es[jj], 0.0, f"s5Es{jj}"), (Ec[jj], 0.25, f"s5Ec{jj}")):
                            if add == 0.0:
                                self.act(tbl[:], iota, AF.Copy, ["s5iot", "s5q1"], [tk], scale=cs_("q1"))
                            else:
                                self.act(tbl[:], iota, AF.Identity, ["s5iot", "s5q1", "s5qtr"], [tk], scale=cs_("q1"), bias=qtr[:])
                            self.cp(tib[:], tbl[:], [tk], ["s5yi"])
                            self.tt(tbl[:], tbl[:], tib[:], ALU.subtract, [tk, "s5yi"], [tk])
                            self.act(tbl[:], tbl[:], AF.Sin, [tk], [tk], scale=TWO_PI)
                        for Rm, rk, ck, sk in ((R512[jj], f"s5R512{jj}", "c512", "s512"), (R256[jj], f"s5R256{jj}", "c256", "s256")):
                            self.act(Rm[:], self.ident, AF.Copy, ["cf", "s5" + ck], [rk], scale=cs_(ck))
                            self.stt(Rm[:], Jm, cs_(sk), Rm[:], ALU.mult, ALU.add, ["cf", "s5" + sk, rk], [rk])
                        chains.append((jj, j, cs_))

                    def mmM(jj, j, bi):
                        t0, n, v = TB[blocks[bi]]
                        ub = u[:, t0:t0 + n]
                        if d == 1:
                            ub = rev(ub, n)
                        M1, k1, M2, k2 = self.psum[2 + 2 * jj], f"ps{2 + 2 * jj}", self.psum[3 + 2 * jj], f"ps{3 + 2 * jj}"
                        self.mm(M1[:, :n], BW1[:, j, :], ub, True, True, [f"s5BW1.{j}", f"s5u.{blocks[bi]}"], [k1])
                        self.mm(M2[:, :n], BW2[:, j, :], ub, True, True, ["s5BW2", f"s5u.{blocks[bi]}"], [k2])

                    for jj, j, cs_ in chains:
                        mmM(jj, j, 0)
                    pend = None
                    for bi, b in enumerate(blocks):
                        t0, n, v = TB[b]
                        for jj, j, cs_ in chains:
                            M1, k1, M2, k2 = self.psum[2 + 2 * jj], f"ps{2 + 2 * jj}", self.psum[3 + 2 * jj], f"ps{3 + 2 * jj}"
                            self.tt(Ta[jj][:, :n], M1[:, :n], Ec[jj][:, :n], ALU.mult, [k1, f"s5Ec{jj}"], [f"s5Ta{jj}"])
                            self.tt(Tb[jj][:, :n], M2[:, :n], Es[jj][:, :n], ALU.mult, [k2, f"s5Es{jj}"], [f"s5Tb{jj}"])
                            if bi < len(blocks) - 1:
                                mmM(jj, j, bi + 1)
                            self.tt(Ta[jj][:, :n], Ta[jj][:, :n], Tb[jj][:, :n], ALU.add, [f"s5Ta{jj}", f"s5Tb{jj}"], [f"s5Ta{jj}"])
                            init = 0.0 if bi == 0 else carry[:, jj:jj + 1]
                            self.scan(Tb[jj][:, :n], cs_("r").to_broadcast([128, n]), Ta[jj][:, :n], init,
                                      [f"s5Ta{jj}", "s5r", f"s5carry{jj}"], [f"s5Tb{jj}"])
                            self.tt(cg[jj][:, :n], Ec[jj][:, :n], Tb[jj][:, :n], ALU.mult, [f"s5Ec{jj}", f"s5Tb{jj}"], [f"s5cg{jj}"], eng="pool")
                            self.tt(sg[jj][:, :n], Es[jj][:, :n], Tb[jj][:, :n], ALU.mult, [f"s5Es{jj}", f"s5Tb{jj}"], [f"s5sg{jj}"], eng="pool")
                            if bi < len(blocks) - 1:
                                Rm, rk = (R256[jj], f"s5R256{jj}") if n == 256 else (R512[jj], f"s5R512{jj}")
                                pc, pck = self.psum[6 + jj], f"ps{6 + jj}"
                                self.mm(pc[:, 0:1], Rm[:], Tb[jj][:, n - 1:n], True, True, [rk, f"s5Tb{jj}"], [pck])
                                self.cp(carry[:, jj:jj + 1], pc[:, 0:1], [pck], [f"s5carry{jj}"], eng="act")
                        if pend is not None:
                            pend()
                        yb, ybk = self.ps()
                        for jj, j, cs_ in chains:
                            self.mm(yb[:, :n], LR1[:, j, :], cg[jj][:, :n], jj == 0, False, ["s5LR1", f"s5cg{jj}"], [ybk])
                            self.mm(yb[:, :n], LR2[:, j, :], sg[jj][:, :n], False, jj == 1, ["s5LR2", f"s5sg{jj}"], [ybk])

                        def pend(yb=yb, ybk=ybk, t0=t0, n=n, b=b):
                            src = yb[:, :n]
                            if d == 1:
                                src = rev(src, n)
                            self.tt(y[:, t0:t0 + n], y[:, t0:t0 + n], src, ALU.add, [f"s5y.{b}", ybk], [f"s5y.{b}"])
                    pend()
                self.ps_avail = list(range(8))
            for b, (t0, n, v) in enumerate(TB):
                yb_ = y[:, t0:t0 + n]
                self.tt(Ta[0][:, :n], yb_, yb_, ALU.mult, [f"s5y.{b}"], ["s5Ta0"])
                self.ts(Ta[0][:, :n], Ta[0][:, :n], 0.044715, 1.0, ALU.mult, ALU.add, ["s5Ta0"], ["s5Ta0"])
                self.tt(Ta[0][:, :n], Ta[0][:, :n], yb_, ALU.mult, ["s5Ta0", f"s5y.{b}"], ["s5Ta0"])
                self.act(Tb[0][:, :n], Ta[0][:, :n], AF.Sigmoid, ["s5Ta0"], ["s5Tb0"], scale=1.5957691216057308)
                self.tt(self.Ma[:, c4, t0:t0 + n], yb_, Tb[0][:, :n], ALU.mult, [f"s5y.{b}", "s5Tb0"], [f"M{c4}.{b}"])
        self.dump(f"s5u{e}", u[:], [128, NT], BF16, [f"s5u.{b}" for b in range(5)])
        self.dump(f"s5y{e}", y[:], [128, NT], F32, [f"s5y.{b}" for b in range(5)])
        self.dump(f"s5prm{e}", prm[:], [128, PE_CR], F32, ["eprm"])
        self.dump(f"s5L1f{e}", L1f[:], [128, 64, 16], BF16, ["s5L1f"])
        self.dump(f"s5Tb{e}", Tb[0][:], [128, 512], F32, ["s5Tb0"])
        self.dump(f"s5r{e}", P64["r"][:], [128, 64], F32, ["s5r"])
        self.dump(f"s5BW1{e}", BW1[:], [128, 8, 128], BF16, [f"s5BW1.{j}" for j in range(8)])
        es.close()
        wg, wgk = self.wload(self.d_glu_w[e], 4, 512)
        SG = self.sb(es_outer, f"s5SG{e}", [128, 4, 512], BF16)
        for b, (t0, n, v) in enumerate(TB):
            for c in range(4):
                pst, pk = self.ps()
                for kc in range(4):
                    self.mm(pst[:, :n], wg[:, kc, c * 128:(c + 1) * 128], self.Ma[:, kc, t0:t0 + n], kc == 0, kc == 3,
                            [wgk, f"M{kc}.{b}"], [pk])
                self.act(SG[:, c, :n], pst[:, :n], AF.Sigmoid, [pk, "eprm"], [f"s5SG{c}"], bias=prm[:, PE_GB + c:PE_GB + c + 1])
            for c in range(4):
                self.tt(self.Ma[:, c, t0:t0 + n], self.Ma[:, c, t0:t0 + n], SG[:, c, :n], ALU.mult,
                        [f"M{c}.{b}", f"s5SG{c}"], [f"M{c}.{b}"])

    def hgrn(self, e, prm, es):
        S = lambda name, shape, dt=F32: self.sb(es, f"hg{name}{e}", shape, dt)
        E0, E1, Ss, LB, OML, NOML = (S(k, [128, 4]) for k in ("E0", "E1", "Ss", "LB", "OML", "NOML"))
        self.act(E0[:], prm[:, PE_LBL:PE_LBL + 4], AF.Exp, ["eprm"], ["hgE0"])
        self.act(E1[:], prm[:, PE_LBL + 4:PE_LBL + 8], AF.Exp, ["eprm"], ["hgE1"])
        self.tt(Ss[:], E0[:], E1[:], ALU.add, ["hgE0", "hgE1"], ["hgSs"])
        self.kb.op("dve", lambda e_: e_.reciprocal(out=Ss[:], in_=Ss[:]), ["hgSs"], ["hgSs"])
        self.tt(E0[:], E0[:], Ss[:], ALU.mult, ["hgE0", "hgSs"], ["hgE0"])
        self.tt(E1[:], E1[:], Ss[:], ALU.mult, ["hgE1", "hgSs"], ["hgE1"])
        if e == 0:
            self.tt(LB[:], E0[:], E0[:], ALU.subtract, ["hgE0"], ["hgLB"])
        else:
            self.tt(LB[:], E0[:], E1[:], ALU.add, ["hgE0", "hgE1"], ["hgLB"])
            self.tt(LB[:], LB[:], E0[:], ALU.subtract, ["hgLB", "hgE0"], ["hgLB"])
        self.ts(OML[:], LB[:], -1.0, 1.0, ALU.mult, ALU.add, ["hgLB"], ["hgOML"])
        self.ts(NOML[:], LB[:], -1.0, None, ALU.add, None, ["hgLB"], ["hgNOML"])

        qs = S("qs", [128, NT], BF16)
        cmk = S("cmk", [128, 1024], BF16)
        self.dma(cmk[:], self.d_c2[:, C2_CMASK:C2_CMASK + 1024], (), ["hgcmk"], q="pool")
        CH = 32
        vtm = S("vtm", [CH, NT // CH, 128], BF16)
        O = S("O", [128, NT])
        T = {i: S(f"T{i}", [128, 512]) for i in (1, 2, 3, 5)}
        EB = S("EB", [128, 512])
        KK = S("KK", [128, 512])
        KH = T[3]
        QA = S("QA", [128, 512])
        QR = S("QR", [128, 512], BF16)
        KT = S("KT", [128, 512], BF16)
        sqb = QR
        PM = [S(f"PM{i}", [CH, CH], BF16) for i in range(3)]
        KHt = [S(f"KHt{i}", [CH, 128], BF16) for i in range(3)]
        St = [S(f"St{i}", [128, 128]) for i in range(2)]
        cm_f = cmk[:, 0:512]
        cm_b = cmk[:, 512:1024]
        mask = {0: self.cf[0:CH, CF_MF:CF_MF + CH], 1: self.cf[0:CH, CF_MB:CF_MB + CH]}
        order = {0: [0, 1, 2, 3, 4], 1: [0, 4, 3, 2, 1]}
        wcols = [512, 1024, 1536, 2048, 2560]
        for h in range(4):
            wl = lambda wi: self.wload(self.d_ab_w_in[e][:, wcols[wi] + 128 * h:wcols[wi] + 128 * h + 128], 8, 128)

            def evac_q(pst, pk, b, t0, n, v):
                self.act(qs[:, t0:t0 + n], pst[:, :n], AF.Silu, [pk], [f"hgqs.{b}"])
            wq, wqk = wl(0)
            self.proj_chunk(wq, wqk, 0, 128, evac_q)

            def evac_v(pst, pk, b, t0, n, v):
                self.cp(T[1][:, :n], pst[:, :n], [pk], ["hgT1"], eng="act")
                for ci in range(n // CH):
                    pt, ptk = self.ps()
                    self.tr(pt[0:CH, 0:128], T[1][:, ci * CH:ci * CH + CH], self.ident, ["hgT1", "cf"], [ptk])
                    self.cp(vtm[:, t0 // CH + ci, :], pt[0:CH, 0:128], [ptk], [f"hgvtm.{b}"])
            wv, wvk = wl(3)
            self.proj_chunk(wv, wvk, 0, 128, evac_v)

            for d in range(2):
                self.memset(St[0][:], 0.0, ["hgSt0"])
                kst = [0]
                lc = CH - 1 if d == 0 else 0
                wf, wfk = wl(1 + d)
                def part1(b):
                    t0, n, v = TB[b]
                    pf, pfk = self.ps()
                    for kc in range(8):
                        self.mm(pf[:, :n], wf[:, kc, :], self.A[:, kc, t0:t0 + n], kc == 0, kc == 7,
                                [wfk, f"A{kc}.{b}"], [pfk])
                    t = lambda i: T[i][:, :n]
                    self.act(t(1), pf[:, :n], AF.Exp, [pfk], ["hgT1"], scale=-1.0)
                    self.ts(t(1), t(1), 1.1420073898156842e26, None, ALU.min, None, ["hgT1"], ["hgT1"])
                    self.act(t(2), t(1), AF.Ln, ["hgT1", "oneT"], ["hgT2"], bias=self.oneT[:])
                    self.act(t(5), t(2), AF.Exp, ["hgT2"], ["hgT5"], scale=-1.0)
                    self.ts(KK[:, :n], t(5), NOML[:, h:h + 1], OML[:, h:h + 1], ALU.mult, ALU.add, ["hgT5", "hgNOML", "hgOML"], ["hgKK"])

                def part2(b):
                    t0, n, v = TB[b]
                    nch = n // CH
                    t = lambda i: T[i][:, :n]
                    v3 = lambda ap: ap.rearrange("p (c s) -> p c s", s=CH)
                    bc = lambda ap, col: v3(ap)[:, :, col:col + 1].to_broadcast([128, nch, CH])
                    self.act(t(3), t(1), AF.Ln, ["hgT1", "oneT", "hgLB"], ["hgT3"], bias=self.oneT[:], scale=LB[:, h:h + 1])
                    self.tt(t(3), t(3), t(2), ALU.subtract, ["hgT3", "hgT2"], ["hgT3"])
                    if d == 0:
                        self.scan(t(5), cm_f[:, :n], t(3), 0.0, ["hgT3", "hgcmk", "hgKK"], ["hgT5"])
                    else:
                        self.scan(rev(t(5), n), rev(cm_b[:, :n], n), rev(t(3), n), 0.0, ["hgT3", "hgcmk", "hgKK"], ["hgT5"])
                    self.act(EB[:, :n], t(5), AF.Exp, ["hgT5"], ["hgEB"])
                    self.tt(QA[:, :n], qs[:, t0:t0 + n], EB[:, :n], ALU.mult, [f"hgqs.{b}", "hgEB"], ["hgQA"])
                    self.tt(v3(t(2)), v3(t(5)), bc(t(5), CH // 2), ALU.subtract, ["hgT5"], ["hgT2"])
                    self.act(t(1), t(2), AF.Exp, ["hgT2"], ["hgT1"])
                    self.tt(QR[:, :n], qs[:, t0:t0 + n], t(1), ALU.mult, [f"hgqs.{b}", "hgT1"], ["hgQR"])
                    self.act(t(1), t(2), AF.Exp, ["hgT2", "hgQR"], ["hgT1"], scale=-1.0)
                    self.tt(KT[:, :n], KK[:, :n], t(1), ALU.mult, ["hgKK", "hgT1"], ["hgKT"])
                    self.tt(v3(t(2)), bc(t(5), lc), v3(t(5)), ALU.subtract, ["hgT5", "hgKT"], ["hgT2"])
                    self.act(t(2), t(2), AF.Exp, ["hgT2"], ["hgT2"])
                    self.tt(KH[:, :n], KK[:, :n], t(2), ALU.mult, ["hgKK", "hgT2", "hgT3"], ["hgT3"])

                blocks_ = order[d]
                part1(blocks_[0])
                for bi_, b in enumerate(blocks_):
                    t0, n, v = TB[b]
                    nch = n // CH
                    nxt_b = blocks_[bi_ + 1] if bi_ + 1 < len(blocks_) else None
                    part2(b)
                    clist = list(range(nch)) if d == 0 else list(range(nch - 1, -1, -1))
                    st1 = {}

                    def stage1a(ci):
                        c0 = ci * CH
                        gch = t0 // CH + ci
                        par = gch % 3
                        pS, pSk = self.ps()
                        self.mm(pS[0:CH, 0:CH], KT[:, c0:c0 + CH], QR[:, c0:c0 + CH], True, True, ["hgKT", "hgQR"], [pSk])
                        self.tt(PM[par][:], pS[0:CH, 0:CH], mask[d], ALU.mult, [pSk, "cf"], [f"hgPM{par}"])
                        pT, pTk = self.ps()
                        self.tr(pT[0:CH, 0:128], KH[:, c0:c0 + CH], self.ident, ["hgT3", "cf"], [pTk])
                        self.cp(KHt[par][:], pT[0:CH, 0:128], [pTk], [f"hgKHt{par}"], eng="act")

                    def stage1b(ci):
                        gch = t0 // CH + ci
                        par = gch % 3
                        pD, pDk = self.ps()
                        self.mm(pD[:, 0:128], KHt[par][:], vtm[:, gch, :], True, True, [f"hgKHt{par}", f"hgvtm.{b}"], [pDk])
                        st1[ci] = (pD, pDk)

                    def stage2(ci):
                        c0 = ci * CH
                        gch = t0 // CH + ci
                        par = gch % 3
                        pD, pDk = st1.pop(ci)
                        sp, sn = kst[0] % 2, (kst[0] + 1) % 2
                        kst[0] += 1
                        pO, pOk = self.ps()
                        self.mm(pO[:, 0:CH], St[sp][:], QA[:, c0:c0 + CH], True, False, [f"hgSt{sp}", "hgQA"], [pOk])
                        self.mm(pO[:, 0:CH], vtm[:, gch, :], PM[par][:], False, True, [f"hgvtm.{b}", f"hgPM{par}"], [pOk])
                        self.stt(St[sn][:], St[sp][:], EB[:, c0 + lc:c0 + lc + 1], pD[:, 0:128], ALU.mult, ALU.add,
                                 [f"hgSt{sp}", "hgEB", pDk], [f"hgSt{sn}"])
                        if d == 0:
                            self.cp(O[:, t0 + c0:t0 + c0 + CH], pO[:, 0:CH], [pOk], [f"hgO.{b}"], eng="act")
                        else:
                            self.tt(O[:, t0 + c0:t0 + c0 + CH], O[:, t0 + c0:t0 + c0 + CH], pO[:, 0:CH], ALU.add,
                                    [pOk, f"hgO.{b}"], [f"hgO.{b}"])

                    stage1a(clist[0])
                    if len(clist) > 1:
                        stage1a(clist[1])
                    stage1b(clist[0])
                    for i_, ci in enumerate(clist):
                        if i_ + 2 < len(clist):
                            stage1a(clist[i_ + 2])
                        if i_ + 1 < len(clist):
                            stage1b(clist[i_ + 1])
                        stage2(ci)
                        if i_ == 1 and nxt_b is not None:
                            part1(nxt_b)
            if h == 3:
                self.dump(f"hgqs{e}", qs[:], [128, NT], BF16, [f"hgqs.{b}" for b in range(5)])
                self.dump(f"hgO{e}", O[:], [128, NT], F32, [f"hgO.{b}" for b in range(5)])
                self.dump(f"hgvtm{e}", vtm[:], [32, 72, 128], BF16, [f"hgvtm.{b}" for b in range(5)])
                self.dump(f"hgEB{e}", EB[:], [128, 512], F32, ["hgEB"])
                self.dump(f"hgKK{e}", KK[:], [128, 512], F32, ["hgKK"])
                self.dump(f"hgT5{e}", T[5][:], [128, 512], F32, ["hgT5"])
                self.dump(f"hgLB{e}", LB[:], [128, 4], F32, ["hgLB"])
            wgt, wgtk = wl(4)
            for b, (t0, n, v) in enumerate(TB):
                self.act(sqb[:, :n], O[:, t0:t0 + n], AF.Square, [f"hgO.{b}"], ["hgQR"])
                pR, pRk = self.ps()
                self.mm(pR[:, :n], self.onesb[:], sqb[:, :n], True, True, ["onesb", "hgQR"], [pRk])
                self.act(T[1][:, :n], pR[:, :n], AF.Ln, [pRk, "epsT"], ["hgT1"], bias=self.epsT[:], scale=1.0 / 128.0)
                self.act(T[1][:, :n], T[1][:, :n], AF.Exp, ["hgT1"], ["hgT1"], scale=-0.5)
                pg, pgk = self.ps()
                for kc in range(8):
                    self.mm(pg[:, :n], wgt[:, kc, :], self.A[:, kc, t0:t0 + n], kc == 0, kc == 7, [wgtk, f"A{kc}.{b}"], [pgk])
                self.act(T[2][:, :n], pg[:, :n], AF.Silu, [pgk], ["hgT2"])
                self.tt(T[1][:, :n], T[1][:, :n], O[:, t0:t0 + n], ALU.mult, ["hgT1", f"hgO.{b}"], ["hgT1"])
                self.tt(T[1][:, :n], T[1][:, :n], T[2][:, :n], ALU.mult, ["hgT1", "hgT2"], ["hgT1"])
                self.act(self.Mb[:, h, t0:t0 + n], T[1][:, :n], AF.Identity, ["hgT1", "eprm"], [f"M{4 + h}.{b}"],
                         scale=prm[:, PE_ON:PE_ON + 1])

    def norm_rope(self, pq, pqk, rows, gm, gmk, inv_dim, gain, t0, n, is_x, dest, destk, tm, tag="", alt=False):
        sq, rs, qn, qb, t1, t2, cosT, sinT = tm
        ksq, krs = f"nr{tag}_sq", f"nr{tag}_rs"
        if alt:
            assert not is_x
            sq, rs, ksq, krs = qb, t1, f"nr{tag}_qb", f"nr{tag}_t1"
        self.act(sq[:rows, :n], pq, AF.Square, [pqk], [ksq])
        pn, pnk = self.ps()
        self.mm(pn[:rows, :n], gm, sq[:rows, :n], True, True, [gmk, ksq], [pnk])
        self.act(rs[:rows, :n], pn[:rows, :n], AF.Ln, [pnk, "epsT"], [krs], bias=self.epsT[:rows, :], scale=inv_dim)
        self.act(rs[:rows, :n], rs[:rows, :n], AF.Exp, [krs], [krs], scale=-0.5)
        if not is_x:
            self.stt(dest, pq, gain, rs[:rows, :n], ALU.mult, ALU.mult, [pqk, krs, "oprm"], destk)
            return
        self.stt(qn[:rows, :n], pq, gain, rs[:rows, :n], ALU.mult, ALU.mult, [pqk, krs, "oprm"], [f"nr{tag}_qn"])
        self.cp(qb[:rows, :n], qn[:rows, :n], [f"nr{tag}_qn"], [f"nr{tag}_qb"], eng="act")
        pr, prk = self.ps()
        self.mm(pr[:rows, :n], self.permb[:rows, :rows], qb[:rows, :n], True, True, ["permb", f"nr{tag}_qb"], [prk])
        x0 = t0 - CTX
        self.dma(cosT[:rows, :n], self.d_cos[0:rows, x0:x0 + n], (), [f"nr{tag}_cos"])
        self.dma(sinT[:rows, :n], self.d_sin[0:rows, x0:x0 + n], (), [f"nr{tag}_sin"])
        self.tt(t1[:rows, :n], qn[:rows, :n], cosT[:rows, :n], ALU.mult, [f"nr{tag}_qn", f"nr{tag}_cos"], [f"nr{tag}_t1"])
        self.tt(t2[:rows, :n], pr[:rows, :n], sinT[:rows, :n], ALU.mult, [prk, f"nr{tag}_sin"], [f"nr{tag}_t2"])
        self.tt(dest, t1[:rows, :n], t2[:rows, :n], ALU.add, [f"nr{tag}_t1", f"nr{tag}_t2"], destk)

    def odd_mixer(self, o, l, es):
        lam_init = 0.8 - 0.6 * math.exp(-0.3 * l)
        prm = self.sb(es, f"oprm{o}", [128, PO_N])
        self.dma(prm[:], self.d_po[o], (), ["oprm"])
        self.permb = self.sb(es, f"permb{o}", [128, 128], BF16)
        self.dma(self.permb[:], self.d_perm, (), ["permb"], q="pool")
        self.bdb = self.sb(es, f"bdb{o}", [128, 128], BF16)
        self.cp(self.bdb[:], self.cf[:, CF_BD:CF_BD + 128], ["cf"], ["bdb"])
        S0 = lambda name, shape, dt=F32: self.sb(es, f"od{name}{o}", shape, dt)
        lp = S0("lp", [128, 2])
        nlam = S0("nlam", [128, 1])
        subg = S0("subg", [128, 1])
        with self.scope() as esl:
            onesf = self.sb(esl, f"onesf{o}", [128, 128])
            self.memset(onesf[:], 1.0, ["onesf"])
            self.memset(lp[:], 0.0, ["odlp"])
            self.tt(lp[0:64, 0:1], prm[0:64, PO_LAM:PO_LAM + 1], prm[0:64, PO_LAM + 1:PO_LAM + 2], ALU.mult, ["oprm", "odlp"], ["odlp"])
            self.tt(lp[0:64, 1:2], prm[0:64, PO_LAM + 2:PO_LAM + 3], prm[0:64, PO_LAM + 3:PO_LAM + 4], ALU.mult, ["oprm", "odlp"], ["odlp"])
            pl, plk = self.ps()
            self.mm(pl[:, 0:2], onesf[:], lp[:], True, True, ["onesf", "odlp"], [plk])
            self.act(lp[:], pl[:, 0:2], AF.Exp, [plk], ["odlp"])
            self.tt(nlam[:], lp[:, 1:2], lp[:, 0:1], ALU.subtract, ["odlp"], ["odnlam"])
            self.ts(nlam[:], nlam[:], -lam_init, None, ALU.add, None, ["odnlam"], ["odnlam"])
            self.ts(subg[:], prm[:, PO_SUB:PO_SUB + 1], 1.0 - lam_init, None, ALU.mult, None, ["oprm"], ["odsubg"])
        tm = (S0("sq", [128, 512], BF16), S0("rs", [128, 512]), S0("qn", [128, 512]), S0("qb", [128, 512], BF16),
              S0("t1", [128, 512]), S0("t2", [128, 512]), S0("cosT", [128, 512]), S0("sinT", [128, 512]))
        Pt = [S0(f"P{i}", [128, 512], BF16) for i in range(4)]
        orec = S0("orec", [128, 512])
        oacc = S0("oacc", [128, 512])

        def attend(qblk, ktiles, score_fn, v_fn, nacc, scale, finish, zacc=None):
            t0, n, v = TB[qblk]
            accs = [(self.psum[2 * i], f"ps{2 * i}", self.psum[2 * i + 1], f"ps{2 * i + 1}") for i in range(nacc)]
            self.ps_avail = list(range(2 * nacc, 8))
            nk = len(ktiles)
            depth = 1 if nacc == 2 else 3
            sc_ = {}

            def do_scores(ki):
                for i in range(nacc):
                    pS, pSk = self.ps()
                    score_fn(i, pS, pSk, ktiles[ki], t0, n)
                    sc_[(ki, i)] = (pS, pSk)

            for ki in range(min(depth, nk)):
                do_scores(ki)
            for ki, kt in enumerate(ktiles):
                if ki + depth < nk:
                    do_scores(ki + depth)
                for i in range(nacc):
                    pS, pSk = sc_.pop((ki, i))
                    P = Pt[(nacc * ki + i) % 4]
                    Pk = f"odP{(nacc * ki + i) % 4}"
                    self.act(P[:, :n], pS[:, :n], AF.Exp, [pSk], [Pk], scale=scale)
                    O_, Ok, Z_, Zk = accs[i]
                    vl, vk = v_fn(kt)
                    self.mm(O_[:, :n], vl, P[:, :n], ki == 0, ki == nk - 1, [vk, Pk], [Ok])
                    if zacc is None or i != 0:
                        self.mm(Z_[:, :n], self.onesb[:], P[:, :n], ki == 0, ki == nk - 1, ["onesb", Pk], [Zk])
                    else:
                        Zf, _ = zacc
                        if ki == 0:
                            self.cp(Zf[:, :n], P[:, :n], [Pk], ["odZf"])
                        else:
                            self.tt(Zf[:, :n], Zf[:, :n], P[:, :n], ALU.add, ["odZf", Pk], ["odZf"])
            if zacc is not None:
                Zf, ones_f = zacc
                O_, Ok, Z_, Zk = accs[0]
                self.mm(Z_[:, :n], ones_f[:], Zf[:, :n], True, True, ["odonesf", "odZf"], [Zk])
            finish(accs, t0, n, qblk)
            self.ps_avail = list(range(8))

        with self.scope() as es2:
            self.Ma = self.sb(es2, f"Ma{self.uid}", [128, 4, NT], BF16)
            self.uid += 1
            S = lambda name, shape, dt=F32: self.sb(es2, f"df{name}{o}", shape, dt)
            QD = S("QD", [128, NT], BF16)
            KD = S("KD", [128, NT], BF16)
            Vt = S("Vt", [128, 18, 128], BF16)
            sqo = S("sqo", [128, 512], BF16)
            tmB = (S("sqB", [128, 512], BF16), S("rsB", [128, 512]), S("qnB", [128, 512]), S("qbB", [128, 512], BF16),
                   S("t1B", [128, 512]), S("t2B", [128, 512]), S("cosB", [128, 512]), S("sinB", [128, 512]))
            tms = [(tm, ""), (tmB, "B")]
            Zf = S("Zf", [128, 512])
            ones_f = S("onesf2", [128, 128])
            self.memset(ones_f[:], 1.0, ["odonesf"])
            zacc = (Zf, ones_f)
            ncall = [0]
            for h in range(4):
                for which, col0, dst, dk, gcol in ((0, 0, QD, "dfQD", PO_QG), (1, 512, KD, "dfKD", PO_KG)):
                    w, wk = self.wload(self.d_cd_w_in[o][:, col0 + 128 * h:col0 + 128 * h + 128], 8, 128)

                    def evac(pst, pk, b, t0, n, v, dst=dst, dk=dk, gcol=gcol):
                        tmx, tagx = tms[ncall[0] % 2]
                        ncall[0] += 1
                        self.norm_rope(pst[:, :n], pk, 128, self.bdb[:], "bdb", 1.0 / 64.0, prm[:, gcol:gcol + 1], t0, n, v == 0,
                                       dst[:, t0:t0 + n], [f"{dk}.{b}"], tmx, tagx)
                    self.proj_chunk(w, wk, 0, 128, evac)
                wv, wvk = self.wload(self.d_cd_w_in[o][:, 1024 + 128 * h:1024 + 128 * h + 128], 8, 128)
                for tt_ in range(18):
                    b = 0 if tt_ < 2 else 1 + (tt_ - 2) // 4
                    pv_, pvk = self.ps()
                    for kc in range(8):
                        self.mm(pv_[:, 0:128], self.A[:, kc, tt_ * 128:(tt_ + 1) * 128], wv[:, kc, :], kc == 0, kc == 7,
                                [wvk, f"A{kc}.{b}"], [pvk])
                    self.cp(Vt[:, tt_, :], pv_[:, 0:128], [pvk], [f"dfVt.{tt_}"], eng="act")

                def score(i, pS, pSk, kt, t0, n):
                    bq = [b for b, tb in enumerate(TB) if tb[0] == t0][0]
                    bk = 0 if kt < 2 else 1 + (kt - 2) // 4
                    self.mm(pS[:, :n], KD[64 * i:64 * i + 64, kt * 128:(kt + 1) * 128], QD[64 * i:64 * i + 64, t0:t0 + n], True, True,
                            [f"dfKD.{bk}", f"dfQD.{bq}"], [pSk])

                def vfn(kt):
                    return Vt[:, kt, :], f"dfVt.{kt}"

                def finish(accs, t0, n, qblk, h=h):
                    (O1, O1k, Z1, Z1k), (O2, O2k, Z2, Z2k) = accs
                    r2 = tm[4]
                    self.act(orec[:, :n], Z1[:, :n], AF.Ln, [Z1k], ["odorec"])
                    self.act(orec[:, :n], orec[:, :n], AF.Exp, ["odorec"], ["odorec"], scale=-1.0)
                    self.act(r2[:, :n], Z2[:, :n], AF.Ln, [Z2k], ["nr_t1"])
                    self.act(r2[:, :n], r2[:, :n], AF.Exp, ["nr_t1"], ["nr_t1"], scale=-1.0)
                    self.tt(oacc[:, :n], O1[:, :n], orec[:, :n], ALU.mult, [O1k, "odorec"], ["odoacc"])
                    self.tt(orec[:, :n], O2[:, :n], r2[:, :n], ALU.mult, [O2k, "nr_t1", "odoacc"], ["odorec"])
                    self.stt(oacc[:, :n], orec[:, :n], nlam[:, 0:1], oacc[:, :n], ALU.mult, ALU.add, ["odorec", "odnlam", "odoacc"], ["odoacc"])
                    self.act(sqo[:, :n], oacc[:, :n], AF.Square, ["odoacc"], ["dfsqo"])
                    pn, pnk = self.ps()
                    self.mm(pn[:, :n], self.onesb[:], sqo[:, :n], True, True, ["onesb", "dfsqo"], [pnk])
                    self.act(orec[:, :n], pn[:, :n], AF.Ln, [pnk, "epsT"], ["odorec"], bias=self.epsT[:], scale=1.0 / 128.0)
                    self.act(orec[:, :n], orec[:, :n], AF.Exp, ["odorec"], ["odorec"], scale=-0.5)
                    self.stt(self.Ma[:, h, t0:t0 + n], oacc[:, :n], subg[:, 0:1], orec[:, :n], ALU.mult, ALU.mult,
                             ["odoacc", "odsubg", "odorec"], [f"M{h}.{qblk}"])
                if not self.skip_ctx:
                    attend(0, [0, 1], score, vfn, 2, 0.125, finish, zacc)
                for qblk in range(1, 5):
                    attend(qblk, list(range(18)), score, vfn, 2, 0.125, finish, zacc)
            self.dump(f"Mo{o}a", self.Ma[:], [128, 4, NT], BF16, [f"M{c}.{b}" for c in range(4) for b in range(5)])
            self.out_proj(l, [0, 1, 2, 3], lambda kc: self.Ma[:, kc, :])

        with self.scope() as es2:
            S = lambda name, shape, dt=F32: self.sb(es2, f"ml{name}{o}", shape, dt)
            CQn = S("CQn", [128, 3, NT], BF16)
            CKVn = S("CKVn", [128, 2, NT], BF16)
            KR = S("KR", [64, NT], BF16)
            Mh = S("Mh", [128, NT], BF16)
            VMh = S("VMh", [128, 18, 128], BF16)
            QN = S("QN", [128, NT], BF16)
            QR = S("QR", [64, NT], BF16)
            KN = S("KN", [128, NT], BF16)
            rawt = [tm[2], tm[4], tm[5]]
            rawk = ["nr_qn", "nr_t1", "nr_t2"]
            sq3, rs3 = tm[0], tm[1]
            for (col0, nch, dst, dk, gofs) in ((1536, 3, CQn, "mlCQn", PO_QA), (1920, 2, CKVn, "mlCKVn", PO_KVA)):
                w, wk = self.wload(self.d_cd_w_in[o][:, col0:col0 + nch * 128], 8, nch * 128)
                for b, (t0, n, v) in enumerate(TB):
                    pn, pnk = self.ps()
                    for c in range(nch):
                        pst, pk = self.ps()
                        for kc in range(8):
                            self.mm(pst[:, :n], w[:, kc, c * 128:(c + 1) * 128], self.A[:, kc, t0:t0 + n], kc == 0, kc == 7,
                                    [wk, f"A{kc}.{b}"], [pk])
                        self.cp(rawt[c][:, :n], pst[:, :n], [pk], [rawk[c]], eng="act")
                        self.act(sq3[:, :n], pst[:, :n], AF.Square, [pk], ["nr_sq"])
                        self.mm(pn[:, :n], self.onesb[:], sq3[:, :n], c == 0, c == nch - 1, ["onesb", "nr_sq"], [pnk])
                    self.act(rs3[:, :n], pn[:, :n], AF.Ln, [pnk, "epsT"], ["nr_rs"], bias=self.epsT[:], scale=1.0 / (nch * 128.0))
                    self.act(rs3[:, :n], rs3[:, :n], AF.Exp, ["nr_rs"], ["nr_rs"], scale=-0.5)
                    for c in range(nch):
                        self.stt(dst[:, c, t0:t0 + n], rawt[c][:, :n], prm[:, gofs + c:gofs + c + 1], rs3[:, :n], ALU.mult, ALU.mult,
                                 [rawk[c], "oprm", "nr_rs"], [f"{dk}{c}.{b}"])
            w, wk = self.wload(self.d_cd_w_in[o][:, 2176:2240], 8, 64)

            def evac_kr(pst, pk, b, t0, n, v):
                self.norm_rope(pst[0:64, :n], pk, 64, self.onesb[0:64, 0:64], "onesb", 1.0 / 64.0, prm[0:64, PO_RK:PO_RK + 1], t0, n,
                               v == 0, KR[:, t0:t0 + n], [f"mlKR.{b}"], tm)
            self.proj_chunk(w, wk, 0, 64, evac_kr)
            mscale = 192.0 ** -0.5
            for h in range(4):
                wuq, wuqk = self.wload(self.d_w_uq[o], 3, 768)
                wukv, wukvk = self.wload(self.d_w_ukv[o], 2, 1024)
                for b, (t0, n, v) in enumerate(TB):
                    pq, pqk = self.ps()
                    for kc in range(3):
                        self.mm(pq[:, :n], wuq[:, kc, h * 192:h * 192 + 128], CQn[:, kc, t0:t0 + n], kc == 0, kc == 2,
                                [wuqk, f"mlCQn{kc}.{b}"], [pqk])
                    self.norm_rope(pq[:, :n], pqk, 128, self.onesb[:], "onesb", 1.0 / 128.0, prm[:, PO_NQ:PO_NQ + 1], t0, n, False,
                                   QN[:, t0:t0 + n], [f"mlQN.{b}"], tm)
                    pk_, pkk_ = self.ps()
                    for kc in range(2):
                        self.mm(pk_[:, :n], wukv[:, kc, h * 256:h * 256 + 128], CKVn[:, kc, t0:t0 + n], kc == 0, kc == 1,
                                [wukvk, f"mlCKVn{kc}.{b}"], [pkk_])
                    self.norm_rope(pk_[:, :n], pkk_, 128, self.onesb[:], "onesb", 1.0 / 128.0, prm[:, PO_NK:PO_NK + 1], t0, n, False,
                                   KN[:, t0:t0 + n], [f"mlKN.{b}"], tm, "", True)
                    pr_, prk_ = self.ps()
                    for kc in range(3):
                        self.mm(pr_[0:64, :n], wuq[:, kc, h * 192 + 128:h * 192 + 192], CQn[:, kc, t0:t0 + n], kc == 0, kc == 2,
                                [wuqk, f"mlCQn{kc}.{b}"], [prk_])
                    self.norm_rope(pr_[0:64, :n], prk_, 64, self.onesb[0:64, 0:64], "onesb", 1.0 / 64.0, prm[0:64, PO_RQ:PO_RQ + 1],
                                   t0, n, v == 0, QR[:, t0:t0 + n], [f"mlQR.{b}"], tm)
                for tt_ in range(18):
                    b = 0 if tt_ < 2 else 1 + (tt_ - 2) // 4
                    pv_, pvk = self.ps()
                    for kc in range(2):
                        self.mm(pv_[:, 0:128], CKVn[:, kc, tt_ * 128:(tt_ + 1) * 128], wukv[:, kc, h * 256 + 128:h * 256 + 256],
                                kc == 0, kc == 1, [wukvk, f"mlCKVn{kc}.{b}"], [pvk])
                    self.cp(VMh[:, tt_, :], pv_[:, 0:128], [pvk], [f"mlVM.{tt_}"], eng="act")

                def score(i, pS, pSk, kt, t0, n):
                    bq = [b for b, tb in enumerate(TB) if tb[0] == t0][0]
                    bk = 0 if kt < 2 else 1 + (kt - 2) // 4
                    self.mm(pS[:, :n], KN[:, kt * 128:(kt + 1) * 128], QN[:, t0:t0 + n], True, False, [f"mlKN.{bk}", f"mlQN.{bq}"], [pSk])
                    self.mm(pS[:, :n], KR[:, kt * 128:(kt + 1) * 128], QR[:, t0:t0 + n], False, True, [f"mlKR.{bk}", f"mlQR.{bq}"], [pSk])

                def vfn(kt):
                    return VMh[:, kt, :], f"mlVM.{kt}"

                def finish(accs, t0, n, qblk, h=h):
                    ((O1, O1k, Z1, Z1k),) = accs
                    self.act(orec[:, :n], Z1[:, :n], AF.Ln, [Z1k], ["odorec"])
                    self.act(orec[:, :n], orec[:, :n], AF.Exp, ["odorec"], ["odorec"], scale=-1.0)
                    self.tt(Mh[:, t0:t0 + n], O1[:, :n], orec[:, :n], ALU.mult, [O1k, "odorec"], [f"mlMh.{qblk}"])
                if not self.skip_ctx:
                    attend(0, [0, 1], score, vfn, 1, mscale, finish)
                for qblk in range(1, 5):
                    attend(qblk, list(range(18)), score, vfn, 1, mscale, finish)
                self.dump(f"Mo{o}b{h}", Mh[:], [128, NT], BF16, [f"mlMh.{b}" for b in range(5)])
                self.out_proj(l, [4 + h], lambda kc: Mh[:, :], keyf=lambda kc, b: f"mlMh.{b}")


def make_in_maps(inp, batches):
    cf, c2 = host_consts()
    cos2, sin2, perm = host_rope()
    pv, pe, bt, po = pack_inputs(inp)
    f = lambda a: np.ascontiguousarray(np.asarray(a, np.float32))
    shared = {
        "cf": cf, "c2": c2, "ropecos": cos2, "ropesin": sin2, "ropeperm": perm, "pv": pv, "pe": pe, "bt": bt, "po": po,
        "ada_w": f(inp["ada_w"]), "w_out": f(inp["w_out"]), "ffn_w_in": f(inp["ffn_w_in"]), "ffn_w_out": f(inp["ffn_w_out"]),
        "ab_w_in": f(inp["ab_w_in"]), "s5_glu_w": f(inp["s5_glu_w"]), "cd_w_in": f(inp["cd_w_in"]),
        "mla_w_uq": f(inp["mla_w_uq"]), "mla_w_ukv": f(inp["mla_w_ukv"]),
    }
    maps = []
    cc = colmaj(f(inp["c_ctx"]), 8)
    for b in batches:
        m = dict(shared)
        m["h0"] = np.ascontiguousarray(np.concatenate([f(inp["ctx"])[b].T, f(inp["x"])[b].T], axis=1))
        m["sc"] = np.ascontiguousarray(np.stack([colmaj(f(inp["c"])[b], 8), cc], axis=-1))
        maps.append(m)
    return maps


def kernel(**inputs):
    nb = 8
    b = Builder(list(range(DEPTH)))
    nc = b.build()
    maps = make_in_maps(inputs, list(range(nb)))
    res = run_bass_kernel_spmd(nc, maps, core_ids=list(range(nb)))
    out = np.stack([np.asarray(res.results[i]["out"], np.float32).T for i in range(nb)], axis=0)
    return np.ascontiguousarray(out)
```
